# Optimizing a Trainium2 kernel written in Bass

```python
import jax, jax.numpy as jnp
from jax import lax
import numpy as np

D_MODEL = 4096
BATCH = 4
SEQ = 2048
DEPTH = 2
DEC_BATCH = 8
DEC_SEQ = 8
PAST_LEN = 16384
PAGE_SIZE = 128

MIX_W = D_MODEL
A_W = MIX_W // 4
B_W = MIX_W // 4
C_W = MIX_W // 4
D_W = MIX_W - A_W - B_W - C_W
D_FF = ((8 * D_MODEL // 3 + 255) // 256) * 256
RMS_EPS = 1e-6

HEAD_DIM_A = 128
N_HEADS_A = A_W // HEAD_DIM_A
DILATED_BRANCHES = ((128, 1), (512, 4), (2048, 16))
WIN_MAX = 2048
Q_BLOCK = 128
ROPE_THETA = 10000.0

HEAD_DIM_B = 64
N_HEADS_B = B_W // HEAD_DIM_B
LORA_W = 64
LORA_A = 64
LORA_G = 160
GN_EPS = 64e-5
B_FEAT = 3 * B_W + LORA_W + LORA_A + LORA_G

CHUNK = 128
N_GROUPS_C = 8
GROUP_C = C_W // N_GROUPS_C

POOL_WINDOWS = (2, 4, 8, 16)
POOL_PREV = max(POOL_WINDOWS) - 1
POOL_GROUP = D_W // len(POOL_WINDOWS)

OFF_QA = 0
OFF_KA = A_W
OFF_VA = 2 * A_W
OFF_B = 3 * A_W
OFF_C = OFF_B + B_FEAT
OFF_D = OFF_C + 2 * C_W
PROJ_W = OFF_D + D_W

kernel_name = 'hybrid_dilated_rwkv7_gmlp_pool_decode_step'

F32 = jnp.float32


def _rmsnorm(x, g):
    xf = x.astype(F32)
    y = xf * lax.rsqrt(jnp.mean(xf * xf, axis=-1, keepdims=True) + RMS_EPS)
    return (y * g.astype(F32)).astype(x.dtype)


def _swiglu(x, wg, wu, wd):
    return (jax.nn.silu(x @ wg) * (x @ wu)) @ wd


def _rope(x, pos):
    half = x.shape[-1] // 2
    inv = ROPE_THETA ** (-jnp.arange(half, dtype=F32) / half)
    ang = pos.astype(F32)[:, None] * inv[None, :]
    cos = jnp.cos(ang)[None, :, None, :]
    sin = jnp.sin(ang)[None, :, None, :]
    x1 = x[..., :half].astype(F32)
    x2 = x[..., half:].astype(F32)
    return jnp.concatenate([x1 * cos - x2 * sin, x1 * sin + x2 * cos], axis=-1).astype(x.dtype)


def _dilated_branch(q, k_all, v_all, rows, window, dilation):
    offs = jnp.arange(window // dilation + 1) * dilation
    idx = rows[:, None] - offs[None, :]
    valid = idx >= 0
    idx = jnp.clip(idx, 0, k_all.shape[1] - 1)
    kg = k_all[:, idx]
    vg = v_all[:, idx]
    s = jnp.einsum('bqhd,bqnhd->bhqn', q, kg, preferred_element_type=F32) * (HEAD_DIM_A ** -0.5)
    s = jnp.where(valid[None, None], s, -jnp.inf)
    lse = jax.nn.logsumexp(s, axis=-1)
    p = jnp.exp(s - lse[..., None])
    o = jnp.einsum('bhqn,bqnhd->bqhd', p.astype(vg.dtype), vg, preferred_element_type=F32)
    return o, lse


def _dilated_attention(q, k_all, v_all, rows):
    B, T, H, hd = q.shape
    qb = min(Q_BLOCK, T)
    nb = -(-T // qb)
    pad = nb * qb - T
    qp = jnp.pad(q, ((0, 0), (0, pad), (0, 0), (0, 0))).reshape(B, nb, qb, H, hd).transpose(1, 0, 2, 3, 4)
    rp = jnp.pad(rows, (0, pad), mode='edge').reshape(nb, qb)

    def block(args):
        qblk, rblk = args
        outs, lses = [], []
        for window, dilation in DILATED_BRANCHES:
            o, lse = _dilated_branch(qblk, k_all, v_all, rblk, window, dilation)
            outs.append(o)
            lses.append(lse)
        lam = jax.nn.softmax(jnp.stack(lses), axis=0)
        lam = jnp.transpose(lam, (0, 1, 3, 2))[..., None]
        return jnp.sum(lam * jnp.stack(outs), axis=0).astype(q.dtype)

    out = lax.map(block, (qp, rp))
    return out.transpose(1, 0, 2, 3, 4).reshape(B, nb * qb, H, hd)[:, :T]


def _rwkv7_recurrence(r, w, k, v, kk, a, s0):
    def step(S, inp):
        r_t, w_t, k_t, v_t, kk_t, a_t = inp
        sa = jnp.einsum('bhvk,bhk->bhv', S, -kk_t)
        S = (S * w_t[:, :, None, :] + sa[..., None] * (kk_t * a_t)[:, :, None, :]
             + v_t[..., None] * k_t[:, :, None, :])
        return S, jnp.einsum('bhvk,bhk->bhv', S, r_t)

    xs = tuple(jnp.swapaxes(t.astype(F32), 0, 1) for t in (r, w, k, v, kk, a))
    S, o = lax.scan(step, s0.astype(F32), xs)
    return jnp.swapaxes(o, 0, 1), S


def _rwkv7_mix(fb, shift_prev, s0, mu, w0, w_up, a0, a_up, g_up, k_k, k_a, r_k, ln_w, ln_b):
    B, T, _ = fb.shape
    prev = jnp.concatenate([shift_prev[:, None, :].astype(fb.dtype), fb[:, :-1]], axis=1)
    fs = fb + mu * (prev - fb)
    r = fs[..., :B_W]
    k = fs[..., B_W:2 * B_W]
    v = fs[..., 2 * B_W:3 * B_W]
    zw = fs[..., 3 * B_W:3 * B_W + LORA_W]
    za = fs[..., 3 * B_W + LORA_W:3 * B_W + LORA_W + LORA_A]
    zg = fs[..., 3 * B_W + LORA_W + LORA_A:]
    w_log = -jax.nn.softplus(-(w0 + jnp.tanh(zw) @ w_up).astype(F32)) - 0.5
    decay = jnp.exp(-jnp.exp(w_log))
    a = jax.nn.sigmoid((a0 + za @ a_up).astype(F32))
    g = (jax.nn.sigmoid(zg) @ g_up).astype(F32)
    hs = (B, T, N_HEADS_B, HEAD_DIM_B)
    kf = k.astype(F32)
    kk = (kf * k_k).reshape(hs)
    kk = kk / jnp.maximum(jnp.sqrt(jnp.sum(kk * kk, axis=-1, keepdims=True)), 1e-12)
    kmod = (kf * (1.0 + (a - 1.0) * k_a)).reshape(hs)
    rf = r.astype(F32).reshape(hs)
    vf = v.astype(F32).reshape(hs)
    o, S = _rwkv7_recurrence(rf, decay.reshape(hs), kmod, vf, kk, a.reshape(hs), s0)
    mean = jnp.mean(o, axis=-1, keepdims=True)
    var = jnp.mean(jnp.square(o - mean), axis=-1, keepdims=True)
    o = ((o - mean) * lax.rsqrt(var + GN_EPS)).reshape(B, T, B_W) * ln_w + ln_b
    bonus = jnp.sum(rf * kmod * r_k, axis=-1, keepdims=True) * vf
    o = (o + bonus.reshape(B, T, B_W)) * g
    return o.astype(fb.dtype), S, fb[:, -1]


def _chunk_gmlp(u, v, w_s, b_s):
    B, T, _ = v.shape
    nc = -(-T // CHUNK)
    pad = nc * CHUNK - T
    vp = jnp.pad(v, ((0, 0), (0, pad), (0, 0))).reshape(B, nc, CHUNK, N_GROUPS_C, GROUP_C)
    ws = jnp.where(jnp.tril(jnp.ones((CHUNK, CHUNK), dtype=bool))[None], w_s, 0.0)
    s = jnp.einsum('gij,bcjgd->bcigd', ws, vp) + jnp.transpose(b_s)[None, None, :, :, None]
    s = s.reshape(B, nc * CHUNK, C_W)[:, :T]
    return u * s


def _pool_mix(p, prefix, pos, w_pool, scale):
    T = p.shape[1]
    ext = jnp.concatenate([prefix.astype(p.dtype), p], axis=1)
    cs = jnp.pad(jnp.cumsum(ext.astype(F32), axis=1), ((0, 0), (1, 0), (0, 0)))
    pf = p.astype(F32)
    outs = []
    for g, win in enumerate(POOL_WINDOWS):
        sl = slice(g * POOL_GROUP, (g + 1) * POOL_GROUP)
        hi = cs[:, POOL_PREV + 1:POOL_PREV + 1 + T, sl]
        lo = cs[:, POOL_PREV + 1 - win:POOL_PREV + 1 - win + T, sl]
        cnt = jnp.minimum(win, pos + 1).astype(F32)[None, :, None]
        pooled = (hi - lo) / cnt - pf[..., sl]
        outs.append(jnp.einsum('btc,cd->btd', pooled.astype(p.dtype), w_pool[g]))
    return jnp.concatenate(outs, axis=-1) * scale, ext[:, -POOL_PREV:]


def _layer(x, l, W, pos0, kv_prefix, shift_prev, wkv0, pool_prefix):
    B, T, _ = x.shape
    pos = pos0 + jnp.arange(T)
    h = x + 0.5 * _swiglu(_rmsnorm(x, W['ln_ffn1'][l]), W['w1_gate'][l], W['w1_up'][l], W['w1_down'][l])
    z = _rmsnorm(h, W['ln_mix'][l]) @ W['w_in'][l]
    hs = (B, T, N_HEADS_A, HEAD_DIM_A)
    qa = _rope(z[..., OFF_QA:OFF_KA].reshape(hs), pos)
    ka = _rope(z[..., OFF_KA:OFF_VA].reshape(hs), pos)
    va = z[..., OFF_VA:OFF_B].reshape(hs)
    if kv_prefix is None:
        k_all, v_all, n_prev = ka, va, 0
    else:
        k_prev, v_prev = kv_prefix
        k_all = jnp.concatenate([k_prev.astype(ka.dtype), ka], axis=1)
        v_all = jnp.concatenate([v_prev.astype(va.dtype), va], axis=1)
        n_prev = k_prev.shape[1]
    out_a = _dilated_attention(qa, k_all, v_all, n_prev + jnp.arange(T)).reshape(B, T, A_W)
    out_b, wkv_new, shift_new = _rwkv7_mix(
        z[..., OFF_B:OFF_C], shift_prev, wkv0, W['mu_b'][l], W['w0'][l], W['w_up'][l], W['a0'][l],
        W['a_up'][l], W['g_up'][l], W['k_k'][l], W['k_a'][l], W['r_k'][l], W['ln_x_w'][l], W['ln_x_b'][l])
    vc = z[..., OFF_C + C_W:OFF_D]
    out_c = _chunk_gmlp(z[..., OFF_C:OFF_C + C_W], vc, W['w_s'][l], W['b_s'][l])
    out_d, pool_new = _pool_mix(z[..., OFF_D:PROJ_W], pool_prefix, pos, W['w_pool'][l], W['pool_scale'][l])
    mix = jnp.concatenate([_rmsnorm(out_a, W['g_out_a'][l]), out_b,
                           _rmsnorm(out_c, W['g_out_c'][l]), _rmsnorm(out_d, W['g_out_d'][l])], axis=-1)
    h = h + mix @ W['w_out'][l]
    h = h + 0.5 * _swiglu(_rmsnorm(h, W['ln_ffn2'][l]), W['w2_gate'][l], W['w2_up'][l], W['w2_down'][l])
    return h, (ka, va, wkv_new, shift_new, pool_new, vc)


def setup_inputs(seed: int = 0) -> dict:
    key = jax.random.key(seed)
    ks = jax.random.split(key, 40)

    def nrm(i, shape, scale):
        return scale * jax.random.normal(ks[i], shape, F32)

    swa_buf = min(WIN_MAX, PAST_LEN)
    return {
        'x_prompt': nrm(0, (BATCH, SEQ, D_MODEL), 1.0),
        'x_sample': nrm(1, (DEC_BATCH, DEC_SEQ, D_MODEL), 1.0),
        'cache_k_swa': nrm(2, (DEPTH, DEC_BATCH, swa_buf, N_HEADS_A, HEAD_DIM_A), 1.0),
        'cache_v_swa': nrm(3, (DEPTH, DEC_BATCH, swa_buf, N_HEADS_A, HEAD_DIM_A), 1.0),
        'state_rwkv_wkv': nrm(4, (DEPTH, DEC_BATCH, N_HEADS_B, HEAD_DIM_B, HEAD_DIM_B), 0.3),
        'state_rwkv_shift': nrm(5, (DEPTH, DEC_BATCH, B_FEAT), 1.0),
        'state_pool': nrm(6, (DEPTH, DEC_BATCH, POOL_PREV, D_W), 1.0),
        'ln_ffn1': 1.0 + nrm(7, (DEPTH, D_MODEL), 0.02),
        'w1_gate': nrm(8, (DEPTH, D_MODEL, D_FF), D_MODEL ** -0.5),
        'w1_up': nrm(9, (DEPTH, D_MODEL, D_FF), D_MODEL ** -0.5),
        'w1_down': nrm(10, (DEPTH, D_FF, D_MODEL), D_FF ** -0.5),
        'ln_mix': 1.0 + nrm(11, (DEPTH, D_MODEL), 0.02),
        'w_in': nrm(12, (DEPTH, D_MODEL, PROJ_W), D_MODEL ** -0.5),
        'g_out_a': 1.0 + nrm(13, (DEPTH, A_W), 0.02),
        'mu_b': jax.random.uniform(ks[14], (DEPTH, B_FEAT), F32),
        'w0': jax.random.uniform(ks[15], (DEPTH, B_W), F32, -4.0, 1.0),
        'w_up': nrm(16, (DEPTH, LORA_W, B_W), LORA_W ** -0.5),
        'a0': nrm(17, (DEPTH, B_W), 0.5),
        'a_up': nrm(18, (DEPTH, LORA_A, B_W), LORA_A ** -0.5),
        'g_up': nrm(19, (DEPTH, LORA_G, B_W), LORA_G ** -0.5),
        'k_k': 0.85 + nrm(20, (DEPTH, B_W), 0.05),
        'k_a': 1.0 + nrm(21, (DEPTH, B_W), 0.05),
        'r_k': nrm(22, (DEPTH, N_HEADS_B, HEAD_DIM_B), 0.1),
        'ln_x_w': 1.0 + nrm(23, (DEPTH, B_W), 0.02),
        'ln_x_b': nrm(24, (DEPTH, B_W), 0.02),
        'w_s': nrm(25, (DEPTH, N_GROUPS_C, CHUNK, CHUNK), CHUNK ** -0.5),
        'b_s': 1.0 + nrm(26, (DEPTH, N_GROUPS_C, CHUNK), 0.02),
        'g_out_c': 1.0 + nrm(27, (DEPTH, C_W), 0.02),
        'w_pool': nrm(28, (DEPTH, len(POOL_WINDOWS), POOL_GROUP, POOL_GROUP), POOL_GROUP ** -0.5),
        'pool_scale': 1.0 + nrm(29, (DEPTH, D_W), 0.1),
        'g_out_d': 1.0 + nrm(30, (DEPTH, D_W), 0.02),
        'w_out': nrm(31, (DEPTH, MIX_W, D_MODEL), MIX_W ** -0.5),
        'ln_ffn2': 1.0 + nrm(32, (DEPTH, D_MODEL), 0.02),
        'w2_gate': nrm(33, (DEPTH, D_MODEL, D_FF), D_MODEL ** -0.5),
        'w2_up': nrm(34, (DEPTH, D_MODEL, D_FF), D_MODEL ** -0.5),
        'w2_down': nrm(35, (DEPTH, D_FF, D_MODEL), D_FF ** -0.5),
        'ln_final': 1.0 + nrm(36, (D_MODEL,), 0.02),
    }


def reference(x_prompt, x_sample, cache_k_swa, cache_v_swa, state_rwkv_wkv, state_rwkv_shift, state_pool,
              ln_ffn1, w1_gate, w1_up, w1_down, ln_mix, w_in, g_out_a, mu_b, w0, w_up, a0, a_up, g_up,
              k_k, k_a, r_k, ln_x_w, ln_x_b, w_s, b_s, g_out_c, w_pool, pool_scale, g_out_d, w_out,
              ln_ffn2, w2_gate, w2_up, w2_down, ln_final):
    W = dict(ln_ffn1=ln_ffn1, w1_gate=w1_gate, w1_up=w1_up, w1_down=w1_down, ln_mix=ln_mix, w_in=w_in,
             g_out_a=g_out_a, mu_b=mu_b, w0=w0, w_up=w_up, a0=a0, a_up=a_up, g_up=g_up, k_k=k_k, k_a=k_a,
             r_k=r_k, ln_x_w=ln_x_w, ln_x_b=ln_x_b, w_s=w_s, b_s=b_s, g_out_c=g_out_c, w_pool=w_pool,
             pool_scale=pool_scale, g_out_d=g_out_d, w_out=w_out, ln_ffn2=ln_ffn2, w2_gate=w2_gate,
             w2_up=w2_up, w2_down=w2_down)
    nbp = x_prompt.shape[0]
    yp, ys = x_prompt, x_sample
    p_states, s_states = [], []
    for l in range(DEPTH):
        yp, sp = _layer(yp, l, W, 0, None,
                        jnp.zeros((nbp, B_FEAT), yp.dtype),
                        jnp.zeros((nbp, N_HEADS_B, HEAD_DIM_B, HEAD_DIM_B), F32),
                        jnp.zeros((nbp, POOL_PREV, D_W), yp.dtype))
        ys, ss = _layer(ys, l, W, PAST_LEN, (cache_k_swa[l], cache_v_swa[l]),
                        state_rwkv_shift[l], state_rwkv_wkv[l], state_pool[l])
        p_states.append(sp)
        s_states.append(ss)
    keep = min(WIN_MAX, x_prompt.shape[1])
    new_k_swa_prompt = jnp.stack([s[0][:, -keep:] for s in p_states])
    new_v_swa_prompt = jnp.stack([s[1][:, -keep:] for s in p_states])
    new_wkv_prompt = jnp.stack([s[2] for s in p_states])
    new_shift_prompt = jnp.stack([s[3] for s in p_states])
    new_pool_prompt = jnp.stack([s[4] for s in p_states])
    new_k_swa_sample = jnp.stack([s[0] for s in s_states])
    new_v_swa_sample = jnp.stack([s[1] for s in s_states])
    new_wkv_sample = jnp.stack([s[2] for s in s_states])
    new_shift_sample = jnp.stack([s[3] for s in s_states])
    new_pool_sample = jnp.stack([s[4] for s in s_states])
    new_gmlp_v_sample = jnp.stack([s[5] for s in s_states])
    y_prompt = _rmsnorm(yp, ln_final)
    y_sample = _rmsnorm(ys, ln_final)
    return (y_prompt, y_sample, new_k_swa_prompt, new_v_swa_prompt, new_wkv_prompt, new_shift_prompt,
            new_pool_prompt, new_k_swa_sample, new_v_swa_sample, new_wkv_sample, new_shift_sample,
            new_pool_sample, new_gmlp_v_sample)
```

```python
import numpy as np
from contextlib import ExitStack
import concourse.bass as bass
import concourse.mybir as mybir
from concourse.bass_utils import run_bass_kernel_spmd

F32 = mybir.dt.float32
BF16 = mybir.dt.bfloat16
AF = mybir.ActivationFunctionType
ALU = mybir.AluOpType
AX = mybir.AxisListType

D = 4096
DFF = 11008
NJ = DFF // 128
PROJ = 9504
TP = 2048
TS = 8
TALL = TP + TS
NPREV = 2048
BFEAT = 3360
MASKW = 3200
C0 = 512
EPS = 1e-6


class Res:
    __slots__ = ("w", "r")

    def __init__(self):
        self.w = None
        self.r = {}


class Eng:
    def __init__(self, fw, key, eng, compute=True):
        self.key = key
        self.e = eng
        self.sem = fw.new_sem("p_" + key) if compute else None
        self.cnt = 0
        self.waited = {}


class FW:
    def __init__(self, nc, stack, n_dma_sems=20):
        self.nc = nc
        self.stack = stack
        self.pe = Eng(self, "pe", nc.tensor)
        self.dve = Eng(self, "dve", nc.vector)
        self.act = Eng(self, "act", nc.scalar)
        self.pool = Eng(self, "pool", nc.gpsimd)
        self.sp = Eng(self, "sp", nc.sync, compute=False)
        self.engs = [self.pe, self.dve, self.act, self.pool, self.sp]
        self.dring = {}
        for q in (self.sp, self.pool):
            self.dring[q.key] = [[self.new_sem("d_%s_%d" % (q.key, i)), 0] for i in range(n_dma_sems)]
        self.dpos = {"sp": 0, "pool": 0}

    def new_sem(self, name):
        return self.stack.enter_context(self.nc.semaphore(name))

    def _wait(self, E, tok):
        if tok is None:
            return
        sem, val, key = tok
        if key == "pe" and E.key == "pe":
            return
        k = id(sem)
        if E.waited.get(k, 0) >= val:
            return
        E.e.wait_ge(sem, val)
        E.waited[k] = val

    def _deps(self, E, reads, writes):
        for r in reads:
            self._wait(E, r.w)
        for w in writes:
            self._wait(E, w.w)
            for t in w.r.values():
                self._wait(E, t)

    def _commit(self, tok, reads, writes):
        for r in reads:
            r.r[id(tok[0])] = tok
        for w in writes:
            w.w = tok
            w.r = {}

    def op(self, E, fn, reads=(), writes=(), inc=True):
        self._deps(E, reads, writes)
        ins = fn()
        if inc:
            E.cnt += 1
            ins.then_inc(E.sem, 1)
            tok = (E.sem, E.cnt, E.key)
        else:
            tok = (E.sem, E.cnt + 1, E.key)
        self._commit(tok, reads, writes)
        return tok

    def dma(self, Q, out, in_, reads=(), writes=()):
        self._deps(Q, reads, writes)
        ring = self.dring[Q.key]
        pos = self.dpos[Q.key]
        self.dpos[Q.key] = (pos + 1) % len(ring)
        ent = ring[pos]
        if ent[1] > 0:
            self._wait(Q, (ent[0], ent[1], "dma"))
        ins = Q.e.dma_start(out=out, in_=in_)
        ent[1] += 16
        ins.then_inc(ent[0], 16)
        tok = (ent[0], ent[1], "dma")
        self._commit(tok, reads, writes)
        return tok

    def all_tokens(self):
        toks = []
        for q in self.dring.values():
            for ent in q:
                if ent[1] > 0:
                    toks.append((ent[0], ent[1], "dma"))
        for E in (self.pe, self.dve, self.act, self.pool):
            if E.cnt > 0:
                toks.append((E.sem, E.cnt, E.key))
        return toks

    def barrier(self):
        toks = self.all_tokens()
        for E in self.engs:
            for t in toks:
                self._wait(E, t)

    def finish(self):
        for t in self.all_tokens():
            self._wait(self.sp, t)


def _host_consts():
    half = 64
    inv = (10000.0 ** (-np.arange(half, dtype=np.float32) / half)).astype(np.float32)
    pos = np.concatenate([np.arange(TP), 16384 + np.arange(TS)]).astype(np.float32)
    ang = pos[:, None] * inv[None, :]
    cs = np.concatenate([np.cos(ang), np.sin(ang)], axis=1).astype(np.float32)
    d = np.arange(MASKW)[None, :] - np.arange(128)[:, None] - C0
    m = ((d >= 0) & (d <= 128)).astype(np.float32) + ((d >= 0) & (d <= 512) & (d % 4 == 0)) + \
        ((d >= 0) & (d <= 2048) & (d % 16 == 0))
    mt = m.astype(np.float32)
    pm = np.zeros((128, 4, 3, 128), np.float32)
    pms = np.zeros((32, 4, 8), np.float32)
    tl = np.arange(128)
    for g, win in enumerate((2, 4, 8, 16)):
        for t in range(128):
            for tp in range(t - win + 1, t + 1):
                if tp >= 0:
                    pm[tp, g, 0, t] += 1.0 / win
                    pm[tp, g, 2, t] += 1.0 / min(win, t + 1)
                else:
                    pm[128 + tp, g, 1, t] += 1.0 / win
            pm[t, g, 0, t] -= 1.0
            pm[t, g, 2, t] -= 1.0
        for t in range(8):
            for e in range(15 + t - win + 1, 15 + t + 1):
                pms[e, g, t] += 1.0 / win
            pms[15 + t, g, t] -= 1.0
    tril = np.tril(np.ones((128, 128), np.float32))
    tri2 = np.zeros((128, 128), np.float32)
    ones2 = np.zeros((128, 128), np.float32)
    mskc = np.zeros((128, 4, 64), np.float32)
    ii = np.arange(64)
    for d_ in range(2):
        sl = slice(d_ * 64, (d_ + 1) * 64)
        tri2[sl, sl] = (ii[:, None] <= ii[None, :])
        ones2[sl, sl] = 1.0
        mskc[sl, 0] = (ii[:, None] < ii[None, :])
        mskc[sl, 1] = (ii[:, None] <= ii[None, :])
        mskc[sl, 2] = (ii[:, None] > ii[None, :])
        mskc[sl, 3] = (ii[:, None] == ii[None, :])
    return dict(c_cs=cs, c_mt=mt, c_pm=pm.reshape(128, 4 * 3 * 128), c_pms=pms.reshape(32, 32), c_tril=tril,
                c_tri2=tri2, c_ones2=ones2, c_msk=mskc.reshape(128, 256))


WNAMES = [("ln_ffn1", [2, 32, 128]), ("w1_gate", [2, D, DFF]), ("w1_up", [2, D, DFF]), ("w1_down", [2, DFF, D]),
          ("ln_mix", [2, 32, 128]), ("w_in", [2, D, PROJ]), ("g_out_a", [2, 1, 1024]), ("mu_b", [2, 1, BFEAT]),
          ("w0", [2, 1, 1024]), ("w_up", [2, 64, 1024]), ("a0", [2, 1, 1024]), ("a_up", [2, 64, 1024]),
          ("g_up", [2, 160, 1024]), ("k_k", [2, 1, 1024]), ("k_a", [2, 1, 1024]), ("r_k", [2, 1, 1024]),
          ("ln_x_w", [2, 1, 1024]), ("ln_x_b", [2, 1, 1024]), ("w_s", [2, 8, 128, 128]), ("b_s", [2, 8, 128]),
          ("g_out_c", [2, 1, 1024]), ("w_pool", [2, 4, 256, 256]), ("pool_scale", [2, 1, 1024]),
          ("g_out_d", [2, 1, 1024]), ("w_out", [2, D, D]), ("ln_ffn2", [2, 32, 128]), ("w2_gate", [2, D, DFF]),
          ("w2_up", [2, D, DFF]), ("w2_down", [2, DFF, D]), ("ln_final", [32, 128])]


def build_nc():
    nc = bass.Bass("TRN2", target_bir_lowering=False)

    def din(name, shape):
        return nc.dram_tensor(name, list(shape), F32, kind="ExternalInput").ap()

    def dout(name, shape):
        return nc.dram_tensor(name, list(shape), F32, kind="ExternalOutput").ap()

    def dscr(name, shape):
        return nc.dram_tensor(name, list(shape), F32, kind="Internal").ap()

    X = din("x_all", [TALL, D])
    CK = din("cache_k", [2, NPREV, 1024])
    CV = din("cache_v", [2, NPREV, 1024])
    WKV0 = din("wkv0", [2, 16, 64, 64])
    SH0 = din("shift0", [2, 1, BFEAT])
    PL0 = din("pool0", [2, 15, 1024])
    Wd = {n: din(n, s) for n, s in WNAMES}
    CCS = din("c_cs", [TALL, 128])
    CMT = din("c_mt", [128, MASKW])
    CPM = din("c_pm", [128, 4 * 3 * 128])
    CPMS = din("c_pms", [32, 32])
    CTRIL = din("c_tril", [128, 128])
    CTRI2 = din("c_tri2", [128, 128])
    CONES2 = din("c_ones2", [128, 128])
    CMSK = din("c_msk", [128, 256])

    Y = dout("y", [TALL, D])
    NEWK = dout("newk", [2, TALL, 1024])
    NEWV = dout("newv", [2, TALL, 1024])
    WKVO = dout("wkvo", [2, 2, 16, 64, 64])
    SHO = dout("sho", [2, 2, BFEAT])
    PLO = dout("plo", [2, 2, 15, 1024])
    GVO = dout("gvo", [2, TS, 1024])

    R1 = dscr("r1", [TALL, D])
    R2 = dscr("r2", [TALL, D])
    RA = dscr("ra", [TALL, D])
    RB = dscr("rb", [TALL, D])
    Z = dscr("z", [TALL, PROJ])
    MIX = dscr("mix", [TALL, D])
    XB = dscr("xb", [TALL, 2, 5, 512])
    OT = dscr("ot", [TALL, 1024])

    rX, rR1, rR2, rRA, rRB, rZ, rMIX, rXB, rOT, rOUT = (Res() for _ in range(10))

    top = ExitStack()
    with top:
        fw = FW(nc, top)
        PE, DVE, ACT, POOL, SP = fw.pe, fw.dve, fw.act, fw.pool, fw.sp
        T, V, A, G = nc.tensor, nc.vector, nc.scalar, nc.gpsimd

        _uid = [0]

        def sb(st, name, shape, dt=F32):
            _uid[0] += 1
            return st.enter_context(nc.sbuf_tensor("%s_%d" % (name, _uid[0]), list(shape), dt))

        def pe(fn, r=(), w=(), inc=True):
            return fw.op(PE, fn, r, w, inc)

        def dve(fn, r=(), w=()):
            return fw.op(DVE, fn, r, w)

        def act(fn, r=(), w=()):
            return fw.op(ACT, fn, r, w)

        def pool(fn, r=(), w=()):
            return fw.op(POOL, fn, r, w)

        def ld(out, in_, r=(), w=()):
            return fw.dma(SP, out, in_, r, w)

        def ldc(out, in_, r=(), w=()):
            return fw.dma(POOL, out, in_, r, w)

        ident_f = sb(top, "ident_f", [128, 128]); r_idf = Res()
        ident_b = sb(top, "ident_b", [128, 128], BF16); r_idb = Res()
        pool(lambda: G.memset(ident_f[:], 0.0), w=[r_idf])
        pool(lambda: G.affine_select(out=ident_f[:], in_=ident_f[:], pattern=[[-1, 128]], compare_op=ALU.not_equal,
                                     fill=1.0, base=0, channel_multiplier=1), r=[r_idf], w=[r_idf])
        dve(lambda: V.tensor_copy(out=ident_b[:], in_=ident_f[:]), r=[r_idf], w=[r_idb])

        NPB = 6
        pb = [top.enter_context(nc.psum_tensor("pb%d" % i, [128, 512], F32)) for i in range(NPB)]
        r_pb = [Res() for _ in range(NPB)]
        ptb = top.enter_context(nc.psum_tensor("ptb", [128, 8, 128], BF16)); r_ptb = [Res(), Res()]
        ptf = top.enter_context(nc.psum_tensor("ptf", [128, 4, 128], F32)); r_ptf = Res()

        def token_local(st, jobs):
            Xn = sb(st, "Xn", [128, 32, 512], BF16); r_Xn = Res()
            H = sb(st, "H", [128, NJ, 512], BF16); r_H = Res()
            NW = 3
            wbuf = [sb(st, "wb%d" % i, [128, 6144], BF16) for i in range(NW)]
            r_wb = [Res() for _ in range(NW)]
            xst = sb(st, "xst", [128, D]); r_xst = Res()
            xnb = sb(st, "xnb", [128, D], BF16); r_xnb = Res()
            ss = sb(st, "ss", [128, 8]); r_ss = Res()
            gcol = sb(st, "gcol", [128, 32]); r_gcol = Res()
            graw = sb(st, "graw", [32, 128]); r_graw = Res()
            sg = [sb(st, "sg%d" % i, [128, 512]) for i in range(2)]; r_sg = [Res(), Res()]
            rsd = [sb(st, "rsd%d" % i, [128, 512]) for i in range(2)]; r_rsd = [Res(), Res()]
            yo = [sb(st, "yo%d" % i, [128, 512]) for i in range(2)]; r_yo = [Res(), Res()]
            cnt = {"w": 0, "sg": 0, "rsd": 0, "yo": 0, "pb": 0}

            def load_gcol(gsrc):
                ld(graw[:], gsrc, w=[r_graw])
                pe(lambda: T.transpose(out=ptf[:, 0, 0:32], in_=graw[:, :], identity=ident_f[0:32, 0:32]),
                   r=[r_graw, r_idf], w=[r_ptf])
                dve(lambda: V.tensor_copy(out=gcol[:], in_=ptf[:, 0, 0:32]), r=[r_ptf], w=[r_gcol])

            def load_norm(src, rsrc, t0, TT, norm=True):
                nsub = (TT + 127) // 128
                for s in range(nsub):
                    sub = min(128, TT - s * 128)
                    rows = slice(t0 + s * 128, t0 + s * 128 + sub)
                    if norm:
                        ld(xst[:sub, :], src[rows, :], r=[rsrc], w=[r_xst])
                        act(lambda: A.activation(out=xnb[:sub, :], in_=xst[:sub, :], func=AF.Square,
                                                 accum_out=ss[:sub, 0:1]), r=[r_xst], w=[r_xnb, r_ss])
                        dve(lambda: V.tensor_scalar(out=ss[:sub, 1:2], in0=ss[:sub, 0:1], scalar1=1.0 / D, scalar2=EPS,
                                                    op0=ALU.mult, op1=ALU.add), r=[r_ss], w=[r_ss])
                        act(lambda: A.activation(out=ss[:sub, 2:3], in_=ss[:sub, 1:2], func=AF.Sqrt), r=[r_ss], w=[r_ss])
                        dve(lambda: V.reciprocal(out=ss[:sub, 3:4], in_=ss[:sub, 2:3]), r=[r_ss], w=[r_ss])
                        act(lambda: A.activation(out=xnb[:sub, :], in_=xst[:sub, :], func=AF.Copy, scale=ss[:sub, 3:4]),
                            r=[r_xst, r_ss], w=[r_xnb])
                    else:
                        ldc(xnb[:sub, :], src[rows, :], r=[rsrc], w=[r_xnb])
                    for c8 in range(4):
                        hb = c8 % 2
                        for j in range(8):
                            c = c8 * 8 + j
                            pe(lambda c=c, j=j: T.transpose(out=ptb[:, j, 0:sub], in_=xnb[:sub, c * 128:(c + 1) * 128],
                                                            identity=ident_b[0:sub, 0:sub]),
                               r=[r_xnb, r_idb], w=[r_ptb[0]], inc=(j == 7))
                        for j in range(8):
                            c = c8 * 8 + j
                            if norm:
                                dve(lambda c=c, j=j: V.tensor_scalar(out=Xn[:, c, s * 128:s * 128 + sub], in0=ptb[:, j, 0:sub],
                                                                     scalar1=gcol[:, c:c + 1], scalar2=None, op0=ALU.mult),
                                    r=[r_ptb[0], r_gcol], w=[r_Xn])
                            else:
                                dve(lambda c=c, j=j: V.tensor_copy(out=Xn[:, c, s * 128:s * 128 + sub], in_=ptb[:, j, 0:sub]),
                                    r=[r_ptb[0]], w=[r_Xn])

            def wload(view_shape, src_ap):
                i = cnt["w"] % NW
                cnt["w"] += 1
                n = 1
                for v in view_shape[1:]:
                    n *= v
                flat = wbuf[i][:, 0:n]
                if len(view_shape) == 3:
                    view = flat.rearrange("p (a b) -> p a b", a=view_shape[1])
                else:
                    view = flat
                ldc(view, src_ap, w=[r_wb[i]])
                return view, r_wb[i]

            def ffn(TT, wg, wu, wdn, resid, r_resid, dst, r_dst, t0):
                nsub = (TT + 127) // 128
                tiles = []
                for mt in range(NJ // 2):
                    for kh in range(2):
                        tiles.append((wg, mt, kh, 0))
                        tiles.append((wu, mt, kh, 1))
                loaded = {}

                def issue(i):
                    if i < len(tiles):
                        wsrc, mt, kh, _ = tiles[i]
                        src = wsrc[kh * 2048:(kh + 1) * 2048, mt * 256:(mt + 1) * 256].rearrange("(c p) n -> p c n", p=128)
                        loaded[i] = wload([128, 16, 256], src)
                issue(0)
                issue(1)
                for i, (wsrc, mt, kh, gu) in enumerate(tiles):
                    issue(i + 2)
                    wv, rw = loaded.pop(i)
                    for mc in range(2):
                        bank = gu * 2 + mc
                        for k in range(16):
                            pe(lambda k=k, mc=mc, bank=bank, wv=wv: T.matmul(
                                pb[bank][:, 0:TT], lhsT=wv[:, k, mc * 128:(mc + 1) * 128], rhs=Xn[:, kh * 16 + k, 0:TT],
                                start=(kh == 0 and k == 0), stop=(kh == 1 and k == 15)),
                               r=[rw, r_Xn], w=[r_pb[bank]], inc=(mc == 1 and k == 15))
                    if kh == 1 and gu == 1:
                        for mc in range(2):
                            m = mt * 2 + mc
                            si = cnt["sg"] % 2
                            cnt["sg"] += 1
                            act(lambda mc=mc, si=si: A.activation(out=sg[si][:, 0:TT], in_=pb[mc][:, 0:TT], func=AF.Silu),
                                r=[r_pb[mc]], w=[r_sg[si]])
                            dve(lambda mc=mc, si=si, m=m: V.tensor_tensor(out=H[:, m, 0:TT], in0=pb[2 + mc][:, 0:TT],
                                                                          in1=sg[si][:, 0:TT], op=ALU.mult),
                                r=[r_pb[2 + mc], r_sg[si]], w=[r_H])
                groups = [(0, 11), (11, 11), (22, 11), (33, 11), (44, 11), (55, 11), (66, 10), (76, 10)]
                tiles = [(fb, g) for fb in range(8) for g in range(8)]
                loaded = {}

                def issue2(i):
                    if i < len(tiles):
                        fb, g = tiles[i]
                        j0, nj = groups[g]
                        src = wdn[j0 * 128:(j0 + nj) * 128, fb * 512:(fb + 1) * 512].rearrange("(c p) n -> p c n", p=128)
                        loaded[i] = wload([128, nj, 512], src)
                issue2(0)
                issue2(1)
                for i, (fb, g) in enumerate(tiles):
                    issue2(i + 2)
                    wv, rw = loaded.pop(i)
                    j0, nj = groups[g]
                    for s in range(nsub):
                        sub = min(128, TT - s * 128)
                        for jj in range(nj):
                            j = j0 + jj
                            pe(lambda s=s, sub=sub, jj=jj, j=j, wv=wv: T.matmul(
                                pb[s][:sub, :], lhsT=H[:, j, s * 128:s * 128 + sub], rhs=wv[:, jj, :],
                                start=(j == 0), stop=(j == NJ - 1)), r=[rw, r_H], w=[r_pb[s]],
                               inc=(s == nsub - 1 and jj == nj - 1))
                    if g == 7:
                        for s in range(nsub):
                            sub = min(128, TT - s * 128)
                            rows = slice(t0 + s * 128, t0 + s * 128 + sub)
                            cols = slice(fb * 512, (fb + 1) * 512)
                            ri = cnt["rsd"] % 2
                            cnt["rsd"] += 1
                            ld(rsd[ri][:sub, :], resid[rows, cols], r=[r_resid], w=[r_rsd[ri]])
                            dve(lambda s=s, sub=sub, ri=ri: V.scalar_tensor_tensor(
                                out=yo[ri][:sub, :], in0=pb[s][:sub, :], scalar=0.5, in1=rsd[ri][:sub, :],
                                op0=ALU.mult, op1=ALU.add), r=[r_pb[s], r_rsd[ri]], w=[r_yo[ri]])
                            ld(dst[rows, cols], yo[ri][:sub, :], r=[r_yo[ri]], w=[r_dst])

            def linear_tm(TT, wsrc, ncols, evac):
                nsub = (TT + 127) // 128
                ncb_n = (ncols + 511) // 512
                tiles = [(cb, kg) for cb in range(ncb_n) for kg in range(4)]
                loaded = {}

                def issue(i):
                    if i < len(tiles):
                        cb, kg = tiles[i]
                        ncb = min(512, ncols - cb * 512)
                        src = wsrc[kg * 1024:(kg + 1) * 1024, cb * 512:cb * 512 + ncb].rearrange("(c p) n -> p c n", p=128)
                        loaded[i] = wload([128, 8, ncb], src)
                issue(0)
                issue(1)
                for i, (cb, kg) in enumerate(tiles):
                    issue(i + 2)
                    wv, rw = loaded.pop(i)
                    ncb = min(512, ncols - cb * 512)
                    for s in range(nsub):
                        sub = min(128, TT - s * 128)
                        for k in range(8):
                            pe(lambda s=s, sub=sub, k=k, wv=wv: T.matmul(
                                pb[s][:sub, 0:ncb], lhsT=Xn[:, kg * 8 + k, s * 128:s * 128 + sub], rhs=wv[:, k, :],
                                start=(kg == 0 and k == 0), stop=(kg == 3 and k == 7)), r=[rw, r_Xn], w=[r_pb[s]],
                               inc=(s == nsub - 1 and k == 7))
                    if kg == 3:
                        for s in range(nsub):
                            sub = min(128, TT - s * 128)
                            evac(s, sub, cb, ncb)

            ctx = dict(load_gcol=load_gcol, load_norm=load_norm, ffn=ffn, linear_tm=linear_tm, rsd=rsd, r_rsd=r_rsd,
                       yo=yo, r_yo=r_yo, cnt=cnt)
            for job in jobs:
                job(ctx)

        TILES = [(0, 512), (512, 512), (1024, 512), (1536, 512), (TP, TS)]

        def job_ffn(l, which, src, rsrc, dst, rdst):
            def run(ctx):
                gname = "ln_ffn1" if which == 1 else "ln_ffn2"
                ctx["load_gcol"](Wd[gname][l])
                for (t0, TT) in TILES:
                    ctx["load_norm"](src, rsrc, t0, TT)
                    ctx["ffn"](TT, Wd["w%d_gate" % which][l], Wd["w%d_up" % which][l], Wd["w%d_down" % which][l],
                               src, rsrc, dst, rdst, t0)
            return run

        def job_win(l, src, rsrc):
            def run(ctx):
                ctx["load_gcol"](Wd["ln_mix"][l])
                yo, r_yo, cnt = ctx["yo"], ctx["r_yo"], ctx["cnt"]
                for (t0, TT) in TILES:
                    ctx["load_norm"](src, rsrc, t0, TT)

                    def evac(s, sub, cb, ncb, t0=t0):
                        ri = cnt["yo"] % 2
                        cnt["yo"] += 1
                        rows = slice(t0 + s * 128, t0 + s * 128 + sub)
                        act(lambda: A.copy(out=yo[ri][:sub, 0:ncb], in_=pb[s][:sub, 0:ncb]), r=[r_pb[s]], w=[r_yo[ri]])
                        ld(Z[rows, cb * 512:cb * 512 + ncb], yo[ri][:sub, 0:ncb], r=[r_yo[ri]], w=[rZ])
                    ctx["linear_tm"](TT, Wd["w_in"][l], PROJ, evac)
            return run

        def job_wout(l, hsrc, rh, dst, rdst):
            def run(ctx):
                yo, r_yo, rsd, r_rsd, cnt = ctx["yo"], ctx["r_yo"], ctx["rsd"], ctx["r_rsd"], ctx["cnt"]
                for (t0, TT) in TILES:
                    ctx["load_norm"](MIX, rMIX, t0, TT, norm=False)

                    def evac(s, sub, cb, ncb, t0=t0):
                        ri = cnt["yo"] % 2
                        cnt["yo"] += 1
                        rows = slice(t0 + s * 128, t0 + s * 128 + sub)
                        cols = slice(cb * 512, cb * 512 + ncb)
                        ld(rsd[ri][:sub, :], hsrc[rows, cols], r=[rh], w=[r_rsd[ri]])
                        dve(lambda: V.tensor_tensor(out=yo[ri][:sub, :], in0=pb[s][:sub, :], in1=rsd[ri][:sub, :], op=ALU.add),
                            r=[r_pb[s], r_rsd[ri]], w=[r_yo[ri]])
                        ld(dst[rows, cols], yo[ri][:sub, :], r=[r_yo[ri]], w=[rdst])
                    ctx["linear_tm"](TT, Wd["w_out"][l], D, evac)
            return run

        def job_final(src, rsrc):
            def run(ctx):
                pass
            return run

        def rms_rows(st_tiles, x_ap, sub, gbc_ap, out_ap, r_x, r_g, r_out, n):
            junk, r_junk, ss, r_ss = st_tiles
            act(lambda: A.activation(out=junk[:sub, 0:n], in_=x_ap, func=AF.Square, accum_out=ss[:sub, 0:1]),
                r=[r_x], w=[r_junk, r_ss])
            dve(lambda: V.tensor_scalar(out=ss[:sub, 1:2], in0=ss[:sub, 0:1], scalar1=1.0 / n, scalar2=EPS,
                                        op0=ALU.mult, op1=ALU.add), r=[r_ss], w=[r_ss])
            act(lambda: A.activation(out=ss[:sub, 2:3], in_=ss[:sub, 1:2], func=AF.Sqrt), r=[r_ss], w=[r_ss])
            dve(lambda: V.reciprocal(out=ss[:sub, 3:4], in_=ss[:sub, 2:3]), r=[r_ss], w=[r_ss])
            dve(lambda: V.scalar_tensor_tensor(out=out_ap, in0=x_ap, scalar=ss[:sub, 3:4], in1=gbc_ap,
                                               op0=ALU.mult, op1=ALU.mult), r=[r_x, r_ss, r_g], w=[r_out])

        SEQS = [dict(r0=0, T=TP, nprev=0, si=0), dict(r0=TP, T=TS, nprev=NPREV, si=1)]

        def final_norm(src, rsrc):
            with ExitStack() as st:
                xst = sb(st, "f_x", [128, D]); r_x = Res()
                gbc = sb(st, "f_g", [128, D]); r_g = Res()
                junk = sb(st, "f_j", [128, D]); r_j = Res()
                ss = sb(st, "f_ss", [128, 8]); r_ss = Res()
                yo = sb(st, "f_y", [128, D]); r_y = Res()
                ld(gbc[:], Wd["ln_final"].rearrange("a b -> (a b)").unsqueeze(0).partition_broadcast(128), w=[r_g])
                for (t0, TT) in [(i * 128, 128) for i in range(16)] + [(TP, TS)]:
                    ld(xst[:TT, :], src[t0:t0 + TT, :], r=[rsrc], w=[r_x])
                    rms_rows((junk, r_j, ss, r_ss), xst[:TT, :], TT, gbc[:TT, :], yo[:TT, :], r_x, r_g, r_y, D)
                    ld(Y[t0:t0 + TT, :], yo[:TT, :], r=[r_y], w=[rOUT])
            fw.barrier()

        def attention(l):
            for sq in SEQS:
                with ExitStack() as st:
                    r0, Tn, nprev = sq["r0"], sq["T"], sq["nprev"]
                    Tk = nprev + Tn
                    nkt = (Tk + 127) // 128
                    qT = sb(st, "qT", [128, 8, max(Tn, 128)], BF16); r_qT = Res()
                    kT = sb(st, "kT", [128, 8, nkt * 128], BF16); r_kT = Res()
                    va = sb(st, "va", [128, nkt, 8, 130], BF16); r_va = Res()
                    mt = sb(st, "mt", [128, MASKW], BF16); r_mt = Res()
                    zqk = sb(st, "zqk", [128, 2048]); r_zqk = Res()
                    rot = sb(st, "rot", [128, 2048]); r_rot = Res()
                    rotb = sb(st, "rotb", [128, 2048], BF16); r_rotb = Res()
                    t1 = sb(st, "t1", [128, 1024]); r_t1 = Res()
                    t2 = sb(st, "t2", [128, 1024]); r_t2 = Res()
                    cs = sb(st, "cs", [128, 128]); r_cs = Res()
                    eb = [sb(st, "eb%d" % i, [128, 512], BF16) for i in range(2)]; r_eb = [Res(), Res()]
                    pbf = [sb(st, "pbf%d" % i, [128, 512], BF16) for i in range(2)]; r_pbf = [Res(), Res()]
                    oa = sb(st, "oa", [128, 4, 1024]); r_oa = [Res() for _ in range(4)]
                    gbc = sb(st, "gbc", [128, 1024]); r_gbc = Res()
                    junk = sb(st, "ajunk", [128, 1024]); r_junk = Res()
                    ss = sb(st, "ass", [128, 8]); r_ss = Res()
                    rd = sb(st, "ard", [128, 8]); r_rd = Res()
                    yo = sb(st, "ayo", [128, 1024]); r_yo = Res()
                    ldc(mt[:], CMT[:, :], w=[r_mt])
                    ld(gbc[:], Wd["g_out_a"][l].partition_broadcast(128), w=[r_gbc])
                    pool(lambda: G.memset(kT[:], 0.0), w=[r_kT])
                    pool(lambda: G.memset(va[:], 0.0), w=[r_va])
                    pool(lambda: G.memset(va[:, :, :, 128:129], 1.0), w=[r_va])
                    if nprev > 0:
                        kc = sb(st, "kc", [128, 16, 1024], BF16); r_kc = Res()
                        ldc(kc[:], CK[l].rearrange("(s p) c -> p s c", p=128), w=[r_kc])
                        for s in range(16):
                            ldc(va[:, s, :, 0:128], CV[l, s * 128:(s + 1) * 128, :].rearrange("p (h d) -> p h d", d=128), r=[r_va], w=[r_va])
                        for s in range(16):
                            for h in range(8):
                                pe(lambda s=s, h=h: T.transpose(out=ptb[:, h, :], in_=kc[:, s, h * 128:(h + 1) * 128],
                                                                identity=ident_b[:, :]), r=[r_kc, r_idb], w=[r_ptb[0]])
                            dve(lambda s=s: V.tensor_copy(out=kT[:, :, s * 128:(s + 1) * 128], in_=ptb[:, :, :]),
                                r=[r_ptb[0]], w=[r_kT])
                    nsub = (Tn + 127) // 128
                    if Tn >= 128:
                        for s in range(nsub):
                            ldc(va[:, nprev // 128 + s, :, 0:128],
                                Z[r0 + s * 128:r0 + (s + 1) * 128, 2048:3072].rearrange("p (h d) -> p h d", d=128), r=[rZ, r_va], w=[r_va])
                    else:
                        ldc(va[0:Tn, nprev // 128, :, 0:128],
                            Z[r0:r0 + Tn, 2048:3072].rearrange("p (h d) -> p h d", d=128), r=[rZ, r_va], w=[r_va])
                    ld(NEWV[l, r0:r0 + Tn, :], Z[r0:r0 + Tn, 2048:3072], r=[rZ], w=[rOUT])
                    for s in range(nsub):
                        sub = min(128, Tn - s * 128)
                        rows = slice(r0 + s * 128, r0 + s * 128 + sub)
                        ld(zqk[:sub, :], Z[rows, 0:2048], r=[rZ], w=[r_zqk])
                        ld(cs[:sub, :], CCS[rows, :], w=[r_cs])
                        zv = zqk[:sub, :].rearrange("p (h two d) -> p h two d", two=2, d=64)
                        rv = rot[:sub, :].rearrange("p (h two d) -> p h two d", two=2, d=64)
                        x1, x2 = zv[:, :, 0, :], zv[:, :, 1, :]
                        cb_ = cs[:sub, 0:64].unsqueeze(1).to_broadcast([sub, 16, 64])
                        sb_ = cs[:sub, 64:128].unsqueeze(1).to_broadcast([sub, 16, 64])
                        t1v = t1[:sub, :].rearrange("p (h d) -> p h d", d=64)
                        t2v = t2[:sub, :].rearrange("p (h d) -> p h d", d=64)
                        dve(lambda: V.tensor_tensor(out=t1v, in0=x1, in1=cb_, op=ALU.mult), r=[r_zqk, r_cs], w=[r_t1])
                        dve(lambda: V.tensor_tensor(out=t2v, in0=x2, in1=sb_, op=ALU.mult), r=[r_zqk, r_cs], w=[r_t2])
                        dve(lambda: V.tensor_tensor(out=rv[:, :, 0, :], in0=t1v, in1=t2v, op=ALU.subtract), r=[r_t1, r_t2], w=[r_rot])
                        dve(lambda: V.tensor_tensor(out=t1v, in0=x1, in1=sb_, op=ALU.mult), r=[r_zqk, r_cs], w=[r_t1])
                        dve(lambda: V.tensor_tensor(out=t2v, in0=x2, in1=cb_, op=ALU.mult), r=[r_zqk, r_cs], w=[r_t2])
                        dve(lambda: V.tensor_tensor(out=rv[:, :, 1, :], in0=t1v, in1=t2v, op=ALU.add), r=[r_t1, r_t2], w=[r_rot])
                        ld(NEWK[l, rows, :], rot[:sub, 1024:2048], r=[r_rot], w=[rOUT])
                        act(lambda: A.copy(out=rotb[:sub, :], in_=rot[:sub, :]), r=[r_rot], w=[r_rotb])
                        for half in range(2):
                            for h in range(8):
                                c = half * 8 + h
                                pe(lambda c=c, h=h: T.transpose(out=ptb[:, h, 0:sub], in_=rotb[:sub, c * 128:(c + 1) * 128],
                                                                identity=ident_b[0:sub, 0:sub]), r=[r_rotb, r_idb], w=[r_ptb[0]])
                            if half == 0:
                                dve(lambda: V.tensor_copy(out=qT[:, :, s * 128:s * 128 + sub], in_=ptb[:, :, 0:sub]),
                                    r=[r_ptb[0]], w=[r_qT])
                            else:
                                dve(lambda: V.tensor_copy(out=kT[:, :, nprev + s * 128:nprev + s * 128 + sub], in_=ptb[:, :, 0:sub]),
                                    r=[r_ptb[0]], w=[r_kT])
                    QB = min(512, Tn)
                    it = 0
                    for qb in range(Tn // QB):
                        q0 = nprev + qb * QB
                        nq = QB
                        kt_lo = max(0, (q0 - 2048) // 128)
                        kt_hi = (q0 + nq - 1) // 128
                        nqs = (nq + 127) // 128
                        for h in range(8):
                            for kt in range(kt_lo, kt_hi + 1):
                                bi = it % 2
                                it += 1
                                c0 = q0 - kt * 128 + C0
                                pe(lambda bi=bi, kt=kt: T.matmul(pb[4 + bi][:, 0:nq], lhsT=kT[:, h, kt * 128:(kt + 1) * 128],
                                                                 rhs=qT[:, h, qb * QB:qb * QB + nq], start=True, stop=True),
                                   r=[r_kT, r_qT], w=[r_pb[4 + bi]])
                                act(lambda bi=bi: A.activation(out=eb[bi][:, 0:nq], in_=pb[4 + bi][:, 0:nq], func=AF.Exp,
                                                               scale=float(128 ** -0.5)), r=[r_pb[4 + bi]], w=[r_eb[bi]])
                                pool(lambda bi=bi, c0=c0: G.tensor_tensor(out=pbf[bi][:, 0:nq], in0=eb[bi][:, 0:nq],
                                                                          in1=mt[:, c0:c0 + nq], op=ALU.mult),
                                     r=[r_eb[bi], r_mt], w=[r_pbf[bi]])
                                for qs in range(nqs):
                                    qsub = min(128, nq - qs * 128)
                                    pe(lambda bi=bi, qs=qs, qsub=qsub, kt=kt: T.matmul(
                                        pb[qs][:qsub, 0:129], lhsT=pbf[bi][:, qs * 128:qs * 128 + qsub], rhs=va[:, kt, h, 0:129],
                                        start=(kt == kt_lo), stop=(kt == kt_hi)), r=[r_pbf[bi], r_va], w=[r_pb[qs]],
                                       inc=(qs == nqs - 1))
                            for qs in range(nqs):
                                qsub = min(128, nq - qs * 128)
                                dve(lambda qs=qs, qsub=qsub: V.reciprocal(out=rd[:qsub, qs:qs + 1], in_=pb[qs][:qsub, 128:129]),
                                    r=[r_pb[qs]], w=[r_rd])
                                dve(lambda qs=qs, qsub=qsub, h=h: V.tensor_scalar(
                                    out=oa[:qsub, qs, h * 128:(h + 1) * 128], in0=pb[qs][:qsub, 0:128], scalar1=rd[:qsub, qs:qs + 1],
                                    scalar2=None, op0=ALU.mult), r=[r_pb[qs], r_rd], w=[r_oa[qs]])
                        for qs in range(nqs):
                            qsub = min(128, nq - qs * 128)
                            rows = slice(r0 + qb * QB + qs * 128, r0 + qb * QB + qs * 128 + qsub)
                            rms_rows((junk, r_junk, ss, r_ss), oa[:qsub, qs, :], qsub, gbc[:qsub, :], yo[:qsub, :],
                                     r_oa[qs], r_gbc, r_yo, 1024)
                            ld(MIX[rows, 0:1024], yo[:qsub, :], r=[r_yo], w=[rMIX])
                fw.barrier()

        def gmlp_pool(l):
            with ExitStack() as st:
                wsT = sb(st, "wsT", [128, 8, 128], BF16); r_wsT = Res()
                wsr = sb(st, "wsr", [128, 128]); r_wsr = Res()
                wsm = sb(st, "wsm", [128, 128], BF16); r_wsm = Res()
                tril = sb(st, "tril", [128, 128]); r_tril = Res()
                bsr = sb(st, "bsr", [8, 128]); r_bsr = Res()
                bcol = sb(st, "bcol", [128, 8]); r_bcol = Res()
                gc = sb(st, "gc", [128, 1024]); r_gc = Res()
                gd = sb(st, "gd", [128, 1024]); r_gd = Res()
                psc = sb(st, "psc", [128, 1024]); r_psc = Res()
                pm = sb(st, "pm", [128, 12, 128], BF16); r_pm = Res()
                pms = sb(st, "pms", [32, 4, 8], BF16); r_pms = Res()
                wp = sb(st, "wp", [128, 4, 2, 256], BF16); r_wp = Res()
                u = sb(st, "gu", [128, 1024]); r_u = Res()
                vb = sb(st, "gvb", [128, 1024], BF16); r_vb = Res()
                oc = sb(st, "goc", [128, 1024]); r_oc = Res()
                pcur = sb(st, "pcur", [128, 1024], BF16); r_pcur = Res()
                pprev = sb(st, "pprev", [128, 1024], BF16); r_pprev = Res()
                pext = sb(st, "pext", [32, 1024], BF16); r_pext = Res()
                plT = sb(st, "plT", [128, 8, 128], BF16); r_plT = Res()
                od = sb(st, "god", [128, 1024]); r_od = Res()
                junk = sb(st, "gjunk", [128, 1024]); r_junk = Res()
                ss = sb(st, "gss", [128, 8]); r_ss = Res()
                yo = sb(st, "gyo", [128, 1024]); r_yo = Res()
                ld(tril[:], CTRIL[:, :], w=[r_tril])
                ld(gc[:], Wd["g_out_c"][l].partition_broadcast(128), w=[r_gc])
                ld(gd[:], Wd["g_out_d"][l].partition_broadcast(128), w=[r_gd])
                ld(psc[:], Wd["pool_scale"][l].partition_broadcast(128), w=[r_psc])
                ldc(pm[:], CPM.rearrange("p (a b) -> p a b", b=128), w=[r_pm])
                ldc(pms[:], CPMS.rearrange("p (a b) -> p a b", b=8), w=[r_pms])
                ldc(wp[:], Wd["w_pool"][l].rearrange("g (c p) d -> p g c d", p=128), w=[r_wp])
                ld(bsr[:], Wd["b_s"][l], w=[r_bsr])
                pe(lambda: T.transpose(out=ptf[:, 0, 0:8], in_=bsr[:, :], identity=ident_f[0:8, 0:8]), r=[r_bsr, r_idf], w=[r_ptf])
                dve(lambda: V.tensor_copy(out=bcol[:], in_=ptf[:, 0, 0:8]), r=[r_ptf], w=[r_bcol])
                for g in range(8):
                    ld(wsr[:], Wd["w_s"][l, g], w=[r_wsr])
                    dve(lambda: V.tensor_tensor(out=wsm[:], in0=wsr[:], in1=tril[:], op=ALU.mult), r=[r_wsr, r_tril], w=[r_wsm])
                    pe(lambda g=g: T.transpose(out=ptb[:, g, :], in_=wsm[:, :], identity=ident_b[:, :]), r=[r_wsm, r_idb], w=[r_ptb[0]])
                dve(lambda: V.tensor_copy(out=wsT[:], in_=ptb[:]), r=[r_ptb[0]], w=[r_wsT])
                for sq in SEQS:
                    r0, Tn, nprev, si = sq["r0"], sq["T"], sq["nprev"], sq["si"]
                    nsub = (Tn + 127) // 128
                    if si == 0:
                        ld(PLO[l, 0], Z[r0 + Tn - 15:r0 + Tn, 8480:9504], r=[rZ], w=[rOUT])
                    else:
                        ld(PLO[l, 1, 0:7, :], PL0[l, 8:15, :], w=[rOUT])
                        ld(PLO[l, 1, 7:15, :], Z[r0:r0 + Tn, 8480:9504], r=[rZ], w=[rOUT])
                        ld(GVO[l], Z[r0:r0 + Tn, 7456:8480], r=[rZ], w=[rOUT])
                    ld(SHO[l, si:si + 1, :], Z[r0 + Tn - 1:r0 + Tn, 3072:3072 + BFEAT], r=[rZ], w=[rOUT])
                    for s in range(nsub):
                        sub = min(128, Tn - s * 128)
                        rows = slice(r0 + s * 128, r0 + s * 128 + sub)
                        ld(u[:sub, :], Z[rows, 6432:7456], r=[rZ], w=[r_u])
                        ldc(vb[:sub, :], Z[rows, 7456:8480], r=[rZ], w=[r_vb])
                        for g in range(8):
                            bank = g // 4
                            pe(lambda g=g, bank=bank: T.matmul(pb[bank][:sub, (g % 4) * 128:(g % 4 + 1) * 128], lhsT=wsT[:sub, g, 0:sub],
                                                               rhs=vb[:sub, g * 128:(g + 1) * 128], start=True, stop=True),
                               r=[r_wsT, r_vb], w=[r_pb[bank]])
                        for g in range(8):
                            bank = g // 4
                            dve(lambda g=g, bank=bank: V.scalar_tensor_tensor(
                                out=oc[:sub, g * 128:(g + 1) * 128], in0=pb[bank][:sub, (g % 4) * 128:(g % 4 + 1) * 128],
                                scalar=bcol[:sub, g:g + 1], in1=u[:sub, g * 128:(g + 1) * 128], op0=ALU.add, op1=ALU.mult),
                                r=[r_pb[bank], r_bcol, r_u], w=[r_oc])
                        rms_rows((junk, r_junk, ss, r_ss), oc[:sub, :], sub, gc[:sub, :], yo[:sub, :], r_oc, r_gc, r_yo, 1024)
                        ld(MIX[rows, 2048:3072], yo[:sub, :], r=[r_yo], w=[rMIX])
                        if si == 0:
                            cur, r_cur = (pcur, r_pcur) if s % 2 == 0 else (pprev, r_pprev)
                            prv, r_prv = (pprev, r_pprev) if s % 2 == 0 else (pcur, r_pcur)
                            ldc(cur[:sub, :], Z[rows, 8480:9504], r=[rZ], w=[r_cur])
                            for g in range(4):
                                for cc in range(2):
                                    ch = g * 2 + cc
                                    cols = slice(ch * 128, (ch + 1) * 128)
                                    if s == 0:
                                        pe(lambda g=g, ch=ch, cols=cols: T.matmul(pb[2 + ch // 4][:, (ch % 4) * 128:(ch % 4 + 1) * 128],
                                                                               lhsT=cur[:sub, cols], rhs=pm[:sub, g * 3 + 2, 0:sub],
                                                                               start=True, stop=True), r=[r_cur, r_pm], w=[r_pb[2 + ch // 4]])
                                    else:
                                        pe(lambda g=g, ch=ch, cols=cols: T.matmul(pb[2 + ch // 4][:, (ch % 4) * 128:(ch % 4 + 1) * 128],
                                                                               lhsT=cur[:sub, cols], rhs=pm[:sub, g * 3 + 0, 0:sub],
                                                                               start=True, stop=False), r=[r_cur, r_pm], w=[r_pb[2 + ch // 4]])
                                        pe(lambda g=g, ch=ch, cols=cols: T.matmul(pb[2 + ch // 4][:, (ch % 4) * 128:(ch % 4 + 1) * 128],
                                                                               lhsT=prv[:, cols], rhs=pm[:, g * 3 + 1, 0:sub],
                                                                               start=False, stop=True), r=[r_prv, r_pm], w=[r_pb[2 + ch // 4]])
                        else:
                            ldc(pext[0:15, :], PL0[l], w=[r_pext])
                            ldc(pext[15:15 + sub, :], Z[rows, 8480:9504], r=[rZ, r_pext], w=[r_pext])
                            for g in range(4):
                                for cc in range(2):
                                    ch = g * 2 + cc
                                    cols = slice(ch * 128, (ch + 1) * 128)
                                    pe(lambda g=g, ch=ch, cols=cols: T.matmul(pb[2 + ch // 4][:, (ch % 4) * 128:(ch % 4) * 128 + sub],
                                                                           lhsT=pext[0:23, cols], rhs=pms[0:23, g, 0:sub],
                                                                           start=True, stop=True), r=[r_pext, r_pms], w=[r_pb[2 + ch // 4]])
                        for hb in range(2):
                            dve(lambda hb=hb: V.tensor_copy(out=plT[:, hb * 4:(hb + 1) * 4, 0:sub],
                                                            in_=pb[2 + hb][:, :].rearrange("p (a b) -> p a b", b=128)[:, :, 0:sub]),
                                r=[r_pb[2 + hb]], w=[r_plT])
                        for g in range(4):
                            bank = g // 2
                            for cc in range(2):
                                pe(lambda g=g, cc=cc, bank=bank: T.matmul(pb[bank][:sub, (g % 2) * 256:(g % 2 + 1) * 256],
                                                                          lhsT=plT[:, g * 2 + cc, 0:sub], rhs=wp[:, g, cc, :],
                                                                          start=(cc == 0), stop=(cc == 1)), r=[r_plT, r_wp], w=[r_pb[bank]])
                        for bank in range(2):
                            dve(lambda bank=bank: V.tensor_tensor(out=od[:sub, bank * 512:(bank + 1) * 512], in0=pb[bank][:sub, :],
                                                                  in1=psc[:sub, bank * 512:(bank + 1) * 512], op=ALU.mult),
                                r=[r_pb[bank], r_psc], w=[r_od])
                        rms_rows((junk, r_junk, ss, r_ss), od[:sub, :], sub, gd[:sub, :], yo[:sub, :], r_od, r_gd, r_yo, 1024)
                        ld(MIX[rows, 3072:4096], yo[:sub, :], r=[r_yo], w=[rMIX])
            fw.barrier()

        def rwkv(l):
            CH = 64
            with ExitStack() as st:
                def bc(name, src):
                    t = sb(st, name, [128, 1024]); r = Res()
                    ld(t[:], src.partition_broadcast(128), w=[r])
                    return t, r
                mu = sb(st, "mu", [128, BFEAT]); r_mu = Res()
                ld(mu[:], Wd["mu_b"][l].partition_broadcast(128), w=[r_mu])
                w0b, r_w0b = bc("w0b", Wd["w0"][l])
                a0b, r_a0b = bc("a0b", Wd["a0"][l])
                kkb, r_kkb = bc("kkb", Wd["k_k"][l])
                kab, r_kab = bc("kab", Wd["k_a"][l])
                rkb, r_rkb = bc("rkb", Wd["r_k"][l])
                lwb, r_lwb = bc("lwb", Wd["ln_x_w"][l])
                lbb, r_lbb = bc("lbb", Wd["ln_x_b"][l])
                wup = sb(st, "wup", [64, 1024], BF16); r_wup = Res()
                aup = sb(st, "aup", [64, 1024], BF16); r_aup = Res()
                gup = sb(st, "gup", [128, 2, 1024], BF16); r_gup = Res()
                ldc(wup[:], Wd["w_up"][l], w=[r_wup])
                ldc(aup[:], Wd["a_up"][l], w=[r_aup])
                ldc(gup[:, 0, :], Wd["g_up"][l, 0:128, :], w=[r_gup])
                ldc(gup[0:32, 1, :], Wd["g_up"][l, 128:160, :], r=[r_gup], w=[r_gup])
                tri2 = sb(st, "tri2", [128, 128]); r_tri2 = Res()
                ones2 = sb(st, "ones2", [128, 128]); r_ones2 = Res()
                msk = sb(st, "msk", [128, 4, 64]); r_msk = Res()
                ld(tri2[:], CTRI2[:, :], w=[r_tri2])
                ld(ones2[:], CONES2[:, :], w=[r_ones2])
                ld(msk[:], CMSK.rearrange("p (a b) -> p a b", b=64), w=[r_msk])

                def mb(i):
                    return msk[:, i:i + 1, :].to_broadcast([128, 8, 64])
                valid = sb(st, "valid", [128, 1]); r_valid = Res()
                fbt = sb(st, "fbt", [128, BFEAT]); r_fb = Res()
                pvt = sb(st, "pvt", [128, BFEAT]); r_pv = Res()
                lz = sb(st, "lz", [128, 288], BF16); r_lz = Res()
                lzT = sb(st, "lzT", [128, 4, 128], BF16); r_lzT = Res()
                NA = 14
                arr = [sb(st, "ar%d" % i, [128, 1024]) for i in range(NA)]
                r_arr = [Res() for _ in range(NA)]
                sm = sb(st, "sm", [128, 64]); r_sm = Res()
                NF = 5
                fm = [sb(st, "fm%d" % i, [128, 8, 64]) for i in range(NF)]; r_fm = [Res() for _ in range(NF)]
                NM = 14
                mm = [sb(st, "mm%d" % i, [128, 8, 64]) for i in range(NM)]; r_mm = [Res() for _ in range(NM)]
                ST = sb(st, "ST", [128, 8, 64]); r_ST = Res()
                stmp = sb(st, "stmp", [64, 8, 128]); r_stmp = Res()

                def pbv(i):
                    return pb[i][:, :].rearrange("p (a b) -> p a b", b=64)

                def headmm(bank, terms, r, first=True, last=True):
                    n = len(terms)
                    for hp in range(8):
                        for hh in range(2):
                            R = slice(hh * 64, (hh + 1) * 64)
                            for ti, (lf, rf) in enumerate(terms):
                                is_last_inst = (hp == 7 and hh == 1 and ti == n - 1)
                                pe(lambda: T.matmul(pb[bank][R, hp * 64:(hp + 1) * 64], lhsT=lf(R, hp, hh), rhs=rf(R, hp, hh),
                                                    start=(first and ti == 0), stop=(last and ti == n - 1)),
                                   r=r, w=[r_pb[bank]], inc=is_last_inst)

                def f3(t):
                    return lambda R, hp, hh: t[R, hp, :]

                def tmcols(ap2d):
                    return lambda R, hp, hh: ap2d[R, hp * 128 + hh * 64:hp * 128 + hh * 64 + 64]

                for sq in SEQS:
                    r0, Tn, nprev, si = sq["r0"], sq["T"], sq["nprev"], sq["si"]
                    nch = (Tn + CH - 1) // CH
                    if si == 0:
                        dve(lambda: V.memset(ST[:], 0.0), w=[r_ST])
                    else:
                        for hh in range(2):
                            ld(stmp[:, :, hh * 64:(hh + 1) * 64],
                               WKV0[l].rearrange("(hp hh) v k -> hh v hp k", hh=2)[hh], r=[r_stmp], w=[r_stmp])
                        for hb in range(2):
                            for j in range(4):
                                pe(lambda: T.transpose(out=ptf[:, j, 0:64], in_=stmp[:, hb * 4 + j, :], identity=ident_f[0:64, 0:64]),
                                   r=[r_stmp, r_idf], w=[r_ptf], inc=(j == 3))
                            dve(lambda: V.tensor_copy(out=ST[:, hb * 4:(hb + 1) * 4, :], in_=ptf[:, :, 0:64]), r=[r_ptf], w=[r_ST])
                    for c in range(nch):
                        nv = min(CH, Tn - c * CH)
                        ta = r0 + c * CH
                        fcol = slice(3072, 3072 + BFEAT)
                        if nv < CH:
                            dve(lambda: V.memset(fbt[:], 0.0), w=[r_fb])
                            dve(lambda: V.memset(pvt[:], 0.0), w=[r_pv])
                        if nv < CH or c == 0:
                            dve(lambda: V.memset(valid[:], 0.0), w=[r_valid])
                            for d in range(2):
                                dve(lambda: V.memset(valid[d * 64:d * 64 + nv, :], 1.0), r=[r_valid], w=[r_valid])
                        for d in range(2):
                            P0 = d * 64
                            ld(fbt[P0:P0 + nv, :], Z[ta:ta + nv, fcol], r=[rZ, r_fb], w=[r_fb])
                            if c == 0:
                                if si == 0:
                                    dve(lambda: V.memset(pvt[P0:P0 + 1, :], 0.0), r=[r_pv], w=[r_pv])
                                else:
                                    ld(pvt[P0:P0 + 1, :], SH0[l], r=[r_pv], w=[r_pv])
                                if nv > 1:
                                    ld(pvt[P0 + 1:P0 + nv, :], Z[ta:ta + nv - 1, fcol], r=[rZ, r_pv], w=[r_pv])
                            else:
                                ld(pvt[P0:P0 + nv, :], Z[ta - 1:ta - 1 + nv, fcol], r=[rZ, r_pv], w=[r_pv])
                        dve(lambda: V.tensor_tensor(out=pvt[:, :], in0=pvt[:, :], in1=fbt[:, :], op=ALU.subtract), r=[r_pv, r_fb], w=[r_pv])
                        pool(lambda: G.tensor_tensor(out=pvt[:, :], in0=pvt[:, :], in1=mu[:, :], op=ALU.mult), r=[r_pv, r_mu], w=[r_pv])
                        dve(lambda: V.tensor_tensor(out=pvt[:, :], in0=pvt[:, :], in1=fbt[:, :], op=ALU.add), r=[r_pv, r_fb], w=[r_pv])
                        fs = pvt
                        R_, K_, V_ = fs[:, 0:1024], fs[:, 1024:2048], fs[:, 2048:3072]
                        act(lambda: A.activation(out=lz[:, 0:64], in_=fs[:, 3072:3136], func=AF.Tanh), r=[r_pv], w=[r_lz])
                        act(lambda: A.copy(out=lz[:, 64:128], in_=fs[:, 3136:3200]), r=[r_pv], w=[r_lz])
                        act(lambda: A.activation(out=lz[:, 128:288], in_=fs[:, 3200:3360], func=AF.Sigmoid), r=[r_pv], w=[r_lz])
                        for j, (c0_, cn) in enumerate([(0, 64), (64, 64), (128, 128), (256, 32)]):
                            pe(lambda: T.transpose(out=ptb[0:cn, j, :], in_=lz[:, c0_:c0_ + cn], identity=ident_b[:, :]),
                               r=[r_lz, r_idb], w=[r_ptb[0]], inc=(j == 3))
                        dve(lambda: V.tensor_copy(out=lzT[:, :, :], in_=ptb[:, 0:4, :]), r=[r_ptb[0]], w=[r_lzT])
                        for hb in range(2):
                            cs_ = slice(hb * 512, (hb + 1) * 512)
                            pe(lambda: T.matmul(pb[hb][:, :], lhsT=lzT[0:64, 0, :], rhs=wup[:, cs_], start=True, stop=True),
                               r=[r_lzT, r_wup], w=[r_pb[hb]])
                            pe(lambda: T.matmul(pb[2 + hb][:, :], lhsT=lzT[0:64, 1, :], rhs=aup[:, cs_], start=True, stop=True),
                               r=[r_lzT, r_aup], w=[r_pb[2 + hb]])
                            pe(lambda: T.matmul(pb[4 + hb][:, :], lhsT=lzT[:, 2, :], rhs=gup[:, 0, cs_], start=True, stop=False),
                               r=[r_lzT, r_gup], w=[r_pb[4 + hb]], inc=False)
                            pe(lambda: T.matmul(pb[4 + hb][:, :], lhsT=lzT[0:32, 3, :], rhs=gup[0:32, 1, cs_], start=False, stop=True),
                               r=[r_lzT, r_gup], w=[r_pb[4 + hb]])
                        (LW, At, Gt, KKt, Bt, KMt, BON, T1, T2, Lsb, Xa, Xb, Xc, Xd) = [a[:, :] for a in arr]
                        (rLW, rA, rG, rKK, rB, rKM, rBON, rT1, rT2, rL, rXa, rXb, rXc, rXd) = r_arr
                        for hb in range(2):
                            cs_ = slice(hb * 512, (hb + 1) * 512)
                            dve(lambda: V.tensor_tensor(out=arr[0][:, cs_], in0=pb[hb][:, :], in1=w0b[:, cs_], op=ALU.add),
                                r=[r_pb[hb], r_w0b], w=[rLW])
                            dve(lambda: V.tensor_tensor(out=arr[1][:, cs_], in0=pb[2 + hb][:, :], in1=a0b[:, cs_], op=ALU.add),
                                r=[r_pb[2 + hb], r_a0b], w=[rA])
                            act(lambda: A.copy(out=arr[2][:, cs_], in_=pb[4 + hb][:, :]), r=[r_pb[4 + hb]], w=[rG])
                        act(lambda: A.activation(out=LW, in_=LW, func=AF.Sigmoid), r=[rLW], w=[rLW])
                        act(lambda: A.activation(out=At, in_=At, func=AF.Sigmoid), r=[rA], w=[rA])
                        dve(lambda: V.tensor_scalar(out=LW, in0=LW, scalar1=valid[:, 0:1], scalar2=-float(np.exp(-0.5)), op0=ALU.mult, op1=ALU.mult),
                            r=[rLW, r_valid], w=[rLW])
                        pool(lambda: G.tensor_tensor(out=KKt, in0=K_, in1=kkb[:, :], op=ALU.mult), r=[r_pv, r_kkb], w=[rKK])
                        dve(lambda: V.tensor_tensor(out=T1, in0=KKt, in1=KKt, op=ALU.mult), r=[rKK], w=[rT1])
                        dve(lambda: V.tensor_reduce(out=sm[:, 0:16], in_=T1.rearrange("p (h d) -> p h d", d=64), axis=AX.X, op=ALU.add), r=[rT1], w=[r_sm])
                        act(lambda: A.activation(out=sm[:, 16:32], in_=sm[:, 0:16], func=AF.Sqrt), r=[r_sm], w=[r_sm])
                        dve(lambda: V.tensor_scalar(out=sm[:, 16:32], in0=sm[:, 16:32], scalar1=1e-12, scalar2=None, op0=ALU.max), r=[r_sm], w=[r_sm])
                        dve(lambda: V.reciprocal(out=sm[:, 32:48], in_=sm[:, 16:32]), r=[r_sm], w=[r_sm])
                        dve(lambda: V.tensor_tensor(out=KKt.rearrange("p (h d) -> p h d", d=64), in0=KKt.rearrange("p (h d) -> p h d", d=64),
                                                    in1=sm[:, 32:48].unsqueeze(2).to_broadcast([128, 16, 64]), op=ALU.mult), r=[rKK, r_sm], w=[rKK])
                        pool(lambda: G.tensor_tensor(out=Bt, in0=KKt, in1=At, op=ALU.mult), r=[rKK, rA], w=[rB])
                        dve(lambda: V.scalar_tensor_tensor(out=T1, in0=At, scalar=-1.0, in1=kab[:, :], op0=ALU.add, op1=ALU.mult), r=[rA, r_kab], w=[rT1])
                        dve(lambda: V.scalar_tensor_tensor(out=KMt, in0=T1, scalar=1.0, in1=K_, op0=ALU.add, op1=ALU.mult), r=[rT1, r_pv], w=[rKM])
                        pool(lambda: G.tensor_tensor(out=T1, in0=R_, in1=KMt, op=ALU.mult), r=[r_pv, rKM], w=[rT1])
                        pool(lambda: G.tensor_tensor(out=T1, in0=T1, in1=rkb[:, :], op=ALU.mult), r=[rT1, r_rkb], w=[rT1])
                        dve(lambda: V.tensor_reduce(out=sm[:, 48:64], in_=T1.rearrange("p (h d) -> p h d", d=64), axis=AX.X, op=ALU.add), r=[rT1], w=[r_sm])
                        pool(lambda: G.tensor_tensor(out=BON.rearrange("p (h d) -> p h d", d=64), in0=V_.rearrange("p (h d) -> p h d", d=64),
                                                     in1=sm[:, 48:64].unsqueeze(2).to_broadcast([128, 16, 64]), op=ALU.mult), r=[r_pv, r_sm], w=[rBON])
                        for hb in range(2):
                            cs_ = slice(hb * 512, (hb + 1) * 512)
                            pe(lambda: T.matmul(pb[hb][:, :], lhsT=tri2[:, :], rhs=arr[0][:, cs_], start=True, stop=True), r=[r_tri2, rLW], w=[r_pb[hb]])
                            pe(lambda: T.matmul(pb[2 + hb][:, :], lhsT=ones2[:, :], rhs=arr[0][:, cs_], start=True, stop=True), r=[r_ones2, rLW], w=[r_pb[2 + hb]])
                        for hb in range(2):
                            cs_ = slice(hb * 512, (hb + 1) * 512)
                            act(lambda: A.copy(out=arr[9][:, cs_], in_=pb[hb][:, :]), r=[r_pb[hb]], w=[rL])
                            dve(lambda: V.tensor_tensor(out=arr[8][:, cs_], in0=pb[2 + hb][:, :], in1=arr[9][:, cs_], op=ALU.subtract), r=[r_pb[2 + hb], rL], w=[rT2])
                        act(lambda: A.activation(out=T2, in_=T2, func=AF.Exp), r=[rT2], w=[rT2])
                        dve(lambda: V.tensor_tensor(out=T1, in0=Lsb, in1=LW, op=ALU.subtract), r=[rL, rLW], w=[rT1])
                        act(lambda: A.activation(out=T1, in_=T1, func=AF.Exp), r=[rT1], w=[rT1])
                        pool(lambda: G.tensor_tensor(out=Xa, in0=KKt, in1=T1, op=ALU.mult), r=[rKK, rT1], w=[rXa])
                        dve(lambda: V.tensor_tensor(out=KKt, in0=Bt, in1=T2, op=ALU.mult), r=[rB, rT2], w=[rKK])
                        pool(lambda: G.tensor_tensor(out=LW, in0=KMt, in1=T2, op=ALU.mult), r=[rKM, rT2], w=[rLW])
                        BH, KH, rBH, rKH = KKt, LW, rKK, rLW
                        act(lambda: A.activation(out=Xd, in_=Lsb, func=AF.Exp), r=[rL], w=[rXd])
                        act(lambda: A.activation(out=T1, in_=Lsb, func=AF.Exp, scale=-1.0), r=[rL], w=[rT1])
                        dve(lambda: V.tensor_tensor(out=Xb, in0=Bt, in1=T1, op=ALU.mult), r=[rB, rT1], w=[rXb])
                        pool(lambda: G.tensor_tensor(out=Xc, in0=KMt, in1=T1, op=ALU.mult), r=[rKM, rT1], w=[rXc])
                        dve(lambda: V.tensor_tensor(out=T2, in0=R_, in1=Xd, op=ALU.mult), r=[r_pv, rXd], w=[rT2])
                        for fi, (src_, rs_) in enumerate([(Xa, rXa), (Xb, rXb), (Xc, rXc), (T2, rT2), (Xd, rXd)]):
                            for hb in range(2):
                                for j in range(4):
                                    hp = hb * 4 + j
                                    pe(lambda: T.transpose(out=ptf[:, j, :], in_=src_[:, hp * 128:(hp + 1) * 128], identity=ident_f[:, :]),
                                       r=[rs_, r_idf], w=[r_ptf], inc=(j == 3))
                                if (fi + hb) % 2 == 0:
                                    dve(lambda: V.tensor_copy(out=fm[fi][:, hb * 4:(hb + 1) * 4, :], in_=ptf[:, :, 0:64]), r=[r_ptf], w=[r_fm[fi]])
                                else:
                                    act(lambda: A.copy(out=fm[fi][:, hb * 4:(hb + 1) * 4, :], in_=ptf[:, :, 0:64]), r=[r_ptf], w=[r_fm[fi]])
                        Af, Bf, Kf, Rf, Ef = fm
                        rAf, rBf, rKf, rRf, rEf = r_fm
                        headmm(0, [(f3(Bf), f3(Af))], [rBf, rAf])
                        headmm(1, [(f3(Af), f3(Bf))], [rBf, rAf])
                        headmm(2, [(f3(Kf), f3(Af))], [rKf, rAf])
                        headmm(3, [(f3(Bf), f3(Rf))], [rBf, rRf])
                        headmm(4, [(f3(Kf), f3(Rf))], [rKf, rRf])
                        E = [mm[0], mm[1]]; ET = [mm[2], mm[3]]; rE = [r_mm[0], r_mm[1]]; rET = [r_mm[2], r_mm[3]]
                        Tm, TTm, Mm, Np, Mp, XT, nU, osb = mm[4], mm[5], mm[6], mm[7], mm[8], mm[9], mm[10], mm[11]
                        rTm, rTTm, rMm, rNp, rMp, rXT, rnU, rosb = (r_mm[i] for i in range(4, 12))
                        dve(lambda: V.scalar_tensor_tensor(out=E[0][:], in0=pbv(0), scalar=-1.0, in1=mb(0), op0=ALU.mult, op1=ALU.mult),
                            r=[r_pb[0], r_msk], w=[rE[0]])
                        dve(lambda: V.scalar_tensor_tensor(out=ET[0][:], in0=pbv(1), scalar=-1.0, in1=mb(2), op0=ALU.mult, op1=ALU.mult),
                            r=[r_pb[1], r_msk], w=[rET[0]])
                        pool(lambda: G.tensor_tensor(out=Tm[:], in0=E[0][:], in1=mb(3), op=ALU.add), r=[rE[0], r_msk], w=[rTm])
                        pool(lambda: G.tensor_tensor(out=TTm[:], in0=ET[0][:], in1=mb(3), op=ALU.add), r=[rET[0], r_msk], w=[rTTm])
                        dve(lambda: V.tensor_tensor(out=Mm[:], in0=pbv(2), in1=mb(0), op=ALU.mult), r=[r_pb[2], r_msk], w=[rMm])
                        dve(lambda: V.tensor_tensor(out=Np[:], in0=pbv(3), in1=mb(1), op=ALU.mult), r=[r_pb[3], r_msk], w=[rNp])
                        dve(lambda: V.tensor_tensor(out=Mp[:], in0=pbv(4), in1=mb(1), op=ALU.mult), r=[r_pb[4], r_msk], w=[rMp])
                        cur = 0
                        for lev in range(5):
                            nxt = 1 - cur
                            lastlev = (lev == 4)
                            headmm(0, [(f3(ET[cur]), f3(E[cur]))], [rET[cur], rE[cur]])
                            if not lastlev:
                                headmm(1, [(f3(E[cur]), f3(ET[cur]))], [rET[cur], rE[cur]])
                            act(lambda: A.copy(out=E[nxt][:], in_=pbv(0)), r=[r_pb[0]], w=[rE[nxt]])
                            if not lastlev:
                                dve(lambda: V.tensor_copy(out=ET[nxt][:], in_=pbv(1)), r=[r_pb[1]], w=[rET[nxt]])
                            headmm(2, [(f3(TTm), f3(E[nxt]))], [rTTm, rE[nxt]])
                            if not lastlev:
                                headmm(3, [(f3(E[nxt]), f3(TTm))], [rTTm, rE[nxt]])
                            dve(lambda: V.tensor_tensor(out=Tm[:], in0=pbv(2), in1=Tm[:], op=ALU.add), r=[r_pb[2], rTm], w=[rTm])
                            if not lastlev:
                                dve(lambda: V.tensor_tensor(out=TTm[:], in0=pbv(3), in1=TTm[:], op=ALU.add), r=[r_pb[3], rTTm], w=[rTTm])
                            cur = nxt
                        vcols = tmcols(V_)
                        headmm(0, [(f3(Af), f3(ST)), (f3(Mm), vcols)], [rAf, r_ST, rMm, r_pv])
                        act(lambda: A.copy(out=XT[:], in_=pbv(0)), r=[r_pb[0]], w=[rXT])
                        headmm(1, [(f3(Tm), f3(XT))], [rTm, rXT])
                        dve(lambda: V.tensor_scalar(out=nU[:], in0=pbv(1), scalar1=-1.0, scalar2=None, op0=ALU.mult), r=[r_pb[1]], w=[rnU])
                        headmm(2, [(f3(Rf), f3(ST)), (f3(Np), f3(nU)), (f3(Mp), vcols)], [rRf, r_ST, rNp, rnU, rMp, r_pv])
                        headmm(3, [(tmcols(BH), f3(nU)), (tmcols(KH), vcols)], [rBH, rnU, rKH, r_pv])
                        act(lambda: A.copy(out=osb[:], in_=pbv(2)), r=[r_pb[2]], w=[rosb])
                        dve(lambda: V.tensor_tensor(out=ST[:], in0=ST[:], in1=Ef[:, :, 63:64].to_broadcast([128, 8, 64]), op=ALU.mult),
                            r=[r_ST, rEf], w=[r_ST])
                        dve(lambda: V.tensor_tensor(out=ST[:], in0=pbv(3), in1=ST[:], op=ALU.add), r=[r_pb[3], r_ST], w=[r_ST])
                        sel = [mm[12], mm[13], XT]
                        rsel = [r_mm[12], r_mm[13], rXT]
                        w2 = [Tm, TTm]
                        rw2 = [rTm, rTTm]
                        for d in range(2):
                            P_ = slice(d * 64, (d + 1) * 64)
                            def hv(ap2d):
                                return ap2d[P_, :].rearrange("p (hp hh v) -> p hp hh v", hh=2, v=64)[:, :, d, :]
                            pool(lambda: G.tensor_copy(out=sel[0][P_, :, :], in_=hv(BON)), r=[rBON], w=[rsel[0]])
                            pool(lambda: G.tensor_copy(out=sel[1][P_, :, :], in_=hv(Gt)), r=[rG], w=[rsel[1]])
                            pool(lambda: G.tensor_copy(out=w2[0][P_, :, :], in_=hv(lwb[:, :])), r=[r_lwb], w=[rw2[0]])
                            pool(lambda: G.tensor_copy(out=w2[1][P_, :, :], in_=hv(lbb[:, :])), r=[r_lbb], w=[rw2[1]])
                        o3 = osb[:]
                        sq3 = nU[:]
                        dve(lambda: V.tensor_reduce(out=sm[:, 0:8], in_=o3, axis=AX.X, op=ALU.add), r=[rosb], w=[r_sm])
                        dve(lambda: V.tensor_scalar(out=sm[:, 0:8], in0=sm[:, 0:8], scalar1=1.0 / 64, scalar2=None, op0=ALU.mult), r=[r_sm], w=[r_sm])
                        dve(lambda: V.tensor_tensor(out=o3, in0=o3, in1=sm[:, 0:8].unsqueeze(2).to_broadcast([128, 8, 64]), op=ALU.subtract),
                            r=[rosb, r_sm], w=[rosb])
                        dve(lambda: V.tensor_tensor(out=sq3, in0=o3, in1=o3, op=ALU.mult), r=[rosb], w=[rnU])
                        dve(lambda: V.tensor_reduce(out=sm[:, 8:16], in_=sq3, axis=AX.X, op=ALU.add), r=[rnU], w=[r_sm])
                        dve(lambda: V.tensor_scalar(out=sm[:, 8:16], in0=sm[:, 8:16], scalar1=1.0 / 64, scalar2=64e-5, op0=ALU.mult, op1=ALU.add),
                            r=[r_sm], w=[r_sm])
                        act(lambda: A.activation(out=sm[:, 8:16], in_=sm[:, 8:16], func=AF.Sqrt), r=[r_sm], w=[r_sm])
                        dve(lambda: V.reciprocal(out=sm[:, 16:24], in_=sm[:, 8:16]), r=[r_sm], w=[r_sm])
                        dve(lambda: V.tensor_tensor(out=o3, in0=o3, in1=sm[:, 16:24].unsqueeze(2).to_broadcast([128, 8, 64]), op=ALU.mult),
                            r=[rosb, r_sm], w=[rosb])
                        dve(lambda: V.tensor_tensor(out=o3, in0=o3, in1=w2[0][:], op=ALU.mult), r=[rosb, rw2[0]], w=[rosb])
                        dve(lambda: V.tensor_tensor(out=o3, in0=o3, in1=w2[1][:], op=ALU.add), r=[rosb, rw2[1]], w=[rosb])
                        dve(lambda: V.tensor_tensor(out=o3, in0=o3, in1=sel[0][:], op=ALU.add), r=[rosb, rsel[0]], w=[rosb])
                        dve(lambda: V.tensor_tensor(out=o3, in0=o3, in1=sel[1][:], op=ALU.mult), r=[rosb, rsel[1]], w=[rosb])
                        for d in range(2):
                            dst = MIX[ta:ta + nv, 1024:2048].rearrange("t (hp hh v) -> t hp hh v", hh=2, v=64)[:, :, d, :]
                            ld(dst, osb[d * 64:d * 64 + nv, :, :], r=[rosb], w=[rMIX])
                    for hb in range(2):
                        for j in range(4):
                            pe(lambda: T.transpose(out=ptf[0:64, j, :], in_=ST[:, hb * 4 + j, :], identity=ident_f[:, :]),
                               r=[r_ST, r_idf], w=[r_ptf], inc=(j == 3))
                        dve(lambda: V.tensor_copy(out=stmp[:, hb * 4:(hb + 1) * 4, :], in_=ptf[0:64, :, :]), r=[r_ptf, r_stmp], w=[r_stmp])
                    for hh in range(2):
                        ld(WKVO[l, si].rearrange("(hp hh) v k -> hh v hp k", hh=2)[hh],
                           stmp[:, :, hh * 64:(hh + 1) * 64], r=[r_stmp], w=[rOUT])
            fw.barrier()

        src, rsrc = X, rX
        outs = [(RA, rRA), (RB, rRB)]
        for l in range(2):
            dst, rdst = outs[l]
            with ExitStack() as st:
                token_local(st, [job_ffn(l, 1, src, rsrc, R1, rR1), job_win(l, R1, rR1)])
            fw.barrier()
            attention(l)
            rwkv(l)
            gmlp_pool(l)
            with ExitStack() as st:
                token_local(st, [job_wout(l, R1, rR1, R2, rR2), job_ffn(l, 2, R2, rR2, dst, rdst)])
            fw.barrier()
            src, rsrc = dst, rdst
        final_norm(src, rsrc)
        fw.finish()
    return nc


_NC_CACHE = {}


def kernel(**inputs):
    inp = {k: np.ascontiguousarray(np.asarray(v, dtype=np.float32)) for k, v in inputs.items()}
    consts = _host_consts()
    if "nc" not in _NC_CACHE:
        _NC_CACHE["nc"] = build_nc()
    nc = _NC_CACHE["nc"]
    shared = {}
    for n, shp in WNAMES:
        shared[n] = inp[n].reshape(shp)
    shared.update(consts)
    in_maps = []
    for c in range(8):
        m = dict(shared)
        m["x_all"] = np.concatenate([inp["x_prompt"][c % 4], inp["x_sample"][c]], axis=0)
        m["cache_k"] = inp["cache_k_swa"][:, c].reshape(2, NPREV, 1024)
        m["cache_v"] = inp["cache_v_swa"][:, c].reshape(2, NPREV, 1024)
        m["wkv0"] = inp["state_rwkv_wkv"][:, c]
        m["shift0"] = inp["state_rwkv_shift"][:, c].reshape(2, 1, BFEAT)
        m["pool0"] = inp["state_pool"][:, c]
        in_maps.append({k: np.ascontiguousarray(v) for k, v in m.items()})
    res = run_bass_kernel_spmd(nc, in_maps, core_ids=list(range(8)))
    R = res.results
    y_p = np.stack([R[b]["y"][:TP] for b in range(4)])
    y_s = np.stack([R[c]["y"][TP:] for c in range(8)])
    nk_p = np.stack([R[b]["newk"][:, :TP] for b in range(4)], axis=1).reshape(2, 4, TP, 8, 128)
    nv_p = np.stack([R[b]["newv"][:, :TP] for b in range(4)], axis=1).reshape(2, 4, TP, 8, 128)
    wkv_p = np.stack([R[b]["wkvo"][:, 0] for b in range(4)], axis=1)
    sh_p = np.stack([R[b]["sho"][:, 0] for b in range(4)], axis=1)
    pl_p = np.stack([R[b]["plo"][:, 0] for b in range(4)], axis=1)
    nk_s = np.stack([R[c]["newk"][:, TP:] for c in range(8)], axis=1).reshape(2, 8, TS, 8, 128)
    nv_s = np.stack([R[c]["newv"][:, TP:] for c in range(8)], axis=1).reshape(2, 8, TS, 8, 128)
    wkv_s = np.stack([R[c]["wkvo"][:, 1] for c in range(8)], axis=1)
    sh_s = np.stack([R[c]["sho"][:, 1] for c in range(8)], axis=1)
    pl_s = np.stack([R[c]["plo"][:, 1] for c in range(8)], axis=1)
    gv_s = np.stack([R[c]["gvo"] for c in range(8)], axis=1)
    outs = (y_p, y_s, nk_p, nv_p, wkv_p, sh_p, pl_p, nk_s, nv_s, wkv_s, sh_s, pl_s, gv_s)
    return tuple(np.ascontiguousarray(o.astype(np.float32)) for o in outs)
```

```python
import numpy as np
from contextlib import ExitStack
import concourse.bass as bass
import concourse.mybir as mybir
from concourse.bass_utils import run_bass_kernel_spmd

F32 = mybir.dt.float32
BF16 = mybir.dt.bfloat16
AF = mybir.ActivationFunctionType
ALU = mybir.AluOpType
AX = mybir.AxisListType

D = 4096
DFF = 11008
NJ = DFF // 128
PROJ = 9504
TP = 2048
TS = 8
TALL = TP + TS
NPREV = 2048
BFEAT = 3360
MASKW = 3200
C0 = 512
EPS = 1e-6


class Res:
    __slots__ = ("w", "r")

    def __init__(self):
        self.w = None
        self.r = {}


class Eng:
    def __init__(self, fw, key, eng, compute=True):
        self.key = key
        self.e = eng
        self.sem = fw.new_sem("p_" + key) if compute else None
        self.cnt = 0
        self.waited = {}


class FW:
    def __init__(self, nc, stack, n_dma_sems=20):
        self.nc = nc
        self.stack = stack
        self.pe = Eng(self, "pe", nc.tensor)
        self.dve = Eng(self, "dve", nc.vector)
        self.act = Eng(self, "act", nc.scalar)
        self.pool = Eng(self, "pool", nc.gpsimd)
        self.sp = Eng(self, "sp", nc.sync, compute=False)
        self.engs = [self.pe, self.dve, self.act, self.pool, self.sp]
        self.dring = {}
        for q in (self.sp, self.pool):
            self.dring[q.key] = [[self.new_sem("d_%s_%d" % (q.key, i)), 0] for i in range(n_dma_sems)]
        self.dpos = {"sp": 0, "pool": 0}

    def new_sem(self, name):
        return self.stack.enter_context(self.nc.semaphore(name))

    def _wait(self, E, tok):
        if tok is None:
            return
        sem, val, key = tok
        if key == "pe" and E.key == "pe":
            return
        k = id(sem)
        if E.waited.get(k, 0) >= val:
            return
        E.e.wait_ge(sem, val)
        E.waited[k] = val

    def _deps(self, E, reads, writes):
        for r in reads:
            self._wait(E, r.w)
        for w in writes:
            self._wait(E, w.w)
            for t in w.r.values():
                self._wait(E, t)

    def _commit(self, tok, reads, writes):
        for r in reads:
            r.r[id(tok[0])] = tok
        for w in writes:
            w.w = tok
            w.r = {}

    def op(self, E, fn, reads=(), writes=(), inc=True):
        self._deps(E, reads, writes)
        ins = fn()
        if inc:
            E.cnt += 1
            ins.then_inc(E.sem, 1)
            tok = (E.sem, E.cnt, E.key)
        else:
            tok = (E.sem, E.cnt + 1, E.key)
        self._commit(tok, reads, writes)
        return tok

    def dma(self, Q, out, in_, reads=(), writes=()):
        self._deps(Q, reads, writes)
        ring = self.dring[Q.key]
        pos = self.dpos[Q.key]
        self.dpos[Q.key] = (pos + 1) % len(ring)
        ent = ring[pos]
        if ent[1] > 0:
            self._wait(Q, (ent[0], ent[1], "dma"))
        ins = Q.e.dma_start(out=out, in_=in_)
        ent[1] += 16
        ins.then_inc(ent[0], 16)
        tok = (ent[0], ent[1], "dma")
        self._commit(tok, reads, writes)
        return tok

    def all_tokens(self):
        toks = []
        for q in self.dring.values():
            for ent in q:
                if ent[1] > 0:
                    toks.append((ent[0], ent[1], "dma"))
        for E in (self.pe, self.dve, self.act, self.pool):
            if E.cnt > 0:
                toks.append((E.sem, E.cnt, E.key))
        return toks

    def barrier(self):
        toks = self.all_tokens()
        for E in self.engs:
            for t in toks:
                self._wait(E, t)

    def finish(self):
        for t in self.all_tokens():
            self._wait(self.sp, t)


def _host_consts():
    half = 64
    inv = (10000.0 ** (-np.arange(half, dtype=np.float32) / half)).astype(np.float32)
    pos = np.concatenate([np.arange(TP), 16384 + np.arange(TS)]).astype(np.float32)
    ang = pos[:, None] * inv[None, :]
    cs = np.concatenate([np.cos(ang), np.sin(ang)], axis=1).astype(np.float32)
    d = np.arange(MASKW)[None, :] - np.arange(128)[:, None] - C0
    m = ((d >= 0) & (d <= 128)).astype(np.float32) + ((d >= 0) & (d <= 512) & (d % 4 == 0)) + \
        ((d >= 0) & (d <= 2048) & (d % 16 == 0))
    mt = m.astype(np.float32)
    pm = np.zeros((128, 4, 3, 128), np.float32)
    pms = np.zeros((32, 4, 8), np.float32)
    tl = np.arange(128)
    for g, win in enumerate((2, 4, 8, 16)):
        for t in range(128):
            for tp in range(t - win + 1, t + 1):
                if tp >= 0:
                    pm[tp, g, 0, t] += 1.0 / win
                    pm[tp, g, 2, t] += 1.0 / min(win, t + 1)
                else:
                    pm[128 + tp, g, 1, t] += 1.0 / win
            pm[t, g, 0, t] -= 1.0
            pm[t, g, 2, t] -= 1.0
        for t in range(8):
            for e in range(15 + t - win + 1, 15 + t + 1):
                pms[e, g, t] += 1.0 / win
            pms[15 + t, g, t] -= 1.0
    tril = np.tril(np.ones((128, 128), np.float32))
    tri2 = np.zeros((128, 128), np.float32)
    ones2 = np.zeros((128, 128), np.float32)
    mskc = np.zeros((128, 4, 64), np.float32)
    ii = np.arange(64)
    for d_ in range(2):
        sl = slice(d_ * 64, (d_ + 1) * 64)
        tri2[sl, sl] = (ii[:, None] <= ii[None, :])
        ones2[sl, sl] = 1.0
        mskc[sl, 0] = (ii[:, None] < ii[None, :])
        mskc[sl, 1] = (ii[:, None] <= ii[None, :])
        mskc[sl, 2] = (ii[:, None] > ii[None, :])
        mskc[sl, 3] = (ii[:, None] == ii[None, :])
    return dict(c_cs=cs, c_mt=mt, c_pm=pm.reshape(128, 4 * 3 * 128), c_pms=pms.reshape(32, 32), c_tril=tril,
                c_tri2=tri2, c_ones2=ones2, c_msk=mskc.reshape(128, 256))


WNAMES = [("ln_ffn1", [2, 32, 128]), ("w1_gate", [2, D, DFF]), ("w1_up", [2, D, DFF]), ("w1_down", [2, DFF, D]),
          ("ln_mix", [2, 32, 128]), ("w_in", [2, D, PROJ]), ("g_out_a", [2, 1, 1024]), ("mu_b", [2, 1, BFEAT]),
          ("w0", [2, 1, 1024]), ("w_up", [2, 64, 1024]), ("a0", [2, 1, 1024]), ("a_up", [2, 64, 1024]),
          ("g_up", [2, 160, 1024]), ("k_k", [2, 1, 1024]), ("k_a", [2, 1, 1024]), ("r_k", [2, 1, 1024]),
          ("ln_x_w", [2, 1, 1024]), ("ln_x_b", [2, 1, 1024]), ("w_s", [2, 8, 128, 128]), ("b_s", [2, 8, 128]),
          ("g_out_c", [2, 1, 1024]), ("w_pool", [2, 4, 256, 256]), ("pool_scale", [2, 1, 1024]),
          ("g_out_d", [2, 1, 1024]), ("w_out", [2, D, D]), ("ln_ffn2", [2, 32, 128]), ("w2_gate", [2, D, DFF]),
          ("w2_up", [2, D, DFF]), ("w2_down", [2, DFF, D]), ("ln_final", [32, 128])]


def build_nc():
    nc = bass.Bass("TRN2", target_bir_lowering=False)

    def din(name, shape):
        return nc.dram_tensor(name, list(shape), F32, kind="ExternalInput").ap()

    def dout(name, shape):
        return nc.dram_tensor(name, list(shape), F32, kind="ExternalOutput").ap()

    def dscr(name, shape):
        return nc.dram_tensor(name, list(shape), F32, kind="Internal").ap()

    X = din("x_all", [TALL, D])
    CK = din("cache_k", [2, NPREV, 1024])
    CV = din("cache_v", [2, NPREV, 1024])
    WKV0 = din("wkv0", [2, 16, 64, 64])
    SH0 = din("shift0", [2, 1, BFEAT])
    PL0 = din("pool0", [2, 15, 1024])
    Wd = {n: din(n, s) for n, s in WNAMES}
    CCS = din("c_cs", [TALL, 128])
    CMT = din("c_mt", [128, MASKW])
    CPM = din("c_pm", [128, 4 * 3 * 128])
    CPMS = din("c_pms", [32, 32])
    CTRIL = din("c_tril", [128, 128])
    CTRI2 = din("c_tri2", [128, 128])
    CONES2 = din("c_ones2", [128, 128])
    CMSK = din("c_msk", [128, 256])

    Y = dout("y", [TALL, D])
    NEWK = dout("newk", [2, TALL, 1024])
    NEWV = dout("newv", [2, TALL, 1024])
    WKVO = dout("wkvo", [2, 2, 16, 64, 64])
    SHO = dout("sho", [2, 2, BFEAT])
    PLO = dout("plo", [2, 2, 15, 1024])
    GVO = dout("gvo", [2, TS, 1024])

    R1 = dscr("r1", [TALL, D])
    R2 = dscr("r2", [TALL, D])
    RA = dscr("ra", [TALL, D])
    RB = dscr("rb", [TALL, D])
    Z = dscr("z", [TALL, PROJ])
    MIX = dscr("mix", [TALL, D])
    XB = dscr("xb", [TALL, 2, 5, 512])
    OT = dscr("ot", [TALL, 1024])

    rX, rR1, rR2, rRA, rRB, rZ, rMIX, rXB, rOT, rOUT = (Res() for _ in range(10))

    top = ExitStack()
    with top:
        fw = FW(nc, top)
        PE, DVE, ACT, POOL, SP = fw.pe, fw.dve, fw.act, fw.pool, fw.sp
        T, V, A, G = nc.tensor, nc.vector, nc.scalar, nc.gpsimd

        _uid = [0]

        def sb(st, name, shape, dt=F32):
            _uid[0] += 1
            return st.enter_context(nc.sbuf_tensor("%s_%d" % (name, _uid[0]), list(shape), dt))

        def pe(fn, r=(), w=(), inc=True):
            return fw.op(PE, fn, r, w, inc)

        def dve(fn, r=(), w=()):
            return fw.op(DVE, fn, r, w)

        def act(fn, r=(), w=()):
            return fw.op(ACT, fn, r, w)

        def pool(fn, r=(), w=()):
            return fw.op(POOL, fn, r, w)

        def ld(out, in_, r=(), w=()):
            return fw.dma(SP, out, in_, r, w)

        def ldc(out, in_, r=(), w=()):
            return fw.dma(POOL, out, in_, r, w)

        ident_f = sb(top, "ident_f", [128, 128]); r_idf = Res()
        ident_b = sb(top, "ident_b", [128, 128], BF16); r_idb = Res()
        pool(lambda: G.memset(ident_f[:], 0.0), w=[r_idf])
        pool(lambda: G.affine_select(out=ident_f[:], in_=ident_f[:], pattern=[[-1, 128]], compare_op=ALU.not_equal,
                                     fill=1.0, base=0, channel_multiplier=1), r=[r_idf], w=[r_idf])
        dve(lambda: V.tensor_copy(out=ident_b[:], in_=ident_f[:]), r=[r_idf], w=[r_idb])

        NPB = 6
        pb = [top.enter_context(nc.psum_tensor("pb%d" % i, [128, 512], F32)) for i in range(NPB)]
        r_pb = [Res() for _ in range(NPB)]
        ptb = top.enter_context(nc.psum_tensor("ptb", [128, 8, 128], BF16)); r_ptb = [Res(), Res()]
        ptf = top.enter_context(nc.psum_tensor("ptf", [128, 4, 128], F32)); r_ptf = Res()

        def token_local(st, jobs):
            Xn = sb(st, "Xn", [128, 32, 688], BF16); r_Xn = Res()
            H = sb(st, "H", [128, 44, 688], BF16); r_H = Res()
            NW = 3
            wbuf = [sb(st, "wb%d" % i, [128, 5632], BF16) for i in range(NW)]
            r_wb = [Res() for _ in range(NW)]
            xst = sb(st, "xst", [128, D]); r_xst = Res()
            xnb = sb(st, "xnb", [128, D], BF16); r_xnb = Res()
            ss = sb(st, "ss", [128, 8]); r_ss = Res()
            gcol = sb(st, "gcol", [128, 32]); r_gcol = Res()
            graw = sb(st, "graw", [32, 128]); r_graw = Res()
            sg = [sb(st, "sg%d" % i, [128, 2, 688]) for i in range(2)]; r_sg = [Res(), Res()]
            rsd = [sb(st, "rsd%d" % i, [128, 512]) for i in range(2)]; r_rsd = [Res(), Res()]
            yo = [sb(st, "yo%d" % i, [128, 512]) for i in range(2)]; r_yo = [Res(), Res()]
            cnt = {"w": 0, "sg": 0, "rsd": 0, "yo": 0, "pb": 0}

            def load_gcol(gsrc):
                ld(graw[:], gsrc, w=[r_graw])
                pe(lambda: T.transpose(out=ptf[:, 0, 0:32], in_=graw[:, :], identity=ident_f[0:32, 0:32]),
                   r=[r_graw, r_idf], w=[r_ptf])
                dve(lambda: V.tensor_copy(out=gcol[:], in_=ptf[:, 0, 0:32]), r=[r_ptf], w=[r_gcol])

            def load_norm(src, rsrc, t0, TT, norm=True):
                nsub = (TT + 127) // 128
                for s in range(nsub):
                    sub = min(128, TT - s * 128)
                    rows = slice(t0 + s * 128, t0 + s * 128 + sub)
                    if norm:
                        ld(xst[:sub, :], src[rows, :], r=[rsrc], w=[r_xst])
                        act(lambda: A.activation(out=xnb[:sub, :], in_=xst[:sub, :], func=AF.Square,
                                                 accum_out=ss[:sub, 0:1]), r=[r_xst], w=[r_xnb, r_ss])
                        dve(lambda: V.tensor_scalar(out=ss[:sub, 1:2], in0=ss[:sub, 0:1], scalar1=1.0 / D, scalar2=EPS,
                                                    op0=ALU.mult, op1=ALU.add), r=[r_ss], w=[r_ss])
                        act(lambda: A.activation(out=ss[:sub, 2:3], in_=ss[:sub, 1:2], func=AF.Sqrt), r=[r_ss], w=[r_ss])
                        dve(lambda: V.reciprocal(out=ss[:sub, 3:4], in_=ss[:sub, 2:3]), r=[r_ss], w=[r_ss])
                        act(lambda: A.activation(out=xnb[:sub, :], in_=xst[:sub, :], func=AF.Copy, scale=ss[:sub, 3:4]),
                            r=[r_xst, r_ss], w=[r_xnb])
                    else:
                        ldc(xnb[:sub, :], src[rows, :], r=[rsrc], w=[r_xnb])
                    for c8 in range(4):
                        hb = c8 % 2
                        for j in range(8):
                            c = c8 * 8 + j
                            pe(lambda c=c, j=j: T.transpose(out=ptb[:, j, 0:sub], in_=xnb[:sub, c * 128:(c + 1) * 128],
                                                            identity=ident_b[0:sub, 0:sub]),
                               r=[r_xnb, r_idb], w=[r_ptb[0]], inc=(j == 7))
                        for j in range(8):
                            c = c8 * 8 + j
                            if norm:
                                dve(lambda c=c, j=j: V.tensor_scalar(out=Xn[:, c, s * 128:s * 128 + sub], in0=ptb[:, j, 0:sub],
                                                                     scalar1=gcol[:, c:c + 1], scalar2=None, op0=ALU.mult),
                                    r=[r_ptb[0], r_gcol], w=[r_Xn])
                            else:
                                dve(lambda c=c, j=j: V.tensor_copy(out=Xn[:, c, s * 128:s * 128 + sub], in_=ptb[:, j, 0:sub]),
                                    r=[r_ptb[0]], w=[r_Xn])

            def wload(view_shape, src_ap):
                i = cnt["w"] % NW
                cnt["w"] += 1
                n = 1
                for v in view_shape[1:]:
                    n *= v
                flat = wbuf[i][:, 0:n]
                if len(view_shape) == 3:
                    view = flat.rearrange("p (a b) -> p a b", a=view_shape[1])
                else:
                    view = flat
                ldc(view, src_ap, w=[r_wb[i]])
                return view, r_wb[i]

            def ffn(TT, wg, wu, wdn, resid, r_resid, dst, r_dst, t0):
                nsub = (TT + 127) // 128
                n0 = TT // 2
                thb = [(0, n0), (n0, TT - n0)]
                pairc = [0]
                for half in range(2):
                    nch = 44 if half == 0 else 42
                    jb = 0 if half == 0 else 44
                    tiles = [(gu, mt, kh) for mt in range(nch // 2) for gu in range(2) for kh in range(2)]
                    loaded = {}

                    def issue(i):
                        if i < len(tiles):
                            gu, mt, kh = tiles[i]
                            wsrc = wg if gu == 0 else wu
                            c0 = (jb + mt * 2) * 128
                            src = wsrc[kh * 2048:(kh + 1) * 2048, c0:c0 + 256].rearrange("(c p) n -> p c n", p=128)
                            loaded[i] = wload([128, 16, 256], src)
                    issue(0)
                    issue(1)
                    pair_of = {}
                    for i, (gu, mt, kh) in enumerate(tiles):
                        issue(i + 2)
                        wv, rw = loaded.pop(i)
                        si = mt % 2
                        for mc in range(2):
                            if kh == 0:
                                pair_of[(gu, mc)] = pairc[0] % 3
                                pairc[0] += 1
                            pr = pair_of[(gu, mc)]
                            for th, (c0, n) in enumerate(thb):
                                bank = pr * 2 + th
                                for k in range(16):
                                    pe(lambda: T.matmul(pb[bank][:, 0:n], lhsT=wv[:, k, mc * 128:(mc + 1) * 128],
                                                        rhs=Xn[:, kh * 16 + k, c0:c0 + n],
                                                        start=(kh == 0 and k == 0), stop=(kh == 1 and k == 15)),
                                       r=[rw, r_Xn], w=[r_pb[bank]], inc=(mc == 1 and th == 1 and k == 15))
                        if kh == 1:
                            for mc in range(2):
                                pr = pair_of[(gu, mc)]
                                for th, (c0, n) in enumerate(thb):
                                    bank = pr * 2 + th
                                    if gu == 0:
                                        act(lambda: A.activation(out=sg[si][:, mc, c0:c0 + n], in_=pb[bank][:, 0:n], func=AF.Silu),
                                            r=[r_pb[bank]], w=[r_sg[si]])
                                    else:
                                        dve(lambda: V.tensor_tensor(out=H[:, mt * 2 + mc, c0:c0 + n], in0=pb[bank][:, 0:n],
                                                                    in1=sg[si][:, mc, c0:c0 + n], op=ALU.mult),
                                            r=[r_pb[bank], r_sg[si]], w=[r_H])
                    groups = [(0, 11), (11, 11), (22, 11), (33, 11)] if half == 0 else [(0, 11), (11, 11), (22, 10), (32, 10)]
                    tiles = [(fb, g) for fb in range(8) for g in range(4)]
                    loaded = {}

                    def issue2(i):
                        if i < len(tiles):
                            fb, g = tiles[i]
                            j0, nj = groups[g]
                            src = wdn[(jb + j0) * 128:(jb + j0 + nj) * 128, fb * 512:(fb + 1) * 512].rearrange("(c p) n -> p c n", p=128)
                            loaded[i] = wload([128, nj, 512], src)
                    issue2(0)
                    issue2(1)
                    rs_src, rs_res = (resid, r_resid) if half == 0 else (dst, r_dst)
                    for i, (fb, g) in enumerate(tiles):
                        issue2(i + 2)
                        wv, rw = loaded.pop(i)
                        j0, nj = groups[g]
                        for s in range(nsub):
                            sub = min(128, TT - s * 128)
                            for jj in range(nj):
                                j = j0 + jj
                                pe(lambda: T.matmul(pb[s][:sub, :], lhsT=H[:, j, s * 128:s * 128 + sub], rhs=wv[:, jj, :],
                                                    start=(j == 0), stop=(j == nch - 1)), r=[rw, r_H], w=[r_pb[s]],
                                   inc=(s == nsub - 1 and jj == nj - 1))
                        if g == 3:
                            for s in range(nsub):
                                sub = min(128, TT - s * 128)
                                rows = slice(t0 + s * 128, t0 + s * 128 + sub)
                                cols = slice(fb * 512, (fb + 1) * 512)
                                ri = cnt["rsd"] % 2
                                cnt["rsd"] += 1
                                ld(rsd[ri][:sub, :], rs_src[rows, cols], r=[rs_res], w=[r_rsd[ri]])
                                dve(lambda: V.scalar_tensor_tensor(
                                    out=yo[ri][:sub, :], in0=pb[s][:sub, :], scalar=0.5, in1=rsd[ri][:sub, :],
                                    op0=ALU.mult, op1=ALU.add), r=[r_pb[s], r_rsd[ri]], w=[r_yo[ri]])
                                ld(dst[rows, cols], yo[ri][:sub, :], r=[r_yo[ri]], w=[r_dst])

            def linear_tm(TT, wsrc, ncols, evac):
                nsub = (TT + 127) // 128
                ncb_n = (ncols + 511) // 512
                tiles = [(cb, kg) for cb in range(ncb_n) for kg in range(4)]
                loaded = {}

                def issue(i):
                    if i < len(tiles):
                        cb, kg = tiles[i]
                        ncb = min(512, ncols - cb * 512)
                        src = wsrc[kg * 1024:(kg + 1) * 1024, cb * 512:cb * 512 + ncb].rearrange("(c p) n -> p c n", p=128)
                        loaded[i] = wload([128, 8, ncb], src)
                issue(0)
                issue(1)
                for i, (cb, kg) in enumerate(tiles):
                    issue(i + 2)
                    wv, rw = loaded.pop(i)
                    ncb = min(512, ncols - cb * 512)
                    for s in range(nsub):
                        sub = min(128, TT - s * 128)
                        for k in range(8):
                            pe(lambda s=s, sub=sub, k=k, wv=wv: T.matmul(
                                pb[s][:sub, 0:ncb], lhsT=Xn[:, kg * 8 + k, s * 128:s * 128 + sub], rhs=wv[:, k, :],
                                start=(kg == 0 and k == 0), stop=(kg == 3 and k == 7)), r=[rw, r_Xn], w=[r_pb[s]],
                               inc=(s == nsub - 1 and k == 7))
                    if kg == 3:
                        for s in range(nsub):
                            sub = min(128, TT - s * 128)
                            evac(s, sub, cb, ncb)

            ctx = dict(load_gcol=load_gcol, load_norm=load_norm, ffn=ffn, linear_tm=linear_tm, rsd=rsd, r_rsd=r_rsd,
                       yo=yo, r_yo=r_yo, cnt=cnt)
            for job in jobs:
                job(ctx)

        TILES = [(0, 688), (688, 688), (1376, 680)]

        def job_ffn(l, which, src, rsrc, dst, rdst):
            def run(ctx):
                gname = "ln_ffn1" if which == 1 else "ln_ffn2"
                ctx["load_gcol"](Wd[gname][l])
                for (t0, TT) in TILES:
                    ctx["load_norm"](src, rsrc, t0, TT)
                    ctx["ffn"](TT, Wd["w%d_gate" % which][l], Wd["w%d_up" % which][l], Wd["w%d_down" % which][l],
                               src, rsrc, dst, rdst, t0)
            return run

        def job_win(l, src, rsrc):
            def run(ctx):
                ctx["load_gcol"](Wd["ln_mix"][l])
                yo, r_yo, cnt = ctx["yo"], ctx["r_yo"], ctx["cnt"]
                for (t0, TT) in TILES:
                    ctx["load_norm"](src, rsrc, t0, TT)

                    def evac(s, sub, cb, ncb, t0=t0):
                        ri = cnt["yo"] % 2
                        cnt["yo"] += 1
                        rows = slice(t0 + s * 128, t0 + s * 128 + sub)
                        act(lambda: A.copy(out=yo[ri][:sub, 0:ncb], in_=pb[s][:sub, 0:ncb]), r=[r_pb[s]], w=[r_yo[ri]])
                        ld(Z[rows, cb * 512:cb * 512 + ncb], yo[ri][:sub, 0:ncb], r=[r_yo[ri]], w=[rZ])
                    ctx["linear_tm"](TT, Wd["w_in"][l], PROJ, evac)
            return run

        def job_wout(l, hsrc, rh, dst, rdst):
            def run(ctx):
                yo, r_yo, rsd, r_rsd, cnt = ctx["yo"], ctx["r_yo"], ctx["rsd"], ctx["r_rsd"], ctx["cnt"]
                for (t0, TT) in TILES:
                    ctx["load_norm"](MIX, rMIX, t0, TT, norm=False)

                    def evac(s, sub, cb, ncb, t0=t0):
                        ri = cnt["yo"] % 2
                        cnt["yo"] += 1
                        rows = slice(t0 + s * 128, t0 + s * 128 + sub)
                        cols = slice(cb * 512, cb * 512 + ncb)
                        ld(rsd[ri][:sub, :], hsrc[rows, cols], r=[rh], w=[r_rsd[ri]])
                        dve(lambda: V.tensor_tensor(out=yo[ri][:sub, :], in0=pb[s][:sub, :], in1=rsd[ri][:sub, :], op=ALU.add),
                            r=[r_pb[s], r_rsd[ri]], w=[r_yo[ri]])
                        ld(dst[rows, cols], yo[ri][:sub, :], r=[r_yo[ri]], w=[rdst])
                    ctx["linear_tm"](TT, Wd["w_out"][l], D, evac)
            return run

        def job_final(src, rsrc):
            def run(ctx):
                pass
            return run

        def rms_rows(st_tiles, x_ap, sub, gbc_ap, out_ap, r_x, r_g, r_out, n):
            junk, r_junk, ss, r_ss = st_tiles
            act(lambda: A.activation(out=junk[:sub, 0:n], in_=x_ap, func=AF.Square, accum_out=ss[:sub, 0:1]),
                r=[r_x], w=[r_junk, r_ss])
            dve(lambda: V.tensor_scalar(out=ss[:sub, 1:2], in0=ss[:sub, 0:1], scalar1=1.0 / n, scalar2=EPS,
                                        op0=ALU.mult, op1=ALU.add), r=[r_ss], w=[r_ss])
            act(lambda: A.activation(out=ss[:sub, 2:3], in_=ss[:sub, 1:2], func=AF.Sqrt), r=[r_ss], w=[r_ss])
            dve(lambda: V.reciprocal(out=ss[:sub, 3:4], in_=ss[:sub, 2:3]), r=[r_ss], w=[r_ss])
            dve(lambda: V.scalar_tensor_tensor(out=out_ap, in0=x_ap, scalar=ss[:sub, 3:4], in1=gbc_ap,
                                               op0=ALU.mult, op1=ALU.mult), r=[r_x, r_ss, r_g], w=[r_out])

        SEQS = [dict(r0=0, T=TP, nprev=0, si=0), dict(r0=TP, T=TS, nprev=NPREV, si=1)]

        def final_norm(src, rsrc):
            with ExitStack() as st:
                xst = sb(st, "f_x", [128, D]); r_x = Res()
                gbc = sb(st, "f_g", [128, D]); r_g = Res()
                junk = sb(st, "f_j", [128, D]); r_j = Res()
                ss = sb(st, "f_ss", [128, 8]); r_ss = Res()
                yo = sb(st, "f_y", [128, D]); r_y = Res()
                ld(gbc[:], Wd["ln_final"].rearrange("a b -> (a b)").unsqueeze(0).partition_broadcast(128), w=[r_g])
                for (t0, TT) in [(i * 128, 128) for i in range(16)] + [(TP, TS)]:
                    ld(xst[:TT, :], src[t0:t0 + TT, :], r=[rsrc], w=[r_x])
                    rms_rows((junk, r_j, ss, r_ss), xst[:TT, :], TT, gbc[:TT, :], yo[:TT, :], r_x, r_g, r_y, D)
                    ld(Y[t0:t0 + TT, :], yo[:TT, :], r=[r_y], w=[rOUT])
            fw.barrier()

        def attention(l):
            for sq in SEQS:
                with ExitStack() as st:
                    r0, Tn, nprev = sq["r0"], sq["T"], sq["nprev"]
                    Tk = nprev + Tn
                    nkt = (Tk + 127) // 128
                    qT = sb(st, "qT", [128, 8, max(Tn, 128)], BF16); r_qT = Res()
                    kT = sb(st, "kT", [128, 8, nkt * 128], BF16); r_kT = Res()
                    va = sb(st, "va", [128, nkt, 8, 130], BF16); r_va = Res()
                    mt = sb(st, "mt", [128, MASKW], BF16); r_mt = Res()
                    zqk = sb(st, "zqk", [128, 2048]); r_zqk = Res()
                    rot = sb(st, "rot", [128, 2048]); r_rot = Res()
                    rotb = sb(st, "rotb", [128, 2048], BF16); r_rotb = Res()
                    t1 = sb(st, "t1", [128, 1024]); r_t1 = Res()
                    t2 = sb(st, "t2", [128, 1024]); r_t2 = Res()
                    cs = sb(st, "cs", [128, 128]); r_cs = Res()
                    eb = [sb(st, "eb%d" % i, [128, 512], BF16) for i in range(2)]; r_eb = [Res(), Res()]
                    pbf = [sb(st, "pbf%d" % i, [128, 512], BF16) for i in range(2)]; r_pbf = [Res(), Res()]
                    oa = sb(st, "oa", [128, 4, 1024]); r_oa = [Res() for _ in range(4)]
                    gbc = sb(st, "gbc", [128, 1024]); r_gbc = Res()
                    junk = sb(st, "ajunk", [128, 1024]); r_junk = Res()
                    ss = sb(st, "ass", [128, 8]); r_ss = Res()
                    rd = sb(st, "ard", [128, 8]); r_rd = Res()
                    yo = sb(st, "ayo", [128, 1024]); r_yo = Res()
                    ldc(mt[:], CMT[:, :], w=[r_mt])
                    ld(gbc[:], Wd["g_out_a"][l].partition_broadcast(128), w=[r_gbc])
                    pool(lambda: G.memset(kT[:], 0.0), w=[r_kT])
                    pool(lambda: G.memset(va[:], 0.0), w=[r_va])
                    pool(lambda: G.memset(va[:, :, :, 128:129], 1.0), w=[r_va])
                    if nprev > 0:
                        kc = sb(st, "kc", [128, 16, 1024], BF16); r_kc = Res()
                        ldc(kc[:], CK[l].rearrange("(s p) c -> p s c", p=128), w=[r_kc])
                        for s in range(16):
                            ldc(va[:, s, :, 0:128], CV[l, s * 128:(s + 1) * 128, :].rearrange("p (h d) -> p h d", d=128), r=[r_va], w=[r_va])
                        for s in range(16):
                            for h in range(8):
                                pe(lambda s=s, h=h: T.transpose(out=ptb[:, h, :], in_=kc[:, s, h * 128:(h + 1) * 128],
                                                                identity=ident_b[:, :]), r=[r_kc, r_idb], w=[r_ptb[0]])
                            dve(lambda s=s: V.tensor_copy(out=kT[:, :, s * 128:(s + 1) * 128], in_=ptb[:, :, :]),
                                r=[r_ptb[0]], w=[r_kT])
                    nsub = (Tn + 127) // 128
                    if Tn >= 128:
                        for s in range(nsub):
                            ldc(va[:, nprev // 128 + s, :, 0:128],
                                Z[r0 + s * 128:r0 + (s + 1) * 128, 2048:3072].rearrange("p (h d) -> p h d", d=128), r=[rZ, r_va], w=[r_va])
                    else:
                        ldc(va[0:Tn, nprev // 128, :, 0:128],
                            Z[r0:r0 + Tn, 2048:3072].rearrange("p (h d) -> p h d", d=128), r=[rZ, r_va], w=[r_va])
                    ld(NEWV[l, r0:r0 + Tn, :], Z[r0:r0 + Tn, 2048:3072], r=[rZ], w=[rOUT])
                    for s in range(nsub):
                        sub = min(128, Tn - s * 128)
                        rows = slice(r0 + s * 128, r0 + s * 128 + sub)
                        ld(zqk[:sub, :], Z[rows, 0:2048], r=[rZ], w=[r_zqk])
                        ld(cs[:sub, :], CCS[rows, :], w=[r_cs])
                        zv = zqk[:sub, :].rearrange("p (h two d) -> p h two d", two=2, d=64)
                        rv = rot[:sub, :].rearrange("p (h two d) -> p h two d", two=2, d=64)
                        x1, x2 = zv[:, :, 0, :], zv[:, :, 1, :]
                        cb_ = cs[:sub, 0:64].unsqueeze(1).to_broadcast([sub, 16, 64])
                        sb_ = cs[:sub, 64:128].unsqueeze(1).to_broadcast([sub, 16, 64])
                        t1v = t1[:sub, :].rearrange("p (h d) -> p h d", d=64)
                        t2v = t2[:sub, :].rearrange("p (h d) -> p h d", d=64)
                        dve(lambda: V.tensor_tensor(out=t1v, in0=x1, in1=cb_, op=ALU.mult), r=[r_zqk, r_cs], w=[r_t1])
                        dve(lambda: V.tensor_tensor(out=t2v, in0=x2, in1=sb_, op=ALU.mult), r=[r_zqk, r_cs], w=[r_t2])
                        dve(lambda: V.tensor_tensor(out=rv[:, :, 0, :], in0=t1v, in1=t2v, op=ALU.subtract), r=[r_t1, r_t2], w=[r_rot])
                        dve(lambda: V.tensor_tensor(out=t1v, in0=x1, in1=sb_, op=ALU.mult), r=[r_zqk, r_cs], w=[r_t1])
                        dve(lambda: V.tensor_tensor(out=t2v, in0=x2, in1=cb_, op=ALU.mult), r=[r_zqk, r_cs], w=[r_t2])
                        dve(lambda: V.tensor_tensor(out=rv[:, :, 1, :], in0=t1v, in1=t2v, op=ALU.add), r=[r_t1, r_t2], w=[r_rot])
                        ld(NEWK[l, rows, :], rot[:sub, 1024:2048], r=[r_rot], w=[rOUT])
                        act(lambda: A.copy(out=rotb[:sub, :], in_=rot[:sub, :]), r=[r_rot], w=[r_rotb])
                        for half in range(2):
                            for h in range(8):
                                c = half * 8 + h
                                pe(lambda c=c, h=h: T.transpose(out=ptb[:, h, 0:sub], in_=rotb[:sub, c * 128:(c + 1) * 128],
                                                                identity=ident_b[0:sub, 0:sub]), r=[r_rotb, r_idb], w=[r_ptb[0]])
                            if half == 0:
                                dve(lambda: V.tensor_copy(out=qT[:, :, s * 128:s * 128 + sub], in_=ptb[:, :, 0:sub]),
                                    r=[r_ptb[0]], w=[r_qT])
                            else:
                                dve(lambda: V.tensor_copy(out=kT[:, :, nprev + s * 128:nprev + s * 128 + sub], in_=ptb[:, :, 0:sub]),
                                    r=[r_ptb[0]], w=[r_kT])
                    QB = min(512, Tn)
                    it = 0
                    for qb in range(Tn // QB):
                        q0 = nprev + qb * QB
                        nq = QB
                        kt_lo = max(0, (q0 - 2048) // 128)
                        kt_hi = (q0 + nq - 1) // 128
                        nqs = (nq + 127) // 128
                        for h in range(8):
                            for kt in range(kt_lo, kt_hi + 1):
                                bi = it % 2
                                it += 1
                                c0 = q0 - kt * 128 + C0
                                pe(lambda bi=bi, kt=kt: T.matmul(pb[4 + bi][:, 0:nq], lhsT=kT[:, h, kt * 128:(kt + 1) * 128],
                                                                 rhs=qT[:, h, qb * QB:qb * QB + nq], start=True, stop=True),
                                   r=[r_kT, r_qT], w=[r_pb[4 + bi]])
                                act(lambda bi=bi: A.activation(out=eb[bi][:, 0:nq], in_=pb[4 + bi][:, 0:nq], func=AF.Exp,
                                                               scale=float(128 ** -0.5)), r=[r_pb[4 + bi]], w=[r_eb[bi]])
                                pool(lambda bi=bi, c0=c0: G.tensor_tensor(out=pbf[bi][:, 0:nq], in0=eb[bi][:, 0:nq],
                                                                          in1=mt[:, c0:c0 + nq], op=ALU.mult),
                                     r=[r_eb[bi], r_mt], w=[r_pbf[bi]])
                                for qs in range(nqs):
                                    qsub = min(128, nq - qs * 128)
                                    pe(lambda bi=bi, qs=qs, qsub=qsub, kt=kt: T.matmul(
                                        pb[qs][:qsub, 0:129], lhsT=pbf[bi][:, qs * 128:qs * 128 + qsub], rhs=va[:, kt, h, 0:129],
                                        start=(kt == kt_lo), stop=(kt == kt_hi)), r=[r_pbf[bi], r_va], w=[r_pb[qs]],
                                       inc=(qs == nqs - 1))
                            for qs in range(nqs):
                                qsub = min(128, nq - qs * 128)
                                dve(lambda qs=qs, qsub=qsub: V.reciprocal(out=rd[:qsub, qs:qs + 1], in_=pb[qs][:qsub, 128:129]),
                                    r=[r_pb[qs]], w=[r_rd])
                                dve(lambda qs=qs, qsub=qsub, h=h: V.tensor_scalar(
                                    out=oa[:qsub, qs, h * 128:(h + 1) * 128], in0=pb[qs][:qsub, 0:128], scalar1=rd[:qsub, qs:qs + 1],
                                    scalar2=None, op0=ALU.mult), r=[r_pb[qs], r_rd], w=[r_oa[qs]])
                        for qs in range(nqs):
                            qsub = min(128, nq - qs * 128)
                            rows = slice(r0 + qb * QB + qs * 128, r0 + qb * QB + qs * 128 + qsub)
                            rms_rows((junk, r_junk, ss, r_ss), oa[:qsub, qs, :], qsub, gbc[:qsub, :], yo[:qsub, :],
                                     r_oa[qs], r_gbc, r_yo, 1024)
                            ld(MIX[rows, 0:1024], yo[:qsub, :], r=[r_yo], w=[rMIX])
                fw.barrier()

        def gmlp_pool(l):
            with ExitStack() as st:
                wsT = sb(st, "wsT", [128, 8, 128], BF16); r_wsT = Res()
                wsr = sb(st, "wsr", [128, 128]); r_wsr = Res()
                wsm = sb(st, "wsm", [128, 128], BF16); r_wsm = Res()
                tril = sb(st, "tril", [128, 128]); r_tril = Res()
                bsr = sb(st, "bsr", [8, 128]); r_bsr = Res()
                bcol = sb(st, "bcol", [128, 8]); r_bcol = Res()
                gc = sb(st, "gc", [128, 1024]); r_gc = Res()
                gd = sb(st, "gd", [128, 1024]); r_gd = Res()
                psc = sb(st, "psc", [128, 1024]); r_psc = Res()
                pm = sb(st, "pm", [128, 12, 128], BF16); r_pm = Res()
                pms = sb(st, "pms", [32, 4, 8], BF16); r_pms = Res()
                wp = sb(st, "wp", [128, 4, 2, 256], BF16); r_wp = Res()
                u = sb(st, "gu", [128, 1024]); r_u = Res()
                vb = sb(st, "gvb", [128, 1024], BF16); r_vb = Res()
                oc = sb(st, "goc", [128, 1024]); r_oc = Res()
                pcur = sb(st, "pcur", [128, 1024], BF16); r_pcur = Res()
                pprev = sb(st, "pprev", [128, 1024], BF16); r_pprev = Res()
                pext = sb(st, "pext", [32, 1024], BF16); r_pext = Res()
                plT = sb(st, "plT", [128, 8, 128], BF16); r_plT = Res()
                od = sb(st, "god", [128, 1024]); r_od = Res()
                junk = sb(st, "gjunk", [128, 1024]); r_junk = Res()
                ss = sb(st, "gss", [128, 8]); r_ss = Res()
                yo = sb(st, "gyo", [128, 1024]); r_yo = Res()
                ld(tril[:], CTRIL[:, :], w=[r_tril])
                ld(gc[:], Wd["g_out_c"][l].partition_broadcast(128), w=[r_gc])
                ld(gd[:], Wd["g_out_d"][l].partition_broadcast(128), w=[r_gd])
                ld(psc[:], Wd["pool_scale"][l].partition_broadcast(128), w=[r_psc])
                ldc(pm[:], CPM.rearrange("p (a b) -> p a b", b=128), w=[r_pm])
                ldc(pms[:], CPMS.rearrange("p (a b) -> p a b", b=8), w=[r_pms])
                ldc(wp[:], Wd["w_pool"][l].rearrange("g (c p) d -> p g c d", p=128), w=[r_wp])
                ld(bsr[:], Wd["b_s"][l], w=[r_bsr])
                pe(lambda: T.transpose(out=ptf[:, 0, 0:8], in_=bsr[:, :], identity=ident_f[0:8, 0:8]), r=[r_bsr, r_idf], w=[r_ptf])
                dve(lambda: V.tensor_copy(out=bcol[:], in_=ptf[:, 0, 0:8]), r=[r_ptf], w=[r_bcol])
                for g in range(8):
                    ld(wsr[:], Wd["w_s"][l, g], w=[r_wsr])
                    dve(lambda: V.tensor_tensor(out=wsm[:], in0=wsr[:], in1=tril[:], op=ALU.mult), r=[r_wsr, r_tril], w=[r_wsm])
                    pe(lambda g=g: T.transpose(out=ptb[:, g, :], in_=wsm[:, :], identity=ident_b[:, :]), r=[r_wsm, r_idb], w=[r_ptb[0]])
                dve(lambda: V.tensor_copy(out=wsT[:], in_=ptb[:]), r=[r_ptb[0]], w=[r_wsT])
                for sq in SEQS:
                    r0, Tn, nprev, si = sq["r0"], sq["T"], sq["nprev"], sq["si"]
                    nsub = (Tn + 127) // 128
                    if si == 0:
                        ld(PLO[l, 0], Z[r0 + Tn - 15:r0 + Tn, 8480:9504], r=[rZ], w=[rOUT])
                    else:
                        ld(PLO[l, 1, 0:7, :], PL0[l, 8:15, :], w=[rOUT])
                        ld(PLO[l, 1, 7:15, :], Z[r0:r0 + Tn, 8480:9504], r=[rZ], w=[rOUT])
                        ld(GVO[l], Z[r0:r0 + Tn, 7456:8480], r=[rZ], w=[rOUT])
                    ld(SHO[l, si:si + 1, :], Z[r0 + Tn - 1:r0 + Tn, 3072:3072 + BFEAT], r=[rZ], w=[rOUT])
                    for s in range(nsub):
                        sub = min(128, Tn - s * 128)
                        rows = slice(r0 + s * 128, r0 + s * 128 + sub)
                        ld(u[:sub, :], Z[rows, 6432:7456], r=[rZ], w=[r_u])
                        ldc(vb[:sub, :], Z[rows, 7456:8480], r=[rZ], w=[r_vb])
                        for g in range(8):
                            bank = g // 4
                            pe(lambda g=g, bank=bank: T.matmul(pb[bank][:sub, (g % 4) * 128:(g % 4 + 1) * 128], lhsT=wsT[:sub, g, 0:sub],
                                                               rhs=vb[:sub, g * 128:(g + 1) * 128], start=True, stop=True),
                               r=[r_wsT, r_vb], w=[r_pb[bank]])
                        for g in range(8):
                            bank = g // 4
                            dve(lambda g=g, bank=bank: V.scalar_tensor_tensor(
                                out=oc[:sub, g * 128:(g + 1) * 128], in0=pb[bank][:sub, (g % 4) * 128:(g % 4 + 1) * 128],
                                scalar=bcol[:sub, g:g + 1], in1=u[:sub, g * 128:(g + 1) * 128], op0=ALU.add, op1=ALU.mult),
                                r=[r_pb[bank], r_bcol, r_u], w=[r_oc])
                        rms_rows((junk, r_junk, ss, r_ss), oc[:sub, :], sub, gc[:sub, :], yo[:sub, :], r_oc, r_gc, r_yo, 1024)
                        ld(MIX[rows, 2048:3072], yo[:sub, :], r=[r_yo], w=[rMIX])
                        if si == 0:
                            cur, r_cur = (pcur, r_pcur) if s % 2 == 0 else (pprev, r_pprev)
                            prv, r_prv = (pprev, r_pprev) if s % 2 == 0 else (pcur, r_pcur)
                            ldc(cur[:sub, :], Z[rows, 8480:9504], r=[rZ], w=[r_cur])
                            for g in range(4):
                                for cc in range(2):
                                    ch = g * 2 + cc
                                    cols = slice(ch * 128, (ch + 1) * 128)
                                    if s == 0:
                                        pe(lambda g=g, ch=ch, cols=cols: T.matmul(pb[2 + ch // 4][:, (ch % 4) * 128:(ch % 4 + 1) * 128],
                                                                               lhsT=cur[:sub, cols], rhs=pm[:sub, g * 3 + 2, 0:sub],
                                                                               start=True, stop=True), r=[r_cur, r_pm], w=[r_pb[2 + ch // 4]])
                                    else:
                                        pe(lambda g=g, ch=ch, cols=cols: T.matmul(pb[2 + ch // 4][:, (ch % 4) * 128:(ch % 4 + 1) * 128],
                                                                               lhsT=cur[:sub, cols], rhs=pm[:sub, g * 3 + 0, 0:sub],
                                                                               start=True, stop=False), r=[r_cur, r_pm], w=[r_pb[2 + ch // 4]])
                                        pe(lambda g=g, ch=ch, cols=cols: T.matmul(pb[2 + ch // 4][:, (ch % 4) * 128:(ch % 4 + 1) * 128],
                                                                               lhsT=prv[:, cols], rhs=pm[:, g * 3 + 1, 0:sub],
                                                                               start=False, stop=True), r=[r_prv, r_pm], w=[r_pb[2 + ch // 4]])
                        else:
                            ldc(pext[0:15, :], PL0[l], w=[r_pext])
                            ldc(pext[15:15 + sub, :], Z[rows, 8480:9504], r=[rZ, r_pext], w=[r_pext])
                            for g in range(4):
                                for cc in range(2):
                                    ch = g * 2 + cc
                                    cols = slice(ch * 128, (ch + 1) * 128)
                                    pe(lambda g=g, ch=ch, cols=cols: T.matmul(pb[2 + ch // 4][:, (ch % 4) * 128:(ch % 4) * 128 + sub],
                                                                           lhsT=pext[0:23, cols], rhs=pms[0:23, g, 0:sub],
                                                                           start=True, stop=True), r=[r_pext, r_pms], w=[r_pb[2 + ch // 4]])
                        for hb in range(2):
                            dve(lambda hb=hb: V.tensor_copy(out=plT[:, hb * 4:(hb + 1) * 4, 0:sub],
                                                            in_=pb[2 + hb][:, :].rearrange("p (a b) -> p a b", b=128)[:, :, 0:sub]),
                                r=[r_pb[2 + hb]], w=[r_plT])
                        for g in range(4):
                            bank = g // 2
                            for cc in range(2):
                                pe(lambda g=g, cc=cc, bank=bank: T.matmul(pb[bank][:sub, (g % 2) * 256:(g % 2 + 1) * 256],
                                                                          lhsT=plT[:, g * 2 + cc, 0:sub], rhs=wp[:, g, cc, :],
                                                                          start=(cc == 0), stop=(cc == 1)), r=[r_plT, r_wp], w=[r_pb[bank]])
                        for bank in range(2):
                            dve(lambda bank=bank: V.tensor_tensor(out=od[:sub, bank * 512:(bank + 1) * 512], in0=pb[bank][:sub, :],
                                                                  in1=psc[:sub, bank * 512:(bank + 1) * 512], op=ALU.mult),
                                r=[r_pb[bank], r_psc], w=[r_od])
                        rms_rows((junk, r_junk, ss, r_ss), od[:sub, :], sub, gd[:sub, :], yo[:sub, :], r_od, r_gd, r_yo, 1024)
                        ld(MIX[rows, 3072:4096], yo[:sub, :], r=[r_yo], w=[rMIX])
            fw.barrier()

        def rwkv(l):
            CH = 64
            with ExitStack() as st:
                def bc(name, src):
                    t = sb(st, name, [128, 1024]); r = Res()
                    ld(t[:], src.partition_broadcast(128), w=[r])
                    return t, r
                mu = sb(st, "mu", [128, BFEAT]); r_mu = Res()
                ld(mu[:], Wd["mu_b"][l].partition_broadcast(128), w=[r_mu])
                w0b, r_w0b = bc("w0b", Wd["w0"][l])
                a0b, r_a0b = bc("a0b", Wd["a0"][l])
                kkb, r_kkb = bc("kkb", Wd["k_k"][l])
                kab, r_kab = bc("kab", Wd["k_a"][l])
                rkb, r_rkb = bc("rkb", Wd["r_k"][l])
                lwb, r_lwb = bc("lwb", Wd["ln_x_w"][l])
                lbb, r_lbb = bc("lbb", Wd["ln_x_b"][l])
                wup = sb(st, "wup", [64, 1024], BF16); r_wup = Res()
                aup = sb(st, "aup", [64, 1024], BF16); r_aup = Res()
                gup = sb(st, "gup", [128, 2, 1024], BF16); r_gup = Res()
                ldc(wup[:], Wd["w_up"][l], w=[r_wup])
                ldc(aup[:], Wd["a_up"][l], w=[r_aup])
                ldc(gup[:, 0, :], Wd["g_up"][l, 0:128, :], w=[r_gup])
                ldc(gup[0:32, 1, :], Wd["g_up"][l, 128:160, :], r=[r_gup], w=[r_gup])
                tri2 = sb(st, "tri2", [128, 128]); r_tri2 = Res()
                ones2 = sb(st, "ones2", [128, 128]); r_ones2 = Res()
                msk = sb(st, "msk", [128, 4, 64]); r_msk = Res()
                ld(tri2[:], CTRI2[:, :], w=[r_tri2])
                ld(ones2[:], CONES2[:, :], w=[r_ones2])
                ld(msk[:], CMSK.rearrange("p (a b) -> p a b", b=64), w=[r_msk])

                def mb(i):
                    return msk[:, i:i + 1, :].to_broadcast([128, 8, 64])
                valid = sb(st, "valid", [128, 1]); r_valid = Res()
                fbt = sb(st, "fbt", [128, BFEAT]); r_fb = Res()
                pvt = sb(st, "pvt", [128, BFEAT]); r_pv = Res()
                lz = sb(st, "lz", [128, 288], BF16); r_lz = Res()
                lzT = sb(st, "lzT", [128, 4, 128], BF16); r_lzT = Res()
                NA = 14
                arr = [sb(st, "ar%d" % i, [128, 1024]) for i in range(NA)]
                r_arr = [Res() for _ in range(NA)]
                sm = sb(st, "sm", [128, 64]); r_sm = Res()
                NF = 5
                fm = [sb(st, "fm%d" % i, [128, 8, 64]) for i in range(NF)]; r_fm = [Res() for _ in range(NF)]
                NM = 14
                mm = [sb(st, "mm%d" % i, [128, 8, 64]) for i in range(NM)]; r_mm = [Res() for _ in range(NM)]
                ST = sb(st, "ST", [128, 8, 64]); r_ST = Res()
                stmp = sb(st, "stmp", [64, 8, 128]); r_stmp = Res()

                def pbv(i):
                    return pb[i][:, :].rearrange("p (a b) -> p a b", b=64)

                def headmm(bank, terms, r, first=True, last=True):
                    n = len(terms)
                    for hp in range(8):
                        for hh in range(2):
                            R = slice(hh * 64, (hh + 1) * 64)
                            for ti, (lf, rf) in enumerate(terms):
                                is_last_inst = (hp == 7 and hh == 1 and ti == n - 1)
                                pe(lambda: T.matmul(pb[bank][R, hp * 64:(hp + 1) * 64], lhsT=lf(R, hp, hh), rhs=rf(R, hp, hh),
                                                    start=(first and ti == 0), stop=(last and ti == n - 1)),
                                   r=r, w=[r_pb[bank]], inc=is_last_inst)

                def f3(t):
                    return lambda R, hp, hh: t[R, hp, :]

                def tmcols(ap2d):
                    return lambda R, hp, hh: ap2d[R, hp * 128 + hh * 64:hp * 128 + hh * 64 + 64]

                for sq in SEQS:
                    r0, Tn, nprev, si = sq["r0"], sq["T"], sq["nprev"], sq["si"]
                    nch = (Tn + CH - 1) // CH
                    if si == 0:
                        dve(lambda: V.memset(ST[:], 0.0), w=[r_ST])
                    else:
                        for hh in range(2):
                            ld(stmp[:, :, hh * 64:(hh + 1) * 64],
                               WKV0[l].rearrange("(hp hh) v k -> hh v hp k", hh=2)[hh], r=[r_stmp], w=[r_stmp])
                        for hb in range(2):
                            for j in range(4):
                                pe(lambda: T.transpose(out=ptf[:, j, 0:64], in_=stmp[:, hb * 4 + j, :], identity=ident_f[0:64, 0:64]),
                                   r=[r_stmp, r_idf], w=[r_ptf], inc=(j == 3))
                            dve(lambda: V.tensor_copy(out=ST[:, hb * 4:(hb + 1) * 4, :], in_=ptf[:, :, 0:64]), r=[r_ptf], w=[r_ST])
                    for c in range(nch):
                        nv = min(CH, Tn - c * CH)
                        ta = r0 + c * CH
                        fcol = slice(3072, 3072 + BFEAT)
                        if nv < CH:
                            dve(lambda: V.memset(fbt[:], 0.0), w=[r_fb])
                            dve(lambda: V.memset(pvt[:], 0.0), w=[r_pv])
                        if nv < CH or c == 0:
                            dve(lambda: V.memset(valid[:], 0.0), w=[r_valid])
                            for d in range(2):
                                dve(lambda: V.memset(valid[d * 64:d * 64 + nv, :], 1.0), r=[r_valid], w=[r_valid])
                        for d in range(2):
                            P0 = d * 64
                            ld(fbt[P0:P0 + nv, :], Z[ta:ta + nv, fcol], r=[rZ, r_fb], w=[r_fb])
                            if c == 0:
                                if si == 0:
                                    dve(lambda: V.memset(pvt[P0:P0 + 1, :], 0.0), r=[r_pv], w=[r_pv])
                                else:
                                    ld(pvt[P0:P0 + 1, :], SH0[l], r=[r_pv], w=[r_pv])
                                if nv > 1:
                                    ld(pvt[P0 + 1:P0 + nv, :], Z[ta:ta + nv - 1, fcol], r=[rZ, r_pv], w=[r_pv])
                            else:
                                ld(pvt[P0:P0 + nv, :], Z[ta - 1:ta - 1 + nv, fcol], r=[rZ, r_pv], w=[r_pv])
                        dve(lambda: V.tensor_tensor(out=pvt[:, :], in0=pvt[:, :], in1=fbt[:, :], op=ALU.subtract), r=[r_pv, r_fb], w=[r_pv])
                        pool(lambda: G.tensor_tensor(out=pvt[:, :], in0=pvt[:, :], in1=mu[:, :], op=ALU.mult), r=[r_pv, r_mu], w=[r_pv])
                        dve(lambda: V.tensor_tensor(out=pvt[:, :], in0=pvt[:, :], in1=fbt[:, :], op=ALU.add), r=[r_pv, r_fb], w=[r_pv])
                        fs = pvt
                        R_, K_, V_ = fs[:, 0:1024], fs[:, 1024:2048], fs[:, 2048:3072]
                        act(lambda: A.activation(out=lz[:, 0:64], in_=fs[:, 3072:3136], func=AF.Tanh), r=[r_pv], w=[r_lz])
                        act(lambda: A.copy(out=lz[:, 64:128], in_=fs[:, 3136:3200]), r=[r_pv], w=[r_lz])
                        act(lambda: A.activation(out=lz[:, 128:288], in_=fs[:, 3200:3360], func=AF.Sigmoid), r=[r_pv], w=[r_lz])
                        for j, (c0_, cn) in enumerate([(0, 64), (64, 64), (128, 128), (256, 32)]):
                            pe(lambda: T.transpose(out=ptb[0:cn, j, :], in_=lz[:, c0_:c0_ + cn], identity=ident_b[:, :]),
                               r=[r_lz, r_idb], w=[r_ptb[0]], inc=(j == 3))
                        dve(lambda: V.tensor_copy(out=lzT[:, :, :], in_=ptb[:, 0:4, :]), r=[r_ptb[0]], w=[r_lzT])
                        for hb in range(2):
                            cs_ = slice(hb * 512, (hb + 1) * 512)
                            pe(lambda: T.matmul(pb[hb][:, :], lhsT=lzT[0:64, 0, :], rhs=wup[:, cs_], start=True, stop=True),
                               r=[r_lzT, r_wup], w=[r_pb[hb]])
                            pe(lambda: T.matmul(pb[2 + hb][:, :], lhsT=lzT[0:64, 1, :], rhs=aup[:, cs_], start=True, stop=True),
                               r=[r_lzT, r_aup], w=[r_pb[2 + hb]])
                            pe(lambda: T.matmul(pb[4 + hb][:, :], lhsT=lzT[:, 2, :], rhs=gup[:, 0, cs_], start=True, stop=False),
                               r=[r_lzT, r_gup], w=[r_pb[4 + hb]], inc=False)
                            pe(lambda: T.matmul(pb[4 + hb][:, :], lhsT=lzT[0:32, 3, :], rhs=gup[0:32, 1, cs_], start=False, stop=True),
                               r=[r_lzT, r_gup], w=[r_pb[4 + hb]])
                        (LW, At, Gt, KKt, Bt, KMt, BON, T1, T2, Lsb, Xa, Xb, Xc, Xd) = [a[:, :] for a in arr]
                        (rLW, rA, rG, rKK, rB, rKM, rBON, rT1, rT2, rL, rXa, rXb, rXc, rXd) = r_arr
                        for hb in range(2):
                            cs_ = slice(hb * 512, (hb + 1) * 512)
                            dve(lambda: V.tensor_tensor(out=arr[0][:, cs_], in0=pb[hb][:, :], in1=w0b[:, cs_], op=ALU.add),
                                r=[r_pb[hb], r_w0b], w=[rLW])
                            dve(lambda: V.tensor_tensor(out=arr[1][:, cs_], in0=pb[2 + hb][:, :], in1=a0b[:, cs_], op=ALU.add),
                                r=[r_pb[2 + hb], r_a0b], w=[rA])
                            act(lambda: A.copy(out=arr[2][:, cs_], in_=pb[4 + hb][:, :]), r=[r_pb[4 + hb]], w=[rG])
                        act(lambda: A.activation(out=LW, in_=LW, func=AF.Sigmoid), r=[rLW], w=[rLW])
                        act(lambda: A.activation(out=At, in_=At, func=AF.Sigmoid), r=[rA], w=[rA])
                        dve(lambda: V.tensor_scalar(out=LW, in0=LW, scalar1=valid[:, 0:1], scalar2=-float(np.exp(-0.5)), op0=ALU.mult, op1=ALU.mult),
                            r=[rLW, r_valid], w=[rLW])
                        pool(lambda: G.tensor_tensor(out=KKt, in0=K_, in1=kkb[:, :], op=ALU.mult), r=[r_pv, r_kkb], w=[rKK])
                        dve(lambda: V.tensor_tensor(out=T1, in0=KKt, in1=KKt, op=ALU.mult), r=[rKK], w=[rT1])
                        dve(lambda: V.tensor_reduce(out=sm[:, 0:16], in_=T1.rearrange("p (h d) -> p h d", d=64), axis=AX.X, op=ALU.add), r=[rT1], w=[r_sm])
                        act(lambda: A.activation(out=sm[:, 16:32], in_=sm[:, 0:16], func=AF.Sqrt), r=[r_sm], w=[r_sm])
                        dve(lambda: V.tensor_scalar(out=sm[:, 16:32], in0=sm[:, 16:32], scalar1=1e-12, scalar2=None, op0=ALU.max), r=[r_sm], w=[r_sm])
                        dve(lambda: V.reciprocal(out=sm[:, 32:48], in_=sm[:, 16:32]), r=[r_sm], w=[r_sm])
                        dve(lambda: V.tensor_tensor(out=KKt.rearrange("p (h d) -> p h d", d=64), in0=KKt.rearrange("p (h d) -> p h d", d=64),
                                                    in1=sm[:, 32:48].unsqueeze(2).to_broadcast([128, 16, 64]), op=ALU.mult), r=[rKK, r_sm], w=[rKK])
                        pool(lambda: G.tensor_tensor(out=Bt, in0=KKt, in1=At, op=ALU.mult), r=[rKK, rA], w=[rB])
                        dve(lambda: V.scalar_tensor_tensor(out=T1, in0=At, scalar=-1.0, in1=kab[:, :], op0=ALU.add, op1=ALU.mult), r=[rA, r_kab], w=[rT1])
                        dve(lambda: V.scalar_tensor_tensor(out=KMt, in0=T1, scalar=1.0, in1=K_, op0=ALU.add, op1=ALU.mult), r=[rT1, r_pv], w=[rKM])
                        pool(lambda: G.tensor_tensor(out=T1, in0=R_, in1=KMt, op=ALU.mult), r=[r_pv, rKM], w=[rT1])
                        pool(lambda: G.tensor_tensor(out=T1, in0=T1, in1=rkb[:, :], op=ALU.mult), r=[rT1, r_rkb], w=[rT1])
                        dve(lambda: V.tensor_reduce(out=sm[:, 48:64], in_=T1.rearrange("p (h d) -> p h d", d=64), axis=AX.X, op=ALU.add), r=[rT1], w=[r_sm])
                        pool(lambda: G.tensor_tensor(out=BON.rearrange("p (h d) -> p h d", d=64), in0=V_.rearrange("p (h d) -> p h d", d=64),
                                                     in1=sm[:, 48:64].unsqueeze(2).to_broadcast([128, 16, 64]), op=ALU.mult), r=[r_pv, r_sm], w=[rBON])
                        for hb in range(2):
                            cs_ = slice(hb * 512, (hb + 1) * 512)
                            pe(lambda: T.matmul(pb[hb][:, :], lhsT=tri2[:, :], rhs=arr[0][:, cs_], start=True, stop=True), r=[r_tri2, rLW], w=[r_pb[hb]])
                            pe(lambda: T.matmul(pb[2 + hb][:, :], lhsT=ones2[:, :], rhs=arr[0][:, cs_], start=True, stop=True), r=[r_ones2, rLW], w=[r_pb[2 + hb]])
                        for hb in range(2):
                            cs_ = slice(hb * 512, (hb + 1) * 512)
                            act(lambda: A.copy(out=arr[9][:, cs_], in_=pb[hb][:, :]), r=[r_pb[hb]], w=[rL])
                            dve(lambda: V.tensor_tensor(out=arr[8][:, cs_], in0=pb[2 + hb][:, :], in1=arr[9][:, cs_], op=ALU.subtract), r=[r_pb[2 + hb], rL], w=[rT2])
                        act(lambda: A.activation(out=T2, in_=T2, func=AF.Exp), r=[rT2], w=[rT2])
                        dve(lambda: V.tensor_tensor(out=T1, in0=Lsb, in1=LW, op=ALU.subtract), r=[rL, rLW], w=[rT1])
                        act(lambda: A.activation(out=T1, in_=T1, func=AF.Exp), r=[rT1], w=[rT1])
                        pool(lambda: G.tensor_tensor(out=Xa, in0=KKt, in1=T1, op=ALU.mult), r=[rKK, rT1], w=[rXa])
                        dve(lambda: V.tensor_tensor(out=KKt, in0=Bt, in1=T2, op=ALU.mult), r=[rB, rT2], w=[rKK])
                        pool(lambda: G.tensor_tensor(out=LW, in0=KMt, in1=T2, op=ALU.mult), r=[rKM, rT2], w=[rLW])
                        BH, KH, rBH, rKH = KKt, LW, rKK, rLW
                        act(lambda: A.activation(out=Xd, in_=Lsb, func=AF.Exp), r=[rL], w=[rXd])
                        act(lambda: A.activation(out=T1, in_=Lsb, func=AF.Exp, scale=-1.0), r=[rL], w=[rT1])
                        dve(lambda: V.tensor_tensor(out=Xb, in0=Bt, in1=T1, op=ALU.mult), r=[rB, rT1], w=[rXb])
                        pool(lambda: G.tensor_tensor(out=Xc, in0=KMt, in1=T1, op=ALU.mult), r=[rKM, rT1], w=[rXc])
                        dve(lambda: V.tensor_tensor(out=T2, in0=R_, in1=Xd, op=ALU.mult), r=[r_pv, rXd], w=[rT2])
                        for fi, (src_, rs_) in enumerate([(Xa, rXa), (Xb, rXb), (Xc, rXc), (T2, rT2), (Xd, rXd)]):
                            for hb in range(2):
                                for j in range(4):
                                    hp = hb * 4 + j
                                    pe(lambda: T.transpose(out=ptf[:, j, :], in_=src_[:, hp * 128:(hp + 1) * 128], identity=ident_f[:, :]),
                                       r=[rs_, r_idf], w=[r_ptf], inc=(j == 3))
                                if (fi + hb) % 2 == 0:
                                    dve(lambda: V.tensor_copy(out=fm[fi][:, hb * 4:(hb + 1) * 4, :], in_=ptf[:, :, 0:64]), r=[r_ptf], w=[r_fm[fi]])
                                else:
                                    act(lambda: A.copy(out=fm[fi][:, hb * 4:(hb + 1) * 4, :], in_=ptf[:, :, 0:64]), r=[r_ptf], w=[r_fm[fi]])
                        Af, Bf, Kf, Rf, Ef = fm
                        rAf, rBf, rKf, rRf, rEf = r_fm
                        headmm(0, [(f3(Bf), f3(Af))], [rBf, rAf])
                        headmm(1, [(f3(Af), f3(Bf))], [rBf, rAf])
                        headmm(2, [(f3(Kf), f3(Af))], [rKf, rAf])
                        headmm(3, [(f3(Bf), f3(Rf))], [rBf, rRf])
                        headmm(4, [(f3(Kf), f3(Rf))], [rKf, rRf])
                        E = [mm[0], mm[1]]; ET = [mm[2], mm[3]]; rE = [r_mm[0], r_mm[1]]; rET = [r_mm[2], r_mm[3]]
                        Tm, TTm, Mm, Np, Mp, XT, nU, osb = mm[4], mm[5], mm[6], mm[7], mm[8], mm[9], mm[10], mm[11]
                        rTm, rTTm, rMm, rNp, rMp, rXT, rnU, rosb = (r_mm[i] for i in range(4, 12))
                        dve(lambda: V.scalar_tensor_tensor(out=E[0][:], in0=pbv(0), scalar=-1.0, in1=mb(0), op0=ALU.mult, op1=ALU.mult),
                            r=[r_pb[0], r_msk], w=[rE[0]])
                        dve(lambda: V.scalar_tensor_tensor(out=ET[0][:], in0=pbv(1), scalar=-1.0, in1=mb(2), op0=ALU.mult, op1=ALU.mult),
                            r=[r_pb[1], r_msk], w=[rET[0]])
                        pool(lambda: G.tensor_tensor(out=Tm[:], in0=E[0][:], in1=mb(3), op=ALU.add), r=[rE[0], r_msk], w=[rTm])
                        pool(lambda: G.tensor_tensor(out=TTm[:], in0=ET[0][:], in1=mb(3), op=ALU.add), r=[rET[0], r_msk], w=[rTTm])
                        dve(lambda: V.tensor_tensor(out=Mm[:], in0=pbv(2), in1=mb(0), op=ALU.mult), r=[r_pb[2], r_msk], w=[rMm])
                        dve(lambda: V.tensor_tensor(out=Np[:], in0=pbv(3), in1=mb(1), op=ALU.mult), r=[r_pb[3], r_msk], w=[rNp])
                        dve(lambda: V.tensor_tensor(out=Mp[:], in0=pbv(4), in1=mb(1), op=ALU.mult), r=[r_pb[4], r_msk], w=[rMp])
                        cur = 0
                        for lev in range(5):
                            nxt = 1 - cur
                            lastlev = (lev == 4)
                            headmm(0, [(f3(ET[cur]), f3(E[cur]))], [rET[cur], rE[cur]])
                            if not lastlev:
                                headmm(1, [(f3(E[cur]), f3(ET[cur]))], [rET[cur], rE[cur]])
                            act(lambda: A.copy(out=E[nxt][:], in_=pbv(0)), r=[r_pb[0]], w=[rE[nxt]])
                            if not lastlev:
                                dve(lambda: V.tensor_copy(out=ET[nxt][:], in_=pbv(1)), r=[r_pb[1]], w=[rET[nxt]])
                            headmm(2, [(f3(TTm), f3(E[nxt]))], [rTTm, rE[nxt]])
                            if not lastlev:
                                headmm(3, [(f3(E[nxt]), f3(TTm))], [rTTm, rE[nxt]])
                            dve(lambda: V.tensor_tensor(out=Tm[:], in0=pbv(2), in1=Tm[:], op=ALU.add), r=[r_pb[2], rTm], w=[rTm])
                            if not lastlev:
                                dve(lambda: V.tensor_tensor(out=TTm[:], in0=pbv(3), in1=TTm[:], op=ALU.add), r=[r_pb[3], rTTm], w=[rTTm])
                            cur = nxt
                        vcols = tmcols(V_)
                        headmm(0, [(f3(Af), f3(ST)), (f3(Mm), vcols)], [rAf, r_ST, rMm, r_pv])
                        act(lambda: A.copy(out=XT[:], in_=pbv(0)), r=[r_pb[0]], w=[rXT])
                        headmm(1, [(f3(Tm), f3(XT))], [rTm, rXT])
                        dve(lambda: V.tensor_scalar(out=nU[:], in0=pbv(1), scalar1=-1.0, scalar2=None, op0=ALU.mult), r=[r_pb[1]], w=[rnU])
                        headmm(2, [(f3(Rf), f3(ST)), (f3(Np), f3(nU)), (f3(Mp), vcols)], [rRf, r_ST, rNp, rnU, rMp, r_pv])
                        headmm(3, [(tmcols(BH), f3(nU)), (tmcols(KH), vcols)], [rBH, rnU, rKH, r_pv])
                        act(lambda: A.copy(out=osb[:], in_=pbv(2)), r=[r_pb[2]], w=[rosb])
                        dve(lambda: V.tensor_tensor(out=ST[:], in0=ST[:], in1=Ef[:, :, 63:64].to_broadcast([128, 8, 64]), op=ALU.mult),
                            r=[r_ST, rEf], w=[r_ST])
                        dve(lambda: V.tensor_tensor(out=ST[:], in0=pbv(3), in1=ST[:], op=ALU.add), r=[r_pb[3], r_ST], w=[r_ST])
                        sel = [mm[12], mm[13], XT]
                        rsel = [r_mm[12], r_mm[13], rXT]
                        w2 = [Tm, TTm]
                        rw2 = [rTm, rTTm]
                        for d in range(2):
                            P_ = slice(d * 64, (d + 1) * 64)
                            def hv(ap2d):
                                return ap2d[P_, :].rearrange("p (hp hh v) -> p hp hh v", hh=2, v=64)[:, :, d, :]
                            pool(lambda: G.tensor_copy(out=sel[0][P_, :, :], in_=hv(BON)), r=[rBON], w=[rsel[0]])
                            pool(lambda: G.tensor_copy(out=sel[1][P_, :, :], in_=hv(Gt)), r=[rG], w=[rsel[1]])
                            pool(lambda: G.tensor_copy(out=w2[0][P_, :, :], in_=hv(lwb[:, :])), r=[r_lwb], w=[rw2[0]])
                            pool(lambda: G.tensor_copy(out=w2[1][P_, :, :], in_=hv(lbb[:, :])), r=[r_lbb], w=[rw2[1]])
                        o3 = osb[:]
                        sq3 = nU[:]
                        dve(lambda: V.tensor_reduce(out=sm[:, 0:8], in_=o3, axis=AX.X, op=ALU.add), r=[rosb], w=[r_sm])
                        dve(lambda: V.tensor_scalar(out=sm[:, 0:8], in0=sm[:, 0:8], scalar1=1.0 / 64, scalar2=None, op0=ALU.mult), r=[r_sm], w=[r_sm])
                        dve(lambda: V.tensor_tensor(out=o3, in0=o3, in1=sm[:, 0:8].unsqueeze(2).to_broadcast([128, 8, 64]), op=ALU.subtract),
                            r=[rosb, r_sm], w=[rosb])
                        dve(lambda: V.tensor_tensor(out=sq3, in0=o3, in1=o3, op=ALU.mult), r=[rosb], w=[rnU])
                        dve(lambda: V.tensor_reduce(out=sm[:, 8:16], in_=sq3, axis=AX.X, op=ALU.add), r=[rnU], w=[r_sm])
                        dve(lambda: V.tensor_scalar(out=sm[:, 8:16], in0=sm[:, 8:16], scalar1=1.0 / 64, scalar2=64e-5, op0=ALU.mult, op1=ALU.add),
                            r=[r_sm], w=[r_sm])
                        act(lambda: A.activation(out=sm[:, 8:16], in_=sm[:, 8:16], func=AF.Sqrt), r=[r_sm], w=[r_sm])
                        dve(lambda: V.reciprocal(out=sm[:, 16:24], in_=sm[:, 8:16]), r=[r_sm], w=[r_sm])
                        dve(lambda: V.tensor_tensor(out=o3, in0=o3, in1=sm[:, 16:24].unsqueeze(2).to_broadcast([128, 8, 64]), op=ALU.mult),
                            r=[rosb, r_sm], w=[rosb])
                        dve(lambda: V.tensor_tensor(out=o3, in0=o3, in1=w2[0][:], op=ALU.mult), r=[rosb, rw2[0]], w=[rosb])
                        dve(lambda: V.tensor_tensor(out=o3, in0=o3, in1=w2[1][:], op=ALU.add), r=[rosb, rw2[1]], w=[rosb])
                        dve(lambda: V.tensor_tensor(out=o3, in0=o3, in1=sel[0][:], op=ALU.add), r=[rosb, rsel[0]], w=[rosb])
                        dve(lambda: V.tensor_tensor(out=o3, in0=o3, in1=sel[1][:], op=ALU.mult), r=[rosb, rsel[1]], w=[rosb])
                        for d in range(2):
                            dst = MIX[ta:ta + nv, 1024:2048].rearrange("t (hp hh v) -> t hp hh v", hh=2, v=64)[:, :, d, :]
                            ld(dst, osb[d * 64:d * 64 + nv, :, :], r=[rosb], w=[rMIX])
                    for hb in range(2):
                        for j in range(4):
                            pe(lambda: T.transpose(out=ptf[0:64, j, :], in_=ST[:, hb * 4 + j, :], identity=ident_f[:, :]),
                               r=[r_ST, r_idf], w=[r_ptf], inc=(j == 3))
                        dve(lambda: V.tensor_copy(out=stmp[:, hb * 4:(hb + 1) * 4, :], in_=ptf[0:64, :, :]), r=[r_ptf, r_stmp], w=[r_stmp])
                    for hh in range(2):
                        ld(WKVO[l, si].rearrange("(hp hh) v k -> hh v hp k", hh=2)[hh],
                           stmp[:, :, hh * 64:(hh + 1) * 64], r=[r_stmp], w=[rOUT])
            fw.barrier()

        src, rsrc = X, rX
        outs = [(RA, rRA), (RB, rRB)]
        for l in range(2):
            dst, rdst = outs[l]
            with ExitStack() as st:
                token_local(st, [job_ffn(l, 1, src, rsrc, R1, rR1), job_win(l, R1, rR1)])
            fw.barrier()
            attention(l)
            rwkv(l)
            gmlp_pool(l)
            with ExitStack() as st:
                token_local(st, [job_wout(l, R1, rR1, R2, rR2), job_ffn(l, 2, R2, rR2, dst, rdst)])
            fw.barrier()
            src, rsrc = dst, rdst
        final_norm(src, rsrc)
        fw.finish()
    return nc


_NC_CACHE = {}


def kernel(**inputs):
    inp = {k: np.ascontiguousarray(np.asarray(v, dtype=np.float32)) for k, v in inputs.items()}
    consts = _host_consts()
    if "nc" not in _NC_CACHE:
        _NC_CACHE["nc"] = build_nc()
    nc = _NC_CACHE["nc"]
    shared = {}
    for n, shp in WNAMES:
        shared[n] = inp[n].reshape(shp)
    shared.update(consts)
    in_maps = []
    for c in range(8):
        m = dict(shared)
        m["x_all"] = np.concatenate([inp["x_prompt"][c % 4], inp["x_sample"][c]], axis=0)
        m["cache_k"] = inp["cache_k_swa"][:, c].reshape(2, NPREV, 1024)
        m["cache_v"] = inp["cache_v_swa"][:, c].reshape(2, NPREV, 1024)
        m["wkv0"] = inp["state_rwkv_wkv"][:, c]
        m["shift0"] = inp["state_rwkv_shift"][:, c].reshape(2, 1, BFEAT)
        m["pool0"] = inp["state_pool"][:, c]
        in_maps.append({k: np.ascontiguousarray(v) for k, v in m.items()})
    res = run_bass_kernel_spmd(nc, in_maps, core_ids=list(range(8)))
    R = res.results
    y_p = np.stack([R[b]["y"][:TP] for b in range(4)])
    y_s = np.stack([R[c]["y"][TP:] for c in range(8)])
    nk_p = np.stack([R[b]["newk"][:, :TP] for b in range(4)], axis=1).reshape(2, 4, TP, 8, 128)
    nv_p = np.stack([R[b]["newv"][:, :TP] for b in range(4)], axis=1).reshape(2, 4, TP, 8, 128)
    wkv_p = np.stack([R[b]["wkvo"][:, 0] for b in range(4)], axis=1)
    sh_p = np.stack([R[b]["sho"][:, 0] for b in range(4)], axis=1)
    pl_p = np.stack([R[b]["plo"][:, 0] for b in range(4)], axis=1)
    nk_s = np.stack([R[c]["newk"][:, TP:] for c in range(8)], axis=1).reshape(2, 8, TS, 8, 128)
    nv_s = np.stack([R[c]["newv"][:, TP:] for c in range(8)], axis=1).reshape(2, 8, TS, 8, 128)
    wkv_s = np.stack([R[c]["wkvo"][:, 1] for c in range(8)], axis=1)
    sh_s = np.stack([R[c]["sho"][:, 1] for c in range(8)], axis=1)
    pl_s = np.stack([R[c]["plo"][:, 1] for c in range(8)], axis=1)
    gv_s = np.stack([R[c]["gvo"] for c in range(8)], axis=1)
    outs = (y_p, y_s, nk_p, nv_p, wkv_p, sh_p, pl_p, nk_s, nv_s, wkv_s, sh_s, pl_s, gv_s)
    return tuple(np.ascontiguousarray(o.astype(np.float32)) for o in outs)
```

```python
import numpy as np
from contextlib import ExitStack
import concourse.bass as bass
import concourse.mybir as mybir
from concourse.bass_utils import run_bass_kernel_spmd

F32 = mybir.dt.float32
BF16 = mybir.dt.bfloat16
AF = mybir.ActivationFunctionType
ALU = mybir.AluOpType
AX = mybir.AxisListType

D = 4096
DFF = 11008
NJ = DFF // 128
PROJ = 9504
TP = 2048
TS = 8
TALL = TP + TS
NPREV = 2048
BFEAT = 3360
MASKW = 3200
C0 = 512
EPS = 1e-6


class Res:
    __slots__ = ("w", "r")

    def __init__(self):
        self.w = None
        self.r = {}


class Eng:
    def __init__(self, fw, key, eng, compute=True):
        self.key = key
        self.e = eng
        self.sem = fw.new_sem("p_" + key) if compute else None
        self.cnt = 0
        self.waited = {}


class FW:
    def __init__(self, nc, stack, n_dma_sems=20):
        self.nc = nc
        self.stack = stack
        self.pe = Eng(self, "pe", nc.tensor)
        self.dve = Eng(self, "dve", nc.vector)
        self.act = Eng(self, "act", nc.scalar)
        self.pool = Eng(self, "pool", nc.gpsimd)
        self.sp = Eng(self, "sp", nc.sync, compute=False)
        self.engs = [self.pe, self.dve, self.act, self.pool, self.sp]
        self.dring = {}
        for q in (self.sp, self.pool):
            self.dring[q.key] = [[self.new_sem("d_%s_%d" % (q.key, i)), 0] for i in range(n_dma_sems)]
        self.dpos = {"sp": 0, "pool": 0}

    def new_sem(self, name):
        return self.stack.enter_context(self.nc.semaphore(name))

    def _wait(self, E, tok):
        if tok is None:
            return
        sem, val, key = tok
        if key == "pe" and E.key == "pe":
            return
        k = id(sem)
        if E.waited.get(k, 0) >= val:
            return
        E.e.wait_ge(sem, val)
        E.waited[k] = val

    def _deps(self, E, reads, writes):
        for r in reads:
            self._wait(E, r.w)
        for w in writes:
            self._wait(E, w.w)
            for t in w.r.values():
                self._wait(E, t)

    def _commit(self, tok, reads, writes):
        for r in reads:
            r.r[id(tok[0])] = tok
        for w in writes:
            w.w = tok
            w.r = {}

    def op(self, E, fn, reads=(), writes=(), inc=True):
        self._deps(E, reads, writes)
        ins = fn()
        if inc:
            E.cnt += 1
            ins.then_inc(E.sem, 1)
            tok = (E.sem, E.cnt, E.key)
        else:
            tok = (E.sem, E.cnt + 1, E.key)
        self._commit(tok, reads, writes)
        return tok

    def dma(self, Q, out, in_, reads=(), writes=()):
        self._deps(Q, reads, writes)
        ring = self.dring[Q.key]
        pos = self.dpos[Q.key]
        self.dpos[Q.key] = (pos + 1) % len(ring)
        ent = ring[pos]
        if ent[1] > 0:
            self._wait(Q, (ent[0], ent[1], "dma"))
        ins = Q.e.dma_start(out=out, in_=in_)
        ent[1] += 16
        ins.then_inc(ent[0], 16)
        tok = (ent[0], ent[1], "dma")
        self._commit(tok, reads, writes)
        return tok

    def all_tokens(self):
        toks = []
        for q in self.dring.values():
            for ent in q:
                if ent[1] > 0:
                    toks.append((ent[0], ent[1], "dma"))
        for E in (self.pe, self.dve, self.act, self.pool):
            if E.cnt > 0:
                toks.append((E.sem, E.cnt, E.key))
        return toks

    def barrier(self):
        toks = self.all_tokens()
        for E in self.engs:
            for t in toks:
                self._wait(E, t)

    def finish(self):
        for t in self.all_tokens():
            self._wait(self.sp, t)


def _host_consts():
    half = 64
    inv = (10000.0 ** (-np.arange(half, dtype=np.float32) / half)).astype(np.float32)
    pos = np.concatenate([np.arange(TP), 16384 + np.arange(TS)]).astype(np.float32)
    ang = pos[:, None] * inv[None, :]
    cs = np.concatenate([np.cos(ang), np.sin(ang)], axis=1).astype(np.float32)
    d = np.arange(MASKW)[None, :] - np.arange(128)[:, None] - C0
    m = ((d >= 0) & (d <= 128)).astype(np.float32) + ((d >= 0) & (d <= 512) & (d % 4 == 0)) + \
        ((d >= 0) & (d <= 2048) & (d % 16 == 0))
    mt = m.astype(np.float32)
    pm = np.zeros((128, 4, 3, 128), np.float32)
    pms = np.zeros((32, 4, 8), np.float32)
    tl = np.arange(128)
    for g, win in enumerate((2, 4, 8, 16)):
        for t in range(128):
            for tp in range(t - win + 1, t + 1):
                if tp >= 0:
                    pm[tp, g, 0, t] += 1.0 / win
                    pm[tp, g, 2, t] += 1.0 / min(win, t + 1)
                else:
                    pm[128 + tp, g, 1, t] += 1.0 / win
            pm[t, g, 0, t] -= 1.0
            pm[t, g, 2, t] -= 1.0
        for t in range(8):
            for e in range(15 + t - win + 1, 15 + t + 1):
                pms[e, g, t] += 1.0 / win
            pms[15 + t, g, t] -= 1.0
    tril = np.tril(np.ones((128, 128), np.float32))
    tri2 = np.zeros((128, 128), np.float32)
    ones2 = np.zeros((128, 128), np.float32)
    mskc = np.zeros((128, 4, 64), np.float32)
    ii = np.arange(64)
    for d_ in range(2):
        sl = slice(d_ * 64, (d_ + 1) * 64)
        tri2[sl, sl] = (ii[:, None] <= ii[None, :])
        ones2[sl, sl] = 1.0
        mskc[sl, 0] = (ii[:, None] < ii[None, :])
        mskc[sl, 1] = (ii[:, None] <= ii[None, :])
        mskc[sl, 2] = (ii[:, None] > ii[None, :])
        mskc[sl, 3] = (ii[:, None] == ii[None, :])
    return dict(c_cs=cs, c_mt=mt, c_pm=pm.reshape(128, 4 * 3 * 128), c_pms=pms.reshape(32, 32), c_tril=tril,
                c_tri2=tri2, c_ones2=ones2, c_msk=mskc.reshape(128, 256))


WNAMES = [("ln_ffn1", [2, 32, 128]), ("w1_gate", [2, D, DFF]), ("w1_up", [2, D, DFF]), ("w1_down", [2, DFF, D]),
          ("ln_mix", [2, 32, 128]), ("w_in", [2, D, PROJ]), ("g_out_a", [2, 1, 1024]), ("mu_b", [2, 1, BFEAT]),
          ("w0", [2, 1, 1024]), ("w_up", [2, 64, 1024]), ("a0", [2, 1, 1024]), ("a_up", [2, 64, 1024]),
          ("g_up", [2, 160, 1024]), ("k_k", [2, 1, 1024]), ("k_a", [2, 1, 1024]), ("r_k", [2, 1, 1024]),
          ("ln_x_w", [2, 1, 1024]), ("ln_x_b", [2, 1, 1024]), ("w_s", [2, 8, 128, 128]), ("b_s", [2, 8, 128]),
          ("g_out_c", [2, 1, 1024]), ("w_pool", [2, 4, 256, 256]), ("pool_scale", [2, 1, 1024]),
          ("g_out_d", [2, 1, 1024]), ("w_out", [2, D, D]), ("ln_ffn2", [2, 32, 128]), ("w2_gate", [2, D, DFF]),
          ("w2_up", [2, D, DFF]), ("w2_down", [2, DFF, D]), ("ln_final", [32, 128])]


def build_nc():
    nc = bass.Bass("TRN2", target_bir_lowering=False)

    def din(name, shape):
        return nc.dram_tensor(name, list(shape), F32, kind="ExternalInput").ap()

    def dout(name, shape):
        return nc.dram_tensor(name, list(shape), F32, kind="ExternalOutput").ap()

    def dscr(name, shape):
        return nc.dram_tensor(name, list(shape), F32, kind="Internal").ap()

    X = din("x_all", [TALL, D])
    CK = din("cache_k", [2, NPREV, 1024])
    CV = din("cache_v", [2, NPREV, 1024])
    WKV0 = din("wkv0", [2, 16, 64, 64])
    SH0 = din("shift0", [2, 1, BFEAT])
    PL0 = din("pool0", [2, 15, 1024])
    Wd = {n: din(n, s) for n, s in WNAMES}
    CCS = din("c_cs", [TALL, 128])
    CMT = din("c_mt", [128, MASKW])
    CPM = din("c_pm", [128, 4 * 3 * 128])
    CPMS = din("c_pms", [32, 32])
    CTRIL = din("c_tril", [128, 128])
    CTRI2 = din("c_tri2", [128, 128])
    CONES2 = din("c_ones2", [128, 128])
    CMSK = din("c_msk", [128, 256])

    Y = dout("y", [TALL, D])
    NEWK = dout("newk", [2, TALL, 1024])
    NEWV = dout("newv", [2, TALL, 1024])
    WKVO = dout("wkvo", [2, 2, 16, 64, 64])
    SHO = dout("sho", [2, 2, BFEAT])
    PLO = dout("plo", [2, 2, 15, 1024])
    GVO = dout("gvo", [2, TS, 1024])

    R1 = dscr("r1", [TALL, D])
    R2 = dscr("r2", [TALL, D])
    RA = dscr("ra", [TALL, D])
    RB = dscr("rb", [TALL, D])
    Z = dscr("z", [TALL, PROJ])
    MIX = dscr("mix", [TALL, D])
    XB = dscr("xb", [TALL, 2, 5, 512])
    OT = dscr("ot", [TALL, 1024])

    rX, rR1, rR2, rRA, rRB, rZ, rMIX, rXB, rOT, rOUT = (Res() for _ in range(10))

    top = ExitStack()
    with top:
        fw = FW(nc, top)
        PE, DVE, ACT, POOL, SP = fw.pe, fw.dve, fw.act, fw.pool, fw.sp
        T, V, A, G = nc.tensor, nc.vector, nc.scalar, nc.gpsimd

        _uid = [0]

        def sb(st, name, shape, dt=F32):
            _uid[0] += 1
            return st.enter_context(nc.sbuf_tensor("%s_%d" % (name, _uid[0]), list(shape), dt))

        def pe(fn, r=(), w=(), inc=True):
            return fw.op(PE, fn, r, w, inc)

        def dve(fn, r=(), w=()):
            return fw.op(DVE, fn, r, w)

        def act(fn, r=(), w=()):
            return fw.op(ACT, fn, r, w)

        def pool(fn, r=(), w=()):
            return fw.op(POOL, fn, r, w)

        def ld(out, in_, r=(), w=()):
            return fw.dma(SP, out, in_, r, w)

        def ldc(out, in_, r=(), w=()):
            return fw.dma(POOL, out, in_, r, w)

        ident_f = sb(top, "ident_f", [128, 128]); r_idf = Res()
        ident_b = sb(top, "ident_b", [128, 128], BF16); r_idb = Res()
        pool(lambda: G.memset(ident_f[:], 0.0), w=[r_idf])
        pool(lambda: G.affine_select(out=ident_f[:], in_=ident_f[:], pattern=[[-1, 128]], compare_op=ALU.not_equal,
                                     fill=1.0, base=0, channel_multiplier=1), r=[r_idf], w=[r_idf])
        dve(lambda: V.tensor_copy(out=ident_b[:], in_=ident_f[:]), r=[r_idf], w=[r_idb])

        NPB = 6
        pb = [top.enter_context(nc.psum_tensor("pb%d" % i, [128, 512], F32)) for i in range(NPB)]
        r_pb = [Res() for _ in range(NPB)]
        ptb = top.enter_context(nc.psum_tensor("ptb", [128, 8, 128], BF16)); r_ptb = [Res(), Res()]
        ptf = top.enter_context(nc.psum_tensor("ptf", [128, 4, 128], F32)); r_ptf = Res()

        def token_local(st, jobs):
            Xn = sb(st, "Xn", [128, 32, 768], BF16); r_Xn = Res()
            H = sb(st, "H", [128, 44, 768], BF16); r_H = Res()
            NW = 3
            wbuf = [sb(st, "wb%d" % i, [128, 5632], BF16) for i in range(NW)]
            r_wb = [Res() for _ in range(NW)]
            xst = sb(st, "xst", [128, D]); r_xst = Res()
            xnb = sb(st, "xnb", [128, D], BF16); r_xnb = Res()
            ss = sb(st, "ss", [128, 8]); r_ss = Res()
            gcol = sb(st, "gcol", [128, 32]); r_gcol = Res()
            graw = sb(st, "graw", [32, 128]); r_graw = Res()
            sg = [sb(st, "sg%d" % i, [128, 2, 768]) for i in range(2)]; r_sg = [Res(), Res()]
            rsd = [sb(st, "rsd%d" % i, [128, 512]) for i in range(2)]; r_rsd = [Res(), Res()]
            yo = [sb(st, "yo%d" % i, [128, 512]) for i in range(2)]; r_yo = [Res(), Res()]
            cnt = {"w": 0, "sg": 0, "rsd": 0, "yo": 0, "pb": 0}

            def load_gcol(gsrc):
                ld(graw[:], gsrc, w=[r_graw])
                pe(lambda: T.transpose(out=ptf[:, 0, 0:32], in_=graw[:, :], identity=ident_f[0:32, 0:32]),
                   r=[r_graw, r_idf], w=[r_ptf])
                dve(lambda: V.tensor_copy(out=gcol[:], in_=ptf[:, 0, 0:32]), r=[r_ptf], w=[r_gcol])

            def load_norm(src, rsrc, t0, TT, norm=True):
                nsub = (TT + 127) // 128
                for s in range(nsub):
                    sub = min(128, TT - s * 128)
                    rows = slice(t0 + s * 128, t0 + s * 128 + sub)
                    if norm:
                        ld(xst[:sub, :], src[rows, :], r=[rsrc], w=[r_xst])
                        act(lambda: A.activation(out=xnb[:sub, :], in_=xst[:sub, :], func=AF.Square,
                                                 accum_out=ss[:sub, 0:1]), r=[r_xst], w=[r_xnb, r_ss])
                        dve(lambda: V.tensor_scalar(out=ss[:sub, 1:2], in0=ss[:sub, 0:1], scalar1=1.0 / D, scalar2=EPS,
                                                    op0=ALU.mult, op1=ALU.add), r=[r_ss], w=[r_ss])
                        act(lambda: A.activation(out=ss[:sub, 2:3], in_=ss[:sub, 1:2], func=AF.Sqrt), r=[r_ss], w=[r_ss])
                        dve(lambda: V.reciprocal(out=ss[:sub, 3:4], in_=ss[:sub, 2:3]), r=[r_ss], w=[r_ss])
                        act(lambda: A.activation(out=xnb[:sub, :], in_=xst[:sub, :], func=AF.Copy, scale=ss[:sub, 3:4]),
                            r=[r_xst, r_ss], w=[r_xnb])
                    else:
                        ldc(xnb[:sub, :], src[rows, :], r=[rsrc], w=[r_xnb])
                    for c8 in range(4):
                        hb = c8 % 2
                        for j in range(8):
                            c = c8 * 8 + j
                            pe(lambda c=c, j=j: T.transpose(out=ptb[:, j, 0:sub], in_=xnb[:sub, c * 128:(c + 1) * 128],
                                                            identity=ident_b[0:sub, 0:sub]),
                               r=[r_xnb, r_idb], w=[r_ptb[0]], inc=(j == 7))
                        for j in range(8):
                            c = c8 * 8 + j
                            if norm:
                                dve(lambda c=c, j=j: V.tensor_scalar(out=Xn[:, c, s * 128:s * 128 + sub], in0=ptb[:, j, 0:sub],
                                                                     scalar1=gcol[:, c:c + 1], scalar2=None, op0=ALU.mult),
                                    r=[r_ptb[0], r_gcol], w=[r_Xn])
                            else:
                                dve(lambda c=c, j=j: V.tensor_copy(out=Xn[:, c, s * 128:s * 128 + sub], in_=ptb[:, j, 0:sub]),
                                    r=[r_ptb[0]], w=[r_Xn])

            def wload(view_shape, src_ap):
                i = cnt["w"] % NW
                cnt["w"] += 1
                n = 1
                for v in view_shape[1:]:
                    n *= v
                flat = wbuf[i][:, 0:n]
                if len(view_shape) == 3:
                    view = flat.rearrange("p (a b) -> p a b", a=view_shape[1])
                else:
                    view = flat
                ldc(view, src_ap, w=[r_wb[i]])
                return view, r_wb[i]

            def ffn(TT, wg, wu, wdn, resid, r_resid, dst, r_dst, t0):
                nsub = (TT + 127) // 128
                n0 = TT // 2
                thb = [(0, n0), (n0, TT - n0)]
                pairc = [0]
                for half in range(2):
                    nch = 44 if half == 0 else 42
                    jb = 0 if half == 0 else 44
                    tiles = [(gu, mt, kh) for mt in range(nch // 2) for gu in range(2) for kh in range(2)]
                    loaded = {}

                    def issue(i):
                        if i < len(tiles):
                            gu, mt, kh = tiles[i]
                            wsrc = wg if gu == 0 else wu
                            c0 = (jb + mt * 2) * 128
                            src = wsrc[kh * 2048:(kh + 1) * 2048, c0:c0 + 256].rearrange("(c p) n -> p c n", p=128)
                            loaded[i] = wload([128, 16, 256], src)
                    issue(0)
                    issue(1)
                    pair_of = {}
                    for i, (gu, mt, kh) in enumerate(tiles):
                        issue(i + 2)
                        wv, rw = loaded.pop(i)
                        si = mt % 2
                        for mc in range(2):
                            if kh == 0:
                                pair_of[(gu, mc)] = pairc[0] % 3
                                pairc[0] += 1
                            pr = pair_of[(gu, mc)]
                            for th, (c0, n) in enumerate(thb):
                                bank = pr * 2 + th
                                for k in range(16):
                                    pe(lambda: T.matmul(pb[bank][:, 0:n], lhsT=wv[:, k, mc * 128:(mc + 1) * 128],
                                                        rhs=Xn[:, kh * 16 + k, c0:c0 + n],
                                                        start=(kh == 0 and k == 0), stop=(kh == 1 and k == 15)),
                                       r=[rw, r_Xn], w=[r_pb[bank]], inc=(mc == 1 and th == 1 and k == 15))
                        if kh == 1:
                            for mc in range(2):
                                pr = pair_of[(gu, mc)]
                                for th, (c0, n) in enumerate(thb):
                                    bank = pr * 2 + th
                                    if gu == 0:
                                        act(lambda: A.activation(out=sg[si][:, mc, c0:c0 + n], in_=pb[bank][:, 0:n], func=AF.Silu),
                                            r=[r_pb[bank]], w=[r_sg[si]])
                                    else:
                                        dve(lambda: V.tensor_tensor(out=H[:, mt * 2 + mc, c0:c0 + n], in0=pb[bank][:, 0:n],
                                                                    in1=sg[si][:, mc, c0:c0 + n], op=ALU.mult),
                                            r=[r_pb[bank], r_sg[si]], w=[r_H])
                    groups = [(0, 11), (11, 11), (22, 11), (33, 11)] if half == 0 else [(0, 11), (11, 11), (22, 10), (32, 10)]
                    tiles = [(fb, g) for fb in range(8) for g in range(4)]
                    loaded = {}

                    def issue2(i):
                        if i < len(tiles):
                            fb, g = tiles[i]
                            j0, nj = groups[g]
                            src = wdn[(jb + j0) * 128:(jb + j0 + nj) * 128, fb * 512:(fb + 1) * 512].rearrange("(c p) n -> p c n", p=128)
                            loaded[i] = wload([128, nj, 512], src)
                    issue2(0)
                    issue2(1)
                    rs_src, rs_res = (resid, r_resid) if half == 0 else (dst, r_dst)
                    for i, (fb, g) in enumerate(tiles):
                        issue2(i + 2)
                        wv, rw = loaded.pop(i)
                        j0, nj = groups[g]
                        for s in range(nsub):
                            sub = min(128, TT - s * 128)
                            for jj in range(nj):
                                j = j0 + jj
                                pe(lambda: T.matmul(pb[s][:sub, :], lhsT=H[:, j, s * 128:s * 128 + sub], rhs=wv[:, jj, :],
                                                    start=(j == 0), stop=(j == nch - 1)), r=[rw, r_H], w=[r_pb[s]],
                                   inc=(s == nsub - 1 and jj == nj - 1))
                        if g == 3:
                            for s in range(nsub):
                                sub = min(128, TT - s * 128)
                                rows = slice(t0 + s * 128, t0 + s * 128 + sub)
                                cols = slice(fb * 512, (fb + 1) * 512)
                                ri = cnt["rsd"] % 2
                                cnt["rsd"] += 1
                                ld(rsd[ri][:sub, :], rs_src[rows, cols], r=[rs_res], w=[r_rsd[ri]])
                                dve(lambda: V.scalar_tensor_tensor(
                                    out=yo[ri][:sub, :], in0=pb[s][:sub, :], scalar=0.5, in1=rsd[ri][:sub, :],
                                    op0=ALU.mult, op1=ALU.add), r=[r_pb[s], r_rsd[ri]], w=[r_yo[ri]])
                                ld(dst[rows, cols], yo[ri][:sub, :], r=[r_yo[ri]], w=[r_dst])

            def linear_tm(TT, wsrc, ncols, evac):
                nsub = (TT + 127) // 128
                ncb_n = (ncols + 511) // 512
                tiles = [(cb, kg) for cb in range(ncb_n) for kg in range(4)]
                loaded = {}

                def issue(i):
                    if i < len(tiles):
                        cb, kg = tiles[i]
                        ncb = min(512, ncols - cb * 512)
                        src = wsrc[kg * 1024:(kg + 1) * 1024, cb * 512:cb * 512 + ncb].rearrange("(c p) n -> p c n", p=128)
                        loaded[i] = wload([128, 8, ncb], src)
                issue(0)
                issue(1)
                for i, (cb, kg) in enumerate(tiles):
                    issue(i + 2)
                    wv, rw = loaded.pop(i)
                    ncb = min(512, ncols - cb * 512)
                    for s in range(nsub):
                        sub = min(128, TT - s * 128)
                        for k in range(8):
                            pe(lambda s=s, sub=sub, k=k, wv=wv: T.matmul(
                                pb[s][:sub, 0:ncb], lhsT=Xn[:, kg * 8 + k, s * 128:s * 128 + sub], rhs=wv[:, k, :],
                                start=(kg == 0 and k == 0), stop=(kg == 3 and k == 7)), r=[rw, r_Xn], w=[r_pb[s]],
                               inc=(s == nsub - 1 and k == 7))
                    if kg == 3:
                        for s in range(nsub):
                            sub = min(128, TT - s * 128)
                            evac(s, sub, cb, ncb)

            ctx = dict(load_gcol=load_gcol, load_norm=load_norm, ffn=ffn, linear_tm=linear_tm, rsd=rsd, r_rsd=r_rsd,
                       yo=yo, r_yo=r_yo, cnt=cnt)
            for job in jobs:
                job(ctx)

        TILES = [(0, 768), (768, 768), (1536, 520)]

        def job_ffn(l, which, src, rsrc, dst, rdst):
            def run(ctx):
                gname = "ln_ffn1" if which == 1 else "ln_ffn2"
                ctx["load_gcol"](Wd[gname][l])
                for (t0, TT) in TILES:
                    ctx["load_norm"](src, rsrc, t0, TT)
                    ctx["ffn"](TT, Wd["w%d_gate" % which][l], Wd["w%d_up" % which][l], Wd["w%d_down" % which][l],
                               src, rsrc, dst, rdst, t0)
            return run

        def job_win(l, src, rsrc):
            def run(ctx):
                ctx["load_gcol"](Wd["ln_mix"][l])
                yo, r_yo, cnt = ctx["yo"], ctx["r_yo"], ctx["cnt"]
                for (t0, TT) in TILES:
                    ctx["load_norm"](src, rsrc, t0, TT)

                    def evac(s, sub, cb, ncb, t0=t0):
                        ri = cnt["yo"] % 2
                        cnt["yo"] += 1
                        rows = slice(t0 + s * 128, t0 + s * 128 + sub)
                        act(lambda: A.copy(out=yo[ri][:sub, 0:ncb], in_=pb[s][:sub, 0:ncb]), r=[r_pb[s]], w=[r_yo[ri]])
                        ld(Z[rows, cb * 512:cb * 512 + ncb], yo[ri][:sub, 0:ncb], r=[r_yo[ri]], w=[rZ])
                    ctx["linear_tm"](TT, Wd["w_in"][l], PROJ, evac)
            return run

        def job_wout(l, hsrc, rh, dst, rdst):
            def run(ctx):
                yo, r_yo, rsd, r_rsd, cnt = ctx["yo"], ctx["r_yo"], ctx["rsd"], ctx["r_rsd"], ctx["cnt"]
                for (t0, TT) in TILES:
                    ctx["load_norm"](MIX, rMIX, t0, TT, norm=False)

                    def evac(s, sub, cb, ncb, t0=t0):
                        ri = cnt["yo"] % 2
                        cnt["yo"] += 1
                        rows = slice(t0 + s * 128, t0 + s * 128 + sub)
                        cols = slice(cb * 512, cb * 512 + ncb)
                        ld(rsd[ri][:sub, :], hsrc[rows, cols], r=[rh], w=[r_rsd[ri]])
                        dve(lambda: V.tensor_tensor(out=yo[ri][:sub, :], in0=pb[s][:sub, :], in1=rsd[ri][:sub, :], op=ALU.add),
                            r=[r_pb[s], r_rsd[ri]], w=[r_yo[ri]])
                        ld(dst[rows, cols], yo[ri][:sub, :], r=[r_yo[ri]], w=[rdst])
                    ctx["linear_tm"](TT, Wd["w_out"][l], D, evac)
            return run

        def job_final(src, rsrc):
            def run(ctx):
                pass
            return run

        def rms_rows(st_tiles, x_ap, sub, gbc_ap, out_ap, r_x, r_g, r_out, n):
            junk, r_junk, ss, r_ss = st_tiles
            act(lambda: A.activation(out=junk[:sub, 0:n], in_=x_ap, func=AF.Square, accum_out=ss[:sub, 0:1]),
                r=[r_x], w=[r_junk, r_ss])
            dve(lambda: V.tensor_scalar(out=ss[:sub, 1:2], in0=ss[:sub, 0:1], scalar1=1.0 / n, scalar2=EPS,
                                        op0=ALU.mult, op1=ALU.add), r=[r_ss], w=[r_ss])
            act(lambda: A.activation(out=ss[:sub, 2:3], in_=ss[:sub, 1:2], func=AF.Sqrt), r=[r_ss], w=[r_ss])
            dve(lambda: V.reciprocal(out=ss[:sub, 3:4], in_=ss[:sub, 2:3]), r=[r_ss], w=[r_ss])
            dve(lambda: V.scalar_tensor_tensor(out=out_ap, in0=x_ap, scalar=ss[:sub, 3:4], in1=gbc_ap,
                                               op0=ALU.mult, op1=ALU.mult), r=[r_x, r_ss, r_g], w=[r_out])

        SEQS = [dict(r0=0, T=TP, nprev=0, si=0), dict(r0=TP, T=TS, nprev=NPREV, si=1)]

        def final_norm(src, rsrc):
            with ExitStack() as st:
                xst = sb(st, "f_x", [128, D]); r_x = Res()
                gbc = sb(st, "f_g", [128, D]); r_g = Res()
                junk = sb(st, "f_j", [128, D]); r_j = Res()
                ss = sb(st, "f_ss", [128, 8]); r_ss = Res()
                yo = sb(st, "f_y", [128, D]); r_y = Res()
                ld(gbc[:], Wd["ln_final"].rearrange("a b -> (a b)").unsqueeze(0).partition_broadcast(128), w=[r_g])
                for (t0, TT) in [(i * 128, 128) for i in range(16)] + [(TP, TS)]:
                    ld(xst[:TT, :], src[t0:t0 + TT, :], r=[rsrc], w=[r_x])
                    rms_rows((junk, r_j, ss, r_ss), xst[:TT, :], TT, gbc[:TT, :], yo[:TT, :], r_x, r_g, r_y, D)
                    ld(Y[t0:t0 + TT, :], yo[:TT, :], r=[r_y], w=[rOUT])
            fw.barrier()

        def attention(l):
            for sq in SEQS:
                with ExitStack() as st:
                    r0, Tn, nprev = sq["r0"], sq["T"], sq["nprev"]
                    Tk = nprev + Tn
                    nkt = (Tk + 127) // 128
                    qT = sb(st, "qT", [128, 8, max(Tn, 128)], BF16); r_qT = Res()
                    kT = sb(st, "kT", [128, 8, nkt * 128], BF16); r_kT = Res()
                    va = sb(st, "va", [128, nkt, 8, 130], BF16); r_va = Res()
                    mt = sb(st, "mt", [128, MASKW], BF16); r_mt = Res()
                    zqk = sb(st, "zqk", [128, 2048]); r_zqk = Res()
                    rot = sb(st, "rot", [128, 2048]); r_rot = Res()
                    rotb = sb(st, "rotb", [128, 2048], BF16); r_rotb = Res()
                    t1 = sb(st, "t1", [128, 1024]); r_t1 = Res()
                    t2 = sb(st, "t2", [128, 1024]); r_t2 = Res()
                    cs = sb(st, "cs", [128, 128]); r_cs = Res()
                    eb = [sb(st, "eb%d" % i, [128, 512], BF16) for i in range(2)]; r_eb = [Res(), Res()]
                    pbf = [sb(st, "pbf%d" % i, [128, 512], BF16) for i in range(2)]; r_pbf = [Res(), Res()]
                    oa = sb(st, "oa", [128, 4, 1024]); r_oa = [Res() for _ in range(4)]
                    gbc = sb(st, "gbc", [128, 1024]); r_gbc = Res()
                    junk = sb(st, "ajunk", [128, 1024]); r_junk = Res()
                    ss = sb(st, "ass", [128, 8]); r_ss = Res()
                    rd = sb(st, "ard", [128, 8]); r_rd = Res()
                    yo = sb(st, "ayo", [128, 1024]); r_yo = Res()
                    ldc(mt[:], CMT[:, :], w=[r_mt])
                    ld(gbc[:], Wd["g_out_a"][l].partition_broadcast(128), w=[r_gbc])
                    pool(lambda: G.memset(kT[:], 0.0), w=[r_kT])
                    pool(lambda: G.memset(va[:], 0.0), w=[r_va])
                    pool(lambda: G.memset(va[:, :, :, 128:129], 1.0), w=[r_va])
                    if nprev > 0:
                        kc = sb(st, "kc", [128, 16, 1024], BF16); r_kc = Res()
                        ldc(kc[:], CK[l].rearrange("(s p) c -> p s c", p=128), w=[r_kc])
                        for s in range(16):
                            ldc(va[:, s, :, 0:128], CV[l, s * 128:(s + 1) * 128, :].rearrange("p (h d) -> p h d", d=128), r=[r_va], w=[r_va])
                        for s in range(16):
                            for h in range(8):
                                pe(lambda s=s, h=h: T.transpose(out=ptb[:, h, :], in_=kc[:, s, h * 128:(h + 1) * 128],
                                                                identity=ident_b[:, :]), r=[r_kc, r_idb], w=[r_ptb[0]])
                            dve(lambda s=s: V.tensor_copy(out=kT[:, :, s * 128:(s + 1) * 128], in_=ptb[:, :, :]),
                                r=[r_ptb[0]], w=[r_kT])
                    nsub = (Tn + 127) // 128
                    if Tn >= 128:
                        for s in range(nsub):
                            ldc(va[:, nprev // 128 + s, :, 0:128],
                                Z[r0 + s * 128:r0 + (s + 1) * 128, 2048:3072].rearrange("p (h d) -> p h d", d=128), r=[rZ, r_va], w=[r_va])
                    else:
                        ldc(va[0:Tn, nprev // 128, :, 0:128],
                            Z[r0:r0 + Tn, 2048:3072].rearrange("p (h d) -> p h d", d=128), r=[rZ, r_va], w=[r_va])
                    ld(NEWV[l, r0:r0 + Tn, :], Z[r0:r0 + Tn, 2048:3072], r=[rZ], w=[rOUT])
                    for s in range(nsub):
                        sub = min(128, Tn - s * 128)
                        rows = slice(r0 + s * 128, r0 + s * 128 + sub)
                        ld(zqk[:sub, :], Z[rows, 0:2048], r=[rZ], w=[r_zqk])
                        ld(cs[:sub, :], CCS[rows, :], w=[r_cs])
                        zv = zqk[:sub, :].rearrange("p (h two d) -> p h two d", two=2, d=64)
                        rv = rot[:sub, :].rearrange("p (h two d) -> p h two d", two=2, d=64)
                        x1, x2 = zv[:, :, 0, :], zv[:, :, 1, :]
                        cb_ = cs[:sub, 0:64].unsqueeze(1).to_broadcast([sub, 16, 64])
                        sb_ = cs[:sub, 64:128].unsqueeze(1).to_broadcast([sub, 16, 64])
                        t1v = t1[:sub, :].rearrange("p (h d) -> p h d", d=64)
                        t2v = t2[:sub, :].rearrange("p (h d) -> p h d", d=64)
                        dve(lambda: V.tensor_tensor(out=t1v, in0=x1, in1=cb_, op=ALU.mult), r=[r_zqk, r_cs], w=[r_t1])
                        dve(lambda: V.tensor_tensor(out=t2v, in0=x2, in1=sb_, op=ALU.mult), r=[r_zqk, r_cs], w=[r_t2])
                        dve(lambda: V.tensor_tensor(out=rv[:, :, 0, :], in0=t1v, in1=t2v, op=ALU.subtract), r=[r_t1, r_t2], w=[r_rot])
                        dve(lambda: V.tensor_tensor(out=t1v, in0=x1, in1=sb_, op=ALU.mult), r=[r_zqk, r_cs], w=[r_t1])
                        dve(lambda: V.tensor_tensor(out=t2v, in0=x2, in1=cb_, op=ALU.mult), r=[r_zqk, r_cs], w=[r_t2])
                        dve(lambda: V.tensor_tensor(out=rv[:, :, 1, :], in0=t1v, in1=t2v, op=ALU.add), r=[r_t1, r_t2], w=[r_rot])
                        ld(NEWK[l, rows, :], rot[:sub, 1024:2048], r=[r_rot], w=[rOUT])
                        act(lambda: A.copy(out=rotb[:sub, :], in_=rot[:sub, :]), r=[r_rot], w=[r_rotb])
                        for half in range(2):
                            for h in range(8):
                                c = half * 8 + h
                                pe(lambda c=c, h=h: T.transpose(out=ptb[:, h, 0:sub], in_=rotb[:sub, c * 128:(c + 1) * 128],
                                                                identity=ident_b[0:sub, 0:sub]), r=[r_rotb, r_idb], w=[r_ptb[0]])
                            if half == 0:
                                dve(lambda: V.tensor_copy(out=qT[:, :, s * 128:s * 128 + sub], in_=ptb[:, :, 0:sub]),
                                    r=[r_ptb[0]], w=[r_qT])
                            else:
                                dve(lambda: V.tensor_copy(out=kT[:, :, nprev + s * 128:nprev + s * 128 + sub], in_=ptb[:, :, 0:sub]),
                                    r=[r_ptb[0]], w=[r_kT])
                    QB = min(512, Tn)
                    iters = []
                    for qb in range(Tn // QB):
                        q0 = nprev + qb * QB
                        kt_lo = max(0, (q0 - 2048) // 128)
                        kt_hi = (q0 + QB - 1) // 128
                        for h in range(8):
                            for kt in range(kt_lo, kt_hi + 1):
                                iters.append((qb, q0, h, kt, kt_lo, kt_hi))
                    nq = QB
                    nqs = (nq + 127) // 128

                    def emit_qk(i):
                        qb, q0, h, kt, kt_lo, kt_hi = iters[i]
                        bi = i % 2
                        pe(lambda: T.matmul(pb[4 + bi][:, 0:nq], lhsT=kT[:, h, kt * 128:(kt + 1) * 128],
                                            rhs=qT[:, h, qb * QB:qb * QB + nq], start=True, stop=True),
                           r=[r_kT, r_qT], w=[r_pb[4 + bi]])

                    def emit_rest(i):
                        qb, q0, h, kt, kt_lo, kt_hi = iters[i]
                        bi = i % 2
                        c0 = q0 - kt * 128 + C0
                        act(lambda: A.activation(out=eb[bi][:, 0:nq], in_=pb[4 + bi][:, 0:nq], func=AF.Exp,
                                                 scale=float(128 ** -0.5)), r=[r_pb[4 + bi]], w=[r_eb[bi]])
                        pool(lambda: G.tensor_tensor(out=pbf[bi][:, 0:nq], in0=eb[bi][:, 0:nq],
                                                     in1=mt[:, c0:c0 + nq], op=ALU.mult),
                             r=[r_eb[bi], r_mt], w=[r_pbf[bi]])
                        for qs in range(nqs):
                            qsub = min(128, nq - qs * 128)
                            pe(lambda: T.matmul(
                                pb[qs][:qsub, 0:129], lhsT=pbf[bi][:, qs * 128:qs * 128 + qsub], rhs=va[:, kt, h, 0:129],
                                start=(kt == kt_lo), stop=(kt == kt_hi)), r=[r_pbf[bi], r_va], w=[r_pb[qs]],
                               inc=(qs == nqs - 1))
                        if kt == kt_hi:
                            for qs in range(nqs):
                                qsub = min(128, nq - qs * 128)
                                dve(lambda: V.reciprocal(out=rd[:qsub, qs:qs + 1], in_=pb[qs][:qsub, 128:129]),
                                    r=[r_pb[qs]], w=[r_rd])
                                dve(lambda: V.tensor_scalar(
                                    out=oa[:qsub, qs, h * 128:(h + 1) * 128], in0=pb[qs][:qsub, 0:128], scalar1=rd[:qsub, qs:qs + 1],
                                    scalar2=None, op0=ALU.mult), r=[r_pb[qs], r_rd], w=[r_oa[qs]])
                            if h == 7:
                                for qs in range(nqs):
                                    qsub = min(128, nq - qs * 128)
                                    rows = slice(r0 + qb * QB + qs * 128, r0 + qb * QB + qs * 128 + qsub)
                                    rms_rows((junk, r_junk, ss, r_ss), oa[:qsub, qs, :], qsub, gbc[:qsub, :], yo[:qsub, :],
                                             r_oa[qs], r_gbc, r_yo, 1024)
                                    ld(MIX[rows, 0:1024], yo[:qsub, :], r=[r_yo], w=[rMIX])

                    emit_qk(0)
                    for i in range(len(iters)):
                        if i + 1 < len(iters):
                            emit_qk(i + 1)
                        emit_rest(i)
                fw.barrier()

        def gmlp_pool(l):
            with ExitStack() as st:
                wsT = sb(st, "wsT", [128, 8, 128], BF16); r_wsT = Res()
                wsr = sb(st, "wsr", [128, 128]); r_wsr = Res()
                wsm = sb(st, "wsm", [128, 128], BF16); r_wsm = Res()
                tril = sb(st, "tril", [128, 128]); r_tril = Res()
                bsr = sb(st, "bsr", [8, 128]); r_bsr = Res()
                bcol = sb(st, "bcol", [128, 8]); r_bcol = Res()
                gc = sb(st, "gc", [128, 1024]); r_gc = Res()
                gd = sb(st, "gd", [128, 1024]); r_gd = Res()
                psc = sb(st, "psc", [128, 1024]); r_psc = Res()
                pm = sb(st, "pm", [128, 12, 128], BF16); r_pm = Res()
                pms = sb(st, "pms", [32, 4, 8], BF16); r_pms = Res()
                wp = sb(st, "wp", [128, 4, 2, 256], BF16); r_wp = Res()
                u = sb(st, "gu", [128, 1024]); r_u = Res()
                vb = sb(st, "gvb", [128, 1024], BF16); r_vb = Res()
                oc = sb(st, "goc", [128, 1024]); r_oc = Res()
                pcur = sb(st, "pcur", [128, 1024], BF16); r_pcur = Res()
                pprev = sb(st, "pprev", [128, 1024], BF16); r_pprev = Res()
                pext = sb(st, "pext", [32, 1024], BF16); r_pext = Res()
                plT = sb(st, "plT", [128, 8, 128], BF16); r_plT = Res()
                od = sb(st, "god", [128, 1024]); r_od = Res()
                junk = sb(st, "gjunk", [128, 1024]); r_junk = Res()
                ss = sb(st, "gss", [128, 8]); r_ss = Res()
                yo = sb(st, "gyo", [128, 1024]); r_yo = Res()
                ld(tril[:], CTRIL[:, :], w=[r_tril])
                ld(gc[:], Wd["g_out_c"][l].partition_broadcast(128), w=[r_gc])
                ld(gd[:], Wd["g_out_d"][l].partition_broadcast(128), w=[r_gd])
                ld(psc[:], Wd["pool_scale"][l].partition_broadcast(128), w=[r_psc])
                ldc(pm[:], CPM.rearrange("p (a b) -> p a b", b=128), w=[r_pm])
                ldc(pms[:], CPMS.rearrange("p (a b) -> p a b", b=8), w=[r_pms])
                ldc(wp[:], Wd["w_pool"][l].rearrange("g (c p) d -> p g c d", p=128), w=[r_wp])
                ld(bsr[:], Wd["b_s"][l], w=[r_bsr])
                pe(lambda: T.transpose(out=ptf[:, 0, 0:8], in_=bsr[:, :], identity=ident_f[0:8, 0:8]), r=[r_bsr, r_idf], w=[r_ptf])
                dve(lambda: V.tensor_copy(out=bcol[:], in_=ptf[:, 0, 0:8]), r=[r_ptf], w=[r_bcol])
                for g in range(8):
                    ld(wsr[:], Wd["w_s"][l, g], w=[r_wsr])
                    dve(lambda: V.tensor_tensor(out=wsm[:], in0=wsr[:], in1=tril[:], op=ALU.mult), r=[r_wsr, r_tril], w=[r_wsm])
                    pe(lambda g=g: T.transpose(out=ptb[:, g, :], in_=wsm[:, :], identity=ident_b[:, :]), r=[r_wsm, r_idb], w=[r_ptb[0]])
                dve(lambda: V.tensor_copy(out=wsT[:], in_=ptb[:]), r=[r_ptb[0]], w=[r_wsT])
                for sq in SEQS:
                    r0, Tn, nprev, si = sq["r0"], sq["T"], sq["nprev"], sq["si"]
                    nsub = (Tn + 127) // 128
                    if si == 0:
                        ld(PLO[l, 0], Z[r0 + Tn - 15:r0 + Tn, 8480:9504], r=[rZ], w=[rOUT])
                    else:
                        ld(PLO[l, 1, 0:7, :], PL0[l, 8:15, :], w=[rOUT])
                        ld(PLO[l, 1, 7:15, :], Z[r0:r0 + Tn, 8480:9504], r=[rZ], w=[rOUT])
                        ld(GVO[l], Z[r0:r0 + Tn, 7456:8480], r=[rZ], w=[rOUT])
                    ld(SHO[l, si:si + 1, :], Z[r0 + Tn - 1:r0 + Tn, 3072:3072 + BFEAT], r=[rZ], w=[rOUT])
                    for s in range(nsub):
                        sub = min(128, Tn - s * 128)
                        rows = slice(r0 + s * 128, r0 + s * 128 + sub)
                        ld(u[:sub, :], Z[rows, 6432:7456], r=[rZ], w=[r_u])
                        ldc(vb[:sub, :], Z[rows, 7456:8480], r=[rZ], w=[r_vb])
                        for g in range(8):
                            bank = g // 4
                            pe(lambda g=g, bank=bank: T.matmul(pb[bank][:sub, (g % 4) * 128:(g % 4 + 1) * 128], lhsT=wsT[:sub, g, 0:sub],
                                                               rhs=vb[:sub, g * 128:(g + 1) * 128], start=True, stop=True),
                               r=[r_wsT, r_vb], w=[r_pb[bank]])
                        for g in range(8):
                            bank = g // 4
                            dve(lambda g=g, bank=bank: V.scalar_tensor_tensor(
                                out=oc[:sub, g * 128:(g + 1) * 128], in0=pb[bank][:sub, (g % 4) * 128:(g % 4 + 1) * 128],
                                scalar=bcol[:sub, g:g + 1], in1=u[:sub, g * 128:(g + 1) * 128], op0=ALU.add, op1=ALU.mult),
                                r=[r_pb[bank], r_bcol, r_u], w=[r_oc])
                        rms_rows((junk, r_junk, ss, r_ss), oc[:sub, :], sub, gc[:sub, :], yo[:sub, :], r_oc, r_gc, r_yo, 1024)
                        ld(MIX[rows, 2048:3072], yo[:sub, :], r=[r_yo], w=[rMIX])
                        if si == 0:
                            cur, r_cur = (pcur, r_pcur) if s % 2 == 0 else (pprev, r_pprev)
                            prv, r_prv = (pprev, r_pprev) if s % 2 == 0 else (pcur, r_pcur)
                            ldc(cur[:sub, :], Z[rows, 8480:9504], r=[rZ], w=[r_cur])
                            for g in range(4):
                                for cc in range(2):
                                    ch = g * 2 + cc
                                    cols = slice(ch * 128, (ch + 1) * 128)
                                    if s == 0:
                                        pe(lambda g=g, ch=ch, cols=cols: T.matmul(pb[2 + ch // 4][:, (ch % 4) * 128:(ch % 4 + 1) * 128],
                                                                               lhsT=cur[:sub, cols], rhs=pm[:sub, g * 3 + 2, 0:sub],
                                                                               start=True, stop=True), r=[r_cur, r_pm], w=[r_pb[2 + ch // 4]])
                                    else:
                                        pe(lambda g=g, ch=ch, cols=cols: T.matmul(pb[2 + ch // 4][:, (ch % 4) * 128:(ch % 4 + 1) * 128],
                                                                               lhsT=cur[:sub, cols], rhs=pm[:sub, g * 3 + 0, 0:sub],
                                                                               start=True, stop=False), r=[r_cur, r_pm], w=[r_pb[2 + ch // 4]])
                                        pe(lambda g=g, ch=ch, cols=cols: T.matmul(pb[2 + ch // 4][:, (ch % 4) * 128:(ch % 4 + 1) * 128],
                                                                               lhsT=prv[:, cols], rhs=pm[:, g * 3 + 1, 0:sub],
                                                                               start=False, stop=True), r=[r_prv, r_pm], w=[r_pb[2 + ch // 4]])
                        else:
                            ldc(pext[0:15, :], PL0[l], w=[r_pext])
                            ldc(pext[15:15 + sub, :], Z[rows, 8480:9504], r=[rZ, r_pext], w=[r_pext])
                            for g in range(4):
                                for cc in range(2):
                                    ch = g * 2 + cc
                                    cols = slice(ch * 128, (ch + 1) * 128)
                                    pe(lambda g=g, ch=ch, cols=cols: T.matmul(pb[2 + ch // 4][:, (ch % 4) * 128:(ch % 4) * 128 + sub],
                                                                           lhsT=pext[0:23, cols], rhs=pms[0:23, g, 0:sub],
                                                                           start=True, stop=True), r=[r_pext, r_pms], w=[r_pb[2 + ch // 4]])
                        for hb in range(2):
                            dve(lambda hb=hb: V.tensor_copy(out=plT[:, hb * 4:(hb + 1) * 4, 0:sub],
                                                            in_=pb[2 + hb][:, :].rearrange("p (a b) -> p a b", b=128)[:, :, 0:sub]),
                                r=[r_pb[2 + hb]], w=[r_plT])
                        for g in range(4):
                            bank = g // 2
                            for cc in range(2):
                                pe(lambda g=g, cc=cc, bank=bank: T.matmul(pb[bank][:sub, (g % 2) * 256:(g % 2 + 1) * 256],
                                                                          lhsT=plT[:, g * 2 + cc, 0:sub], rhs=wp[:, g, cc, :],
                                                                          start=(cc == 0), stop=(cc == 1)), r=[r_plT, r_wp], w=[r_pb[bank]])
                        for bank in range(2):
                            dve(lambda bank=bank: V.tensor_tensor(out=od[:sub, bank * 512:(bank + 1) * 512], in0=pb[bank][:sub, :],
                                                                  in1=psc[:sub, bank * 512:(bank + 1) * 512], op=ALU.mult),
                                r=[r_pb[bank], r_psc], w=[r_od])
                        rms_rows((junk, r_junk, ss, r_ss), od[:sub, :], sub, gd[:sub, :], yo[:sub, :], r_od, r_gd, r_yo, 1024)
                        ld(MIX[rows, 3072:4096], yo[:sub, :], r=[r_yo], w=[rMIX])
            fw.barrier()

        def rwkv(l):
            CH = 64
            with ExitStack() as st:
                def bc(name, src):
                    t = sb(st, name, [128, 1024]); r = Res()
                    ld(t[:], src.partition_broadcast(128), w=[r])
                    return t, r
                mu = sb(st, "mu", [128, BFEAT]); r_mu = Res()
                ld(mu[:], Wd["mu_b"][l].partition_broadcast(128), w=[r_mu])
                w0b, r_w0b = bc("w0b", Wd["w0"][l])
                a0b, r_a0b = bc("a0b", Wd["a0"][l])
                kkb, r_kkb = bc("kkb", Wd["k_k"][l])
                kab, r_kab = bc("kab", Wd["k_a"][l])
                rkb, r_rkb = bc("rkb", Wd["r_k"][l])
                lwb, r_lwb = bc("lwb", Wd["ln_x_w"][l])
                lbb, r_lbb = bc("lbb", Wd["ln_x_b"][l])
                wup = sb(st, "wup", [64, 1024], BF16); r_wup = Res()
                aup = sb(st, "aup", [64, 1024], BF16); r_aup = Res()
                gup = sb(st, "gup", [128, 2, 1024], BF16); r_gup = Res()
                ldc(wup[:], Wd["w_up"][l], w=[r_wup])
                ldc(aup[:], Wd["a_up"][l], w=[r_aup])
                ldc(gup[:, 0, :], Wd["g_up"][l, 0:128, :], w=[r_gup])
                ldc(gup[0:32, 1, :], Wd["g_up"][l, 128:160, :], r=[r_gup], w=[r_gup])
                tri2 = sb(st, "tri2", [128, 128]); r_tri2 = Res()
                ones2 = sb(st, "ones2", [128, 128]); r_ones2 = Res()
                msk = sb(st, "msk", [128, 4, 64]); r_msk = Res()
                ld(tri2[:], CTRI2[:, :], w=[r_tri2])
                ld(ones2[:], CONES2[:, :], w=[r_ones2])
                ld(msk[:], CMSK.rearrange("p (a b) -> p a b", b=64), w=[r_msk])

                def mb(i):
                    return msk[:, i:i + 1, :].to_broadcast([128, 8, 64])
                valid = sb(st, "valid", [128, 1]); r_valid = Res()
                fbt = sb(st, "fbt", [128, BFEAT]); r_fb = Res()
                pvt = sb(st, "pvt", [128, BFEAT]); r_pv = Res()
                lz = sb(st, "lz", [128, 288], BF16); r_lz = Res()
                lzT = sb(st, "lzT", [128, 4, 128], BF16); r_lzT = Res()
                NA = 14
                arr = [sb(st, "ar%d" % i, [128, 1024]) for i in range(NA)]
                r_arr = [Res() for _ in range(NA)]
                sm = sb(st, "sm", [128, 64]); r_sm = Res()
                NF = 5
                fm = [sb(st, "fm%d" % i, [128, 8, 64]) for i in range(NF)]; r_fm = [Res() for _ in range(NF)]
                NM = 14
                mm = [sb(st, "mm%d" % i, [128, 8, 64]) for i in range(NM)]; r_mm = [Res() for _ in range(NM)]
                ST = sb(st, "ST", [128, 8, 64]); r_ST = Res()
                stmp = sb(st, "stmp", [64, 8, 128]); r_stmp = Res()

                def pbv(i):
                    return pb[i][:, :].rearrange("p (a b) -> p a b", b=64)

                def headmm(bank, terms, r, first=True, last=True):
                    n = len(terms)
                    for hp in range(8):
                        for hh in range(2):
                            R = slice(hh * 64, (hh + 1) * 64)
                            for ti, (lf, rf) in enumerate(terms):
                                is_last_inst = (hp == 7 and hh == 1 and ti == n - 1)
                                pe(lambda: T.matmul(pb[bank][R, hp * 64:(hp + 1) * 64], lhsT=lf(R, hp, hh), rhs=rf(R, hp, hh),
                                                    start=(first and ti == 0), stop=(last and ti == n - 1)),
                                   r=r, w=[r_pb[bank]], inc=is_last_inst)

                def f3(t):
                    return lambda R, hp, hh: t[R, hp, :]

                def tmcols(ap2d):
                    return lambda R, hp, hh: ap2d[R, hp * 128 + hh * 64:hp * 128 + hh * 64 + 64]

                for sq in SEQS:
                    r0, Tn, nprev, si = sq["r0"], sq["T"], sq["nprev"], sq["si"]
                    nch = (Tn + CH - 1) // CH
                    if si == 0:
                        dve(lambda: V.memset(ST[:], 0.0), w=[r_ST])
                    else:
                        for hh in range(2):
                            ld(stmp[:, :, hh * 64:(hh + 1) * 64],
                               WKV0[l].rearrange("(hp hh) v k -> hh v hp k", hh=2)[hh], r=[r_stmp], w=[r_stmp])
                        for hb in range(2):
                            for j in range(4):
                                pe(lambda: T.transpose(out=ptf[:, j, 0:64], in_=stmp[:, hb * 4 + j, :], identity=ident_f[0:64, 0:64]),
                                   r=[r_stmp, r_idf], w=[r_ptf], inc=(j == 3))
                            dve(lambda: V.tensor_copy(out=ST[:, hb * 4:(hb + 1) * 4, :], in_=ptf[:, :, 0:64]), r=[r_ptf], w=[r_ST])
                    for c in range(nch):
                        nv = min(CH, Tn - c * CH)
                        ta = r0 + c * CH
                        fcol = slice(3072, 3072 + BFEAT)
                        if nv < CH:
                            dve(lambda: V.memset(fbt[:], 0.0), w=[r_fb])
                            dve(lambda: V.memset(pvt[:], 0.0), w=[r_pv])
                        if nv < CH or c == 0:
                            dve(lambda: V.memset(valid[:], 0.0), w=[r_valid])
                            for d in range(2):
                                dve(lambda: V.memset(valid[d * 64:d * 64 + nv, :], 1.0), r=[r_valid], w=[r_valid])
                        for d in range(2):
                            P0 = d * 64
                            ld(fbt[P0:P0 + nv, :], Z[ta:ta + nv, fcol], r=[rZ, r_fb], w=[r_fb])
                            if c == 0:
                                if si == 0:
                                    dve(lambda: V.memset(pvt[P0:P0 + 1, :], 0.0), r=[r_pv], w=[r_pv])
                                else:
                                    ld(pvt[P0:P0 + 1, :], SH0[l], r=[r_pv], w=[r_pv])
                                if nv > 1:
                                    ld(pvt[P0 + 1:P0 + nv, :], Z[ta:ta + nv - 1, fcol], r=[rZ, r_pv], w=[r_pv])
                            else:
                                ld(pvt[P0:P0 + nv, :], Z[ta - 1:ta - 1 + nv, fcol], r=[rZ, r_pv], w=[r_pv])
                        dve(lambda: V.tensor_tensor(out=pvt[:, :], in0=pvt[:, :], in1=fbt[:, :], op=ALU.subtract), r=[r_pv, r_fb], w=[r_pv])
                        pool(lambda: G.tensor_tensor(out=pvt[:, :], in0=pvt[:, :], in1=mu[:, :], op=ALU.mult), r=[r_pv, r_mu], w=[r_pv])
                        dve(lambda: V.tensor_tensor(out=pvt[:, :], in0=pvt[:, :], in1=fbt[:, :], op=ALU.add), r=[r_pv, r_fb], w=[r_pv])
                        fs = pvt
                        R_, K_, V_ = fs[:, 0:1024], fs[:, 1024:2048], fs[:, 2048:3072]
                        act(lambda: A.activation(out=lz[:, 0:64], in_=fs[:, 3072:3136], func=AF.Tanh), r=[r_pv], w=[r_lz])
                        act(lambda: A.copy(out=lz[:, 64:128], in_=fs[:, 3136:3200]), r=[r_pv], w=[r_lz])
                        act(lambda: A.activation(out=lz[:, 128:288], in_=fs[:, 3200:3360], func=AF.Sigmoid), r=[r_pv], w=[r_lz])
                        for j, (c0_, cn) in enumerate([(0, 64), (64, 64), (128, 128), (256, 32)]):
                            pe(lambda: T.transpose(out=ptb[0:cn, j, :], in_=lz[:, c0_:c0_ + cn], identity=ident_b[:, :]),
                               r=[r_lz, r_idb], w=[r_ptb[0]], inc=(j == 3))
                        dve(lambda: V.tensor_copy(out=lzT[:, :, :], in_=ptb[:, 0:4, :]), r=[r_ptb[0]], w=[r_lzT])
                        for hb in range(2):
                            cs_ = slice(hb * 512, (hb + 1) * 512)
                            pe(lambda: T.matmul(pb[hb][:, :], lhsT=lzT[0:64, 0, :], rhs=wup[:, cs_], start=True, stop=True),
                               r=[r_lzT, r_wup], w=[r_pb[hb]])
                            pe(lambda: T.matmul(pb[2 + hb][:, :], lhsT=lzT[0:64, 1, :], rhs=aup[:, cs_], start=True, stop=True),
                               r=[r_lzT, r_aup], w=[r_pb[2 + hb]])
                            pe(lambda: T.matmul(pb[4 + hb][:, :], lhsT=lzT[:, 2, :], rhs=gup[:, 0, cs_], start=True, stop=False),
                               r=[r_lzT, r_gup], w=[r_pb[4 + hb]], inc=False)
                            pe(lambda: T.matmul(pb[4 + hb][:, :], lhsT=lzT[0:32, 3, :], rhs=gup[0:32, 1, cs_], start=False, stop=True),
                               r=[r_lzT, r_gup], w=[r_pb[4 + hb]])
                        (LW, At, Gt, KKt, Bt, KMt, BON, T1, T2, Lsb, Xa, Xb, Xc, Xd) = [a[:, :] for a in arr]
                        (rLW, rA, rG, rKK, rB, rKM, rBON, rT1, rT2, rL, rXa, rXb, rXc, rXd) = r_arr
                        for hb in range(2):
                            cs_ = slice(hb * 512, (hb + 1) * 512)
                            dve(lambda: V.tensor_tensor(out=arr[0][:, cs_], in0=pb[hb][:, :], in1=w0b[:, cs_], op=ALU.add),
                                r=[r_pb[hb], r_w0b], w=[rLW])
                            dve(lambda: V.tensor_tensor(out=arr[1][:, cs_], in0=pb[2 + hb][:, :], in1=a0b[:, cs_], op=ALU.add),
                                r=[r_pb[2 + hb], r_a0b], w=[rA])
                            act(lambda: A.copy(out=arr[2][:, cs_], in_=pb[4 + hb][:, :]), r=[r_pb[4 + hb]], w=[rG])
                        act(lambda: A.activation(out=LW, in_=LW, func=AF.Sigmoid), r=[rLW], w=[rLW])
                        act(lambda: A.activation(out=At, in_=At, func=AF.Sigmoid), r=[rA], w=[rA])
                        dve(lambda: V.tensor_scalar(out=LW, in0=LW, scalar1=valid[:, 0:1], scalar2=-float(np.exp(-0.5)), op0=ALU.mult, op1=ALU.mult),
                            r=[rLW, r_valid], w=[rLW])
                        pool(lambda: G.tensor_tensor(out=KKt, in0=K_, in1=kkb[:, :], op=ALU.mult), r=[r_pv, r_kkb], w=[rKK])
                        dve(lambda: V.tensor_tensor(out=T1, in0=KKt, in1=KKt, op=ALU.mult), r=[rKK], w=[rT1])
                        dve(lambda: V.tensor_reduce(out=sm[:, 0:16], in_=T1.rearrange("p (h d) -> p h d", d=64), axis=AX.X, op=ALU.add), r=[rT1], w=[r_sm])
                        act(lambda: A.activation(out=sm[:, 16:32], in_=sm[:, 0:16], func=AF.Sqrt), r=[r_sm], w=[r_sm])
                        dve(lambda: V.tensor_scalar(out=sm[:, 16:32], in0=sm[:, 16:32], scalar1=1e-12, scalar2=None, op0=ALU.max), r=[r_sm], w=[r_sm])
                        dve(lambda: V.reciprocal(out=sm[:, 32:48], in_=sm[:, 16:32]), r=[r_sm], w=[r_sm])
                        dve(lambda: V.tensor_tensor(out=KKt.rearrange("p (h d) -> p h d", d=64), in0=KKt.rearrange("p (h d) -> p h d", d=64),
                                                    in1=sm[:, 32:48].unsqueeze(2).to_broadcast([128, 16, 64]), op=ALU.mult), r=[rKK, r_sm], w=[rKK])
                        pool(lambda: G.tensor_tensor(out=Bt, in0=KKt, in1=At, op=ALU.mult), r=[rKK, rA], w=[rB])
                        dve(lambda: V.scalar_tensor_tensor(out=T1, in0=At, scalar=-1.0, in1=kab[:, :], op0=ALU.add, op1=ALU.mult), r=[rA, r_kab], w=[rT1])
                        dve(lambda: V.scalar_tensor_tensor(out=KMt, in0=T1, scalar=1.0, in1=K_, op0=ALU.add, op1=ALU.mult), r=[rT1, r_pv], w=[rKM])
                        pool(lambda: G.tensor_tensor(out=T1, in0=R_, in1=KMt, op=ALU.mult), r=[r_pv, rKM], w=[rT1])
                        pool(lambda: G.tensor_tensor(out=T1, in0=T1, in1=rkb[:, :], op=ALU.mult), r=[rT1, r_rkb], w=[rT1])
                        dve(lambda: V.tensor_reduce(out=sm[:, 48:64], in_=T1.rearrange("p (h d) -> p h d", d=64), axis=AX.X, op=ALU.add), r=[rT1], w=[r_sm])
                        pool(lambda: G.tensor_tensor(out=BON.rearrange("p (h d) -> p h d", d=64), in0=V_.rearrange("p (h d) -> p h d", d=64),
                                                     in1=sm[:, 48:64].unsqueeze(2).to_broadcast([128, 16, 64]), op=ALU.mult), r=[r_pv, r_sm], w=[rBON])
                        for hb in range(2):
                            cs_ = slice(hb * 512, (hb + 1) * 512)
                            pe(lambda: T.matmul(pb[hb][:, :], lhsT=tri2[:, :], rhs=arr[0][:, cs_], start=True, stop=True), r=[r_tri2, rLW], w=[r_pb[hb]])
                            pe(lambda: T.matmul(pb[2 + hb][:, :], lhsT=ones2[:, :], rhs=arr[0][:, cs_], start=True, stop=True), r=[r_ones2, rLW], w=[r_pb[2 + hb]])
                        for hb in range(2):
                            cs_ = slice(hb * 512, (hb + 1) * 512)
                            act(lambda: A.copy(out=arr[9][:, cs_], in_=pb[hb][:, :]), r=[r_pb[hb]], w=[rL])
                            dve(lambda: V.tensor_tensor(out=arr[8][:, cs_], in0=pb[2 + hb][:, :], in1=arr[9][:, cs_], op=ALU.subtract), r=[r_pb[2 + hb], rL], w=[rT2])
                        act(lambda: A.activation(out=T2, in_=T2, func=AF.Exp), r=[rT2], w=[rT2])
                        dve(lambda: V.tensor_tensor(out=T1, in0=Lsb, in1=LW, op=ALU.subtract), r=[rL, rLW], w=[rT1])
                        act(lambda: A.activation(out=T1, in_=T1, func=AF.Exp), r=[rT1], w=[rT1])
                        pool(lambda: G.tensor_tensor(out=Xa, in0=KKt, in1=T1, op=ALU.mult), r=[rKK, rT1], w=[rXa])
                        dve(lambda: V.tensor_tensor(out=KKt, in0=Bt, in1=T2, op=ALU.mult), r=[rB, rT2], w=[rKK])
                        pool(lambda: G.tensor_tensor(out=LW, in0=KMt, in1=T2, op=ALU.mult), r=[rKM, rT2], w=[rLW])
                        BH, KH, rBH, rKH = KKt, LW, rKK, rLW
                        act(lambda: A.activation(out=Xd, in_=Lsb, func=AF.Exp), r=[rL], w=[rXd])
                        act(lambda: A.activation(out=T1, in_=Lsb, func=AF.Exp, scale=-1.0), r=[rL], w=[rT1])
                        dve(lambda: V.tensor_tensor(out=Xb, in0=Bt, in1=T1, op=ALU.mult), r=[rB, rT1], w=[rXb])
                        pool(lambda: G.tensor_tensor(out=Xc, in0=KMt, in1=T1, op=ALU.mult), r=[rKM, rT1], w=[rXc])
                        dve(lambda: V.tensor_tensor(out=T2, in0=R_, in1=Xd, op=ALU.mult), r=[r_pv, rXd], w=[rT2])
                        for fi, (src_, rs_) in enumerate([(Xa, rXa), (Xb, rXb), (Xc, rXc), (T2, rT2), (Xd, rXd)]):
                            for hb in range(2):
                                for j in range(4):
                                    hp = hb * 4 + j
                                    pe(lambda: T.transpose(out=ptf[:, j, :], in_=src_[:, hp * 128:(hp + 1) * 128], identity=ident_f[:, :]),
                                       r=[rs_, r_idf], w=[r_ptf], inc=(j == 3))
                                if (fi + hb) % 2 == 0:
                                    dve(lambda: V.tensor_copy(out=fm[fi][:, hb * 4:(hb + 1) * 4, :], in_=ptf[:, :, 0:64]), r=[r_ptf], w=[r_fm[fi]])
                                else:
                                    act(lambda: A.copy(out=fm[fi][:, hb * 4:(hb + 1) * 4, :], in_=ptf[:, :, 0:64]), r=[r_ptf], w=[r_fm[fi]])
                        Af, Bf, Kf, Rf, Ef = fm
                        rAf, rBf, rKf, rRf, rEf = r_fm
                        headmm(0, [(f3(Bf), f3(Af))], [rBf, rAf])
                        headmm(1, [(f3(Af), f3(Bf))], [rBf, rAf])
                        headmm(2, [(f3(Kf), f3(Af))], [rKf, rAf])
                        headmm(3, [(f3(Bf), f3(Rf))], [rBf, rRf])
                        headmm(4, [(f3(Kf), f3(Rf))], [rKf, rRf])
                        E = [mm[0], mm[1]]; ET = [mm[2], mm[3]]; rE = [r_mm[0], r_mm[1]]; rET = [r_mm[2], r_mm[3]]
                        Tm, TTm, Mm, Np, Mp, XT, nU, osb = mm[4], mm[5], mm[6], mm[7], mm[8], mm[9], mm[10], mm[11]
                        rTm, rTTm, rMm, rNp, rMp, rXT, rnU, rosb = (r_mm[i] for i in range(4, 12))
                        dve(lambda: V.scalar_tensor_tensor(out=E[0][:], in0=pbv(0), scalar=-1.0, in1=mb(0), op0=ALU.mult, op1=ALU.mult),
                            r=[r_pb[0], r_msk], w=[rE[0]])
                        dve(lambda: V.scalar_tensor_tensor(out=ET[0][:], in0=pbv(1), scalar=-1.0, in1=mb(2), op0=ALU.mult, op1=ALU.mult),
                            r=[r_pb[1], r_msk], w=[rET[0]])
                        pool(lambda: G.tensor_tensor(out=Tm[:], in0=E[0][:], in1=mb(3), op=ALU.add), r=[rE[0], r_msk], w=[rTm])
                        pool(lambda: G.tensor_tensor(out=TTm[:], in0=ET[0][:], in1=mb(3), op=ALU.add), r=[rET[0], r_msk], w=[rTTm])
                        dve(lambda: V.tensor_tensor(out=Mm[:], in0=pbv(2), in1=mb(0), op=ALU.mult), r=[r_pb[2], r_msk], w=[rMm])
                        dve(lambda: V.tensor_tensor(out=Np[:], in0=pbv(3), in1=mb(1), op=ALU.mult), r=[r_pb[3], r_msk], w=[rNp])
                        dve(lambda: V.tensor_tensor(out=Mp[:], in0=pbv(4), in1=mb(1), op=ALU.mult), r=[r_pb[4], r_msk], w=[rMp])
                        cur = 0
                        for lev in range(5):
                            nxt = 1 - cur
                            lastlev = (lev == 4)
                            headmm(0, [(f3(ET[cur]), f3(E[cur]))], [rET[cur], rE[cur]])
                            if not lastlev:
                                headmm(1, [(f3(E[cur]), f3(ET[cur]))], [rET[cur], rE[cur]])
                            act(lambda: A.copy(out=E[nxt][:], in_=pbv(0)), r=[r_pb[0]], w=[rE[nxt]])
                            if not lastlev:
                                dve(lambda: V.tensor_copy(out=ET[nxt][:], in_=pbv(1)), r=[r_pb[1]], w=[rET[nxt]])
                            headmm(2, [(f3(TTm), f3(E[nxt]))], [rTTm, rE[nxt]])
                            if not lastlev:
                                headmm(3, [(f3(E[nxt]), f3(TTm))], [rTTm, rE[nxt]])
                            dve(lambda: V.tensor_tensor(out=Tm[:], in0=pbv(2), in1=Tm[:], op=ALU.add), r=[r_pb[2], rTm], w=[rTm])
                            if not lastlev:
                                dve(lambda: V.tensor_tensor(out=TTm[:], in0=pbv(3), in1=TTm[:], op=ALU.add), r=[r_pb[3], rTTm], w=[rTTm])
                            cur = nxt
                        vcols = tmcols(V_)
                        headmm(0, [(f3(Af), f3(ST)), (f3(Mm), vcols)], [rAf, r_ST, rMm, r_pv])
                        act(lambda: A.copy(out=XT[:], in_=pbv(0)), r=[r_pb[0]], w=[rXT])
                        headmm(1, [(f3(Tm), f3(XT))], [rTm, rXT])
                        dve(lambda: V.tensor_scalar(out=nU[:], in0=pbv(1), scalar1=-1.0, scalar2=None, op0=ALU.mult), r=[r_pb[1]], w=[rnU])
                        headmm(2, [(f3(Rf), f3(ST)), (f3(Np), f3(nU)), (f3(Mp), vcols)], [rRf, r_ST, rNp, rnU, rMp, r_pv])
                        headmm(3, [(tmcols(BH), f3(nU)), (tmcols(KH), vcols)], [rBH, rnU, rKH, r_pv])
                        act(lambda: A.copy(out=osb[:], in_=pbv(2)), r=[r_pb[2]], w=[rosb])
                        dve(lambda: V.tensor_tensor(out=ST[:], in0=ST[:], in1=Ef[:, :, 63:64].to_broadcast([128, 8, 64]), op=ALU.mult),
                            r=[r_ST, rEf], w=[r_ST])
                        dve(lambda: V.tensor_tensor(out=ST[:], in0=pbv(3), in1=ST[:], op=ALU.add), r=[r_pb[3], r_ST], w=[r_ST])
                        sel = [mm[12], mm[13], XT]
                        rsel = [r_mm[12], r_mm[13], rXT]
                        w2 = [Tm, TTm]
                        rw2 = [rTm, rTTm]
                        for d in range(2):
                            P_ = slice(d * 64, (d + 1) * 64)
                            def hv(ap2d):
                                return ap2d[P_, :].rearrange("p (hp hh v) -> p hp hh v", hh=2, v=64)[:, :, d, :]
                            pool(lambda: G.tensor_copy(out=sel[0][P_, :, :], in_=hv(BON)), r=[rBON], w=[rsel[0]])
                            pool(lambda: G.tensor_copy(out=sel[1][P_, :, :], in_=hv(Gt)), r=[rG], w=[rsel[1]])
                            pool(lambda: G.tensor_copy(out=w2[0][P_, :, :], in_=hv(lwb[:, :])), r=[r_lwb], w=[rw2[0]])
                            pool(lambda: G.tensor_copy(out=w2[1][P_, :, :], in_=hv(lbb[:, :])), r=[r_lbb], w=[rw2[1]])
                        o3 = osb[:]
                        sq3 = nU[:]
                        dve(lambda: V.tensor_reduce(out=sm[:, 0:8], in_=o3, axis=AX.X, op=ALU.add), r=[rosb], w=[r_sm])
                        dve(lambda: V.tensor_scalar(out=sm[:, 0:8], in0=sm[:, 0:8], scalar1=1.0 / 64, scalar2=None, op0=ALU.mult), r=[r_sm], w=[r_sm])
                        dve(lambda: V.tensor_tensor(out=o3, in0=o3, in1=sm[:, 0:8].unsqueeze(2).to_broadcast([128, 8, 64]), op=ALU.subtract),
                            r=[rosb, r_sm], w=[rosb])
                        dve(lambda: V.tensor_tensor(out=sq3, in0=o3, in1=o3, op=ALU.mult), r=[rosb], w=[rnU])
                        dve(lambda: V.tensor_reduce(out=sm[:, 8:16], in_=sq3, axis=AX.X, op=ALU.add), r=[rnU], w=[r_sm])
                        dve(lambda: V.tensor_scalar(out=sm[:, 8:16], in0=sm[:, 8:16], scalar1=1.0 / 64, scalar2=64e-5, op0=ALU.mult, op1=ALU.add),
                            r=[r_sm], w=[r_sm])
                        act(lambda: A.activation(out=sm[:, 8:16], in_=sm[:, 8:16], func=AF.Sqrt), r=[r_sm], w=[r_sm])
                        dve(lambda: V.reciprocal(out=sm[:, 16:24], in_=sm[:, 8:16]), r=[r_sm], w=[r_sm])
                        dve(lambda: V.tensor_tensor(out=o3, in0=o3, in1=sm[:, 16:24].unsqueeze(2).to_broadcast([128, 8, 64]), op=ALU.mult),
                            r=[rosb, r_sm], w=[rosb])
                        dve(lambda: V.tensor_tensor(out=o3, in0=o3, in1=w2[0][:], op=ALU.mult), r=[rosb, rw2[0]], w=[rosb])
                        dve(lambda: V.tensor_tensor(out=o3, in0=o3, in1=w2[1][:], op=ALU.add), r=[rosb, rw2[1]], w=[rosb])
                        dve(lambda: V.tensor_tensor(out=o3, in0=o3, in1=sel[0][:], op=ALU.add), r=[rosb, rsel[0]], w=[rosb])
                        dve(lambda: V.tensor_tensor(out=o3, in0=o3, in1=sel[1][:], op=ALU.mult), r=[rosb, rsel[1]], w=[rosb])
                        for d in range(2):
                            dst = MIX[ta:ta + nv, 1024:2048].rearrange("t (hp hh v) -> t hp hh v", hh=2, v=64)[:, :, d, :]
                            ld(dst, osb[d * 64:d * 64 + nv, :, :], r=[rosb], w=[rMIX])
                    for hb in range(2):
                        for j in range(4):
                            pe(lambda: T.transpose(out=ptf[0:64, j, :], in_=ST[:, hb * 4 + j, :], identity=ident_f[:, :]),
                               r=[r_ST, r_idf], w=[r_ptf], inc=(j == 3))
                        dve(lambda: V.tensor_copy(out=stmp[:, hb * 4:(hb + 1) * 4, :], in_=ptf[0:64, :, :]), r=[r_ptf, r_stmp], w=[r_stmp])
                    for hh in range(2):
                        ld(WKVO[l, si].rearrange("(hp hh) v k -> hh v hp k", hh=2)[hh],
                           stmp[:, :, hh * 64:(hh + 1) * 64], r=[r_stmp], w=[rOUT])
            fw.barrier()

        src, rsrc = X, rX
        outs = [(RA, rRA), (RB, rRB)]
        for l in range(2):
            dst, rdst = outs[l]
            with ExitStack() as st:
                token_local(st, [job_ffn(l, 1, src, rsrc, R1, rR1), job_win(l, R1, rR1)])
            fw.barrier()
            attention(l)
            rwkv(l)
            gmlp_pool(l)
            with ExitStack() as st:
                token_local(st, [job_wout(l, R1, rR1, R2, rR2), job_ffn(l, 2, R2, rR2, dst, rdst)])
            fw.barrier()
            src, rsrc = dst, rdst
        final_norm(src, rsrc)
        fw.finish()
    return nc


_NC_CACHE = {}


def kernel(**inputs):
    inp = {k: np.ascontiguousarray(np.asarray(v, dtype=np.float32)) for k, v in inputs.items()}
    consts = _host_consts()
    if "nc" not in _NC_CACHE:
        _NC_CACHE["nc"] = build_nc()
    nc = _NC_CACHE["nc"]
    shared = {}
    for n, shp in WNAMES:
        shared[n] = inp[n].reshape(shp)
    shared.update(consts)
    in_maps = []
    for c in range(8):
        m = dict(shared)
        m["x_all"] = np.concatenate([inp["x_prompt"][c % 4], inp["x_sample"][c]], axis=0)
        m["cache_k"] = inp["cache_k_swa"][:, c].reshape(2, NPREV, 1024)
        m["cache_v"] = inp["cache_v_swa"][:, c].reshape(2, NPREV, 1024)
        m["wkv0"] = inp["state_rwkv_wkv"][:, c]
        m["shift0"] = inp["state_rwkv_shift"][:, c].reshape(2, 1, BFEAT)
        m["pool0"] = inp["state_pool"][:, c]
        in_maps.append({k: np.ascontiguousarray(v) for k, v in m.items()})
    res = run_bass_kernel_spmd(nc, in_maps, core_ids=list(range(8)))
    R = res.results
    y_p = np.stack([R[b]["y"][:TP] for b in range(4)])
    y_s = np.stack([R[c]["y"][TP:] for c in range(8)])
    nk_p = np.stack([R[b]["newk"][:, :TP] for b in range(4)], axis=1).reshape(2, 4, TP, 8, 128)
    nv_p = np.stack([R[b]["newv"][:, :TP] for b in range(4)], axis=1).reshape(2, 4, TP, 8, 128)
    wkv_p = np.stack([R[b]["wkvo"][:, 0] for b in range(4)], axis=1)
    sh_p = np.stack([R[b]["sho"][:, 0] for b in range(4)], axis=1)
    pl_p = np.stack([R[b]["plo"][:, 0] for b in range(4)], axis=1)
    nk_s = np.stack([R[c]["newk"][:, TP:] for c in range(8)], axis=1).reshape(2, 8, TS, 8, 128)
    nv_s = np.stack([R[c]["newv"][:, TP:] for c in range(8)], axis=1).reshape(2, 8, TS, 8, 128)
    wkv_s = np.stack([R[c]["wkvo"][:, 1] for c in range(8)], axis=1)
    sh_s = np.stack([R[c]["sho"][:, 1] for c in range(8)], axis=1)
    pl_s = np.stack([R[c]["plo"][:, 1] for c in range(8)], axis=1)
    gv_s = np.stack([R[c]["gvo"] for c in range(8)], axis=1)
    outs = (y_p, y_s, nk_p, nv_p, wkv_p, sh_p, pl_p, nk_s, nv_s, wkv_s, sh_s, pl_s, gv_s)
    return tuple(np.ascontiguousarray(o.astype(np.float32)) for o in outs)
```

```python
import numpy as np
from contextlib import ExitStack
import concourse.bass as bass
import concourse.mybir as mybir
from concourse.bass_utils import run_bass_kernel_spmd

F32 = mybir.dt.float32
BF16 = mybir.dt.bfloat16
AF = mybir.ActivationFunctionType
ALU = mybir.AluOpType
AX = mybir.AxisListType

D = 4096
DFF = 11008
NJ = DFF // 128
PROJ = 9504
TP = 2048
TS = 8
TALL = TP + TS
NPREV = 2048
BFEAT = 3360
MASKW = 3200
C0 = 512
EPS = 1e-6


class Res:
    __slots__ = ("w", "r")

    def __init__(self):
        self.w = None
        self.r = {}


class Eng:
    def __init__(self, fw, key, eng, compute=True):
        self.key = key
        self.e = eng
        self.sem = fw.new_sem("p_" + key) if compute else None
        self.cnt = 0
        self.waited = {}


class FW:
    def __init__(self, nc, stack, n_dma_sems=20):
        self.nc = nc
        self.stack = stack
        self.pe = Eng(self, "pe", nc.tensor)
        self.dve = Eng(self, "dve", nc.vector)
        self.act = Eng(self, "act", nc.scalar)
        self.pool = Eng(self, "pool", nc.gpsimd)
        self.sp = Eng(self, "sp", nc.sync, compute=False)
        self.engs = [self.pe, self.dve, self.act, self.pool, self.sp]
        self.dring = {}
        for q in (self.sp, self.pool):
            self.dring[q.key] = [[self.new_sem("d_%s_%d" % (q.key, i)), 0] for i in range(n_dma_sems)]
        self.dpos = {"sp": 0, "pool": 0}

    def new_sem(self, name):
        return self.stack.enter_context(self.nc.semaphore(name))

    def _wait(self, E, tok):
        if tok is None:
            return
        sem, val, key = tok
        if key == "pe" and E.key == "pe":
            return
        k = id(sem)
        if E.waited.get(k, 0) >= val:
            return
        E.e.wait_ge(sem, val)
        E.waited[k] = val

    def _deps(self, E, reads, writes):
        for r in reads:
            self._wait(E, r.w)
        for w in writes:
            self._wait(E, w.w)
            for t in w.r.values():
                self._wait(E, t)

    def _commit(self, tok, reads, writes):
        for r in reads:
            r.r[id(tok[0])] = tok
        for w in writes:
            w.w = tok
            w.r = {}

    def op(self, E, fn, reads=(), writes=(), inc=True):
        self._deps(E, reads, writes)
        ins = fn()
        if inc:
            E.cnt += 1
            ins.then_inc(E.sem, 1)
            tok = (E.sem, E.cnt, E.key)
        else:
            tok = (E.sem, E.cnt + 1, E.key)
        self._commit(tok, reads, writes)
        return tok

    def dma(self, Q, out, in_, reads=(), writes=()):
        self._deps(Q, reads, writes)
        ring = self.dring[Q.key]
        pos = self.dpos[Q.key]
        self.dpos[Q.key] = (pos + 1) % len(ring)
        ent = ring[pos]
        if ent[1] > 0:
            self._wait(Q, (ent[0], ent[1], "dma"))
        ins = Q.e.dma_start(out=out, in_=in_)
        ent[1] += 16
        ins.then_inc(ent[0], 16)
        tok = (ent[0], ent[1], "dma")
        self._commit(tok, reads, writes)
        return tok

    def all_tokens(self):
        toks = []
        for q in self.dring.values():
            for ent in q:
                if ent[1] > 0:
                    toks.append((ent[0], ent[1], "dma"))
        for E in (self.pe, self.dve, self.act, self.pool):
            if E.cnt > 0:
                toks.append((E.sem, E.cnt, E.key))
        return toks

    def barrier(self):
        toks = self.all_tokens()
        for E in self.engs:
            for t in toks:
                self._wait(E, t)

    def finish(self):
        for t in self.all_tokens():
            self._wait(self.sp, t)


def _host_consts():
    half = 64
    inv = (10000.0 ** (-np.arange(half, dtype=np.float32) / half)).astype(np.float32)
    pos = np.concatenate([np.arange(TP), 16384 + np.arange(TS)]).astype(np.float32)
    ang = pos[:, None] * inv[None, :]
    cs = np.concatenate([np.cos(ang), np.sin(ang)], axis=1).astype(np.float32)
    d = np.arange(MASKW)[None, :] - np.arange(128)[:, None] - C0
    m = ((d >= 0) & (d <= 128)).astype(np.float32) + ((d >= 0) & (d <= 512) & (d % 4 == 0)) + \
        ((d >= 0) & (d <= 2048) & (d % 16 == 0))
    mt = m.astype(np.float32)
    pm = np.zeros((128, 4, 3, 128), np.float32)
    pms = np.zeros((32, 4, 8), np.float32)
    tl = np.arange(128)
    for g, win in enumerate((2, 4, 8, 16)):
        for t in range(128):
            for tp in range(t - win + 1, t + 1):
                if tp >= 0:
                    pm[tp, g, 0, t] += 1.0 / win
                    pm[tp, g, 2, t] += 1.0 / min(win, t + 1)
                else:
                    pm[128 + tp, g, 1, t] += 1.0 / win
            pm[t, g, 0, t] -= 1.0
            pm[t, g, 2, t] -= 1.0
        for t in range(8):
            for e in range(15 + t - win + 1, 15 + t + 1):
                pms[e, g, t] += 1.0 / win
            pms[15 + t, g, t] -= 1.0
    tril = np.tril(np.ones((128, 128), np.float32))
    tri2 = np.zeros((128, 128), np.float32)
    ones2 = np.zeros((128, 128), np.float32)
    mskc = np.zeros((128, 4, 64), np.float32)
    ii = np.arange(64)
    for d_ in range(2):
        sl = slice(d_ * 64, (d_ + 1) * 64)
        tri2[sl, sl] = (ii[:, None] <= ii[None, :])
        ones2[sl, sl] = 1.0
        mskc[sl, 0] = (ii[:, None] < ii[None, :])
        mskc[sl, 1] = (ii[:, None] <= ii[None, :])
        mskc[sl, 2] = (ii[:, None] > ii[None, :])
        mskc[sl, 3] = (ii[:, None] == ii[None, :])
    return dict(c_cs=cs, c_mt=mt, c_pm=pm.reshape(128, 4 * 3 * 128), c_pms=pms.reshape(32, 32), c_tril=tril,
                c_tri2=tri2, c_ones2=ones2, c_msk=mskc.reshape(128, 256))


WNAMES = [("ln_ffn1", [2, 32, 128]), ("w1_gate", [2, D, DFF]), ("w1_up", [2, D, DFF]), ("w1_down", [2, DFF, D]),
          ("ln_mix", [2, 32, 128]), ("w_in", [2, D, PROJ]), ("g_out_a", [2, 1, 1024]), ("mu_b", [2, 1, BFEAT]),
          ("w0", [2, 1, 1024]), ("w_up", [2, 64, 1024]), ("a0", [2, 1, 1024]), ("a_up", [2, 64, 1024]),
          ("g_up", [2, 160, 1024]), ("k_k", [2, 1, 1024]), ("k_a", [2, 1, 1024]), ("r_k", [2, 1, 1024]),
          ("ln_x_w", [2, 1, 1024]), ("ln_x_b", [2, 1, 1024]), ("w_s", [2, 8, 128, 128]), ("b_s", [2, 8, 128]),
          ("g_out_c", [2, 1, 1024]), ("w_pool", [2, 4, 256, 256]), ("pool_scale", [2, 1, 1024]),
          ("g_out_d", [2, 1, 1024]), ("w_out", [2, D, D]), ("ln_ffn2", [2, 32, 128]), ("w2_gate", [2, D, DFF]),
          ("w2_up", [2, D, DFF]), ("w2_down", [2, DFF, D]), ("ln_final", [32, 128])]


def build_nc():
    nc = bass.Bass("TRN2", target_bir_lowering=False)

    def din(name, shape):
        return nc.dram_tensor(name, list(shape), F32, kind="ExternalInput").ap()

    def dout(name, shape):
        return nc.dram_tensor(name, list(shape), F32, kind="ExternalOutput").ap()

    def dscr(name, shape):
        return nc.dram_tensor(name, list(shape), F32, kind="Internal").ap()

    X = din("x_all", [TALL, D])
    CK = din("cache_k", [2, NPREV, 1024])
    CV = din("cache_v", [2, NPREV, 1024])
    WKV0 = din("wkv0", [2, 16, 64, 64])
    SH0 = din("shift0", [2, 1, BFEAT])
    PL0 = din("pool0", [2, 15, 1024])
    Wd = {n: din(n, s) for n, s in WNAMES}
    CCS = din("c_cs", [TALL, 128])
    CMT = din("c_mt", [128, MASKW])
    CPM = din("c_pm", [128, 4 * 3 * 128])
    CPMS = din("c_pms", [32, 32])
    CTRIL = din("c_tril", [128, 128])
    CTRI2 = din("c_tri2", [128, 128])
    CONES2 = din("c_ones2", [128, 128])
    CMSK = din("c_msk", [128, 256])

    Y = dout("y", [TALL, D])
    NEWK = dout("newk", [2, TALL, 1024])
    NEWV = dout("newv", [2, TALL, 1024])
    WKVO = dout("wkvo", [2, 2, 16, 64, 64])
    SHO = dout("sho", [2, 2, BFEAT])
    PLO = dout("plo", [2, 2, 15, 1024])
    GVO = dout("gvo", [2, TS, 1024])

    R1 = dscr("r1", [TALL, D])
    R2 = dscr("r2", [TALL, D])
    RA = dscr("ra", [TALL, D])
    RB = dscr("rb", [TALL, D])
    Z = dscr("z", [TALL, PROJ])
    MIX = dscr("mix", [TALL, D])
    XB = dscr("xb", [TALL, 2, 5, 512])
    OT = dscr("ot", [TALL, 1024])

    rX, rR1, rR2, rRA, rRB, rZ, rMIX, rXB, rOT, rOUT = (Res() for _ in range(10))

    top = ExitStack()
    with top:
        fw = FW(nc, top)
        PE, DVE, ACT, POOL, SP = fw.pe, fw.dve, fw.act, fw.pool, fw.sp
        T, V, A, G = nc.tensor, nc.vector, nc.scalar, nc.gpsimd

        _uid = [0]

        def sb(st, name, shape, dt=F32):
            _uid[0] += 1
            return st.enter_context(nc.sbuf_tensor("%s_%d" % (name, _uid[0]), list(shape), dt))

        def pe(fn, r=(), w=(), inc=True):
            return fw.op(PE, fn, r, w, inc)

        def dve(fn, r=(), w=()):
            return fw.op(DVE, fn, r, w)

        def act(fn, r=(), w=()):
            return fw.op(ACT, fn, r, w)

        def pool(fn, r=(), w=()):
            return fw.op(POOL, fn, r, w)

        def ld(out, in_, r=(), w=()):
            return fw.dma(SP, out, in_, r, w)

        def ldc(out, in_, r=(), w=()):
            return fw.dma(POOL, out, in_, r, w)

        ident_f = sb(top, "ident_f", [128, 128]); r_idf = Res()
        ident_b = sb(top, "ident_b", [128, 128], BF16); r_idb = Res()
        pool(lambda: G.memset(ident_f[:], 0.0), w=[r_idf])
        pool(lambda: G.affine_select(out=ident_f[:], in_=ident_f[:], pattern=[[-1, 128]], compare_op=ALU.not_equal,
                                     fill=1.0, base=0, channel_multiplier=1), r=[r_idf], w=[r_idf])
        dve(lambda: V.tensor_copy(out=ident_b[:], in_=ident_f[:]), r=[r_idf], w=[r_idb])

        NPB = 6
        pb = [top.enter_context(nc.psum_tensor("pb%d" % i, [128, 512], F32)) for i in range(NPB)]
        r_pb = [Res() for _ in range(NPB)]
        ptb = top.enter_context(nc.psum_tensor("ptb", [128, 8, 128], BF16)); r_ptb = [Res(), Res()]
        ptf = top.enter_context(nc.psum_tensor("ptf", [128, 4, 128], F32)); r_ptf = Res()

        def token_local(st, jobs):
            Xn = sb(st, "Xn", [128, 32, 768], BF16); r_Xn = Res()
            H = sb(st, "H", [128, 44, 768], BF16); r_H = Res()
            NW = 3
            wbuf = [sb(st, "wb%d" % i, [128, 5632], BF16) for i in range(NW)]
            r_wb = [Res() for _ in range(NW)]
            xst = sb(st, "xst", [128, D]); r_xst = Res()
            xnb = sb(st, "xnb", [128, D], BF16); r_xnb = Res()
            ss = sb(st, "ss", [128, 8]); r_ss = Res()
            gcol = sb(st, "gcol", [128, 32]); r_gcol = Res()
            graw = sb(st, "graw", [32, 128]); r_graw = Res()
            sg = [sb(st, "sg%d" % i, [128, 2, 768]) for i in range(2)]; r_sg = [Res(), Res()]
            rsd = [sb(st, "rsd%d" % i, [128, 512]) for i in range(2)]; r_rsd = [Res(), Res()]
            yo = [sb(st, "yo%d" % i, [128, 512]) for i in range(2)]; r_yo = [Res(), Res()]
            cnt = {"w": 0, "sg": 0, "rsd": 0, "yo": 0, "pb": 0}

            def load_gcol(gsrc):
                ld(graw[:], gsrc, w=[r_graw])
                pe(lambda: T.transpose(out=ptf[:, 0, 0:32], in_=graw[:, :], identity=ident_f[0:32, 0:32]),
                   r=[r_graw, r_idf], w=[r_ptf])
                dve(lambda: V.tensor_copy(out=gcol[:], in_=ptf[:, 0, 0:32]), r=[r_ptf], w=[r_gcol])

            def load_norm(src, rsrc, t0, TT, norm=True):
                nsub = (TT + 127) // 128
                for s in range(nsub):
                    sub = min(128, TT - s * 128)
                    rows = slice(t0 + s * 128, t0 + s * 128 + sub)
                    if norm:
                        ld(xst[:sub, :], src[rows, :], r=[rsrc], w=[r_xst])
                        act(lambda: A.activation(out=xnb[:sub, :], in_=xst[:sub, :], func=AF.Square,
                                                 accum_out=ss[:sub, 0:1]), r=[r_xst], w=[r_xnb, r_ss])
                        dve(lambda: V.tensor_scalar(out=ss[:sub, 1:2], in0=ss[:sub, 0:1], scalar1=1.0 / D, scalar2=EPS,
                                                    op0=ALU.mult, op1=ALU.add), r=[r_ss], w=[r_ss])
                        act(lambda: A.activation(out=ss[:sub, 2:3], in_=ss[:sub, 1:2], func=AF.Sqrt), r=[r_ss], w=[r_ss])
                        dve(lambda: V.reciprocal(out=ss[:sub, 3:4], in_=ss[:sub, 2:3]), r=[r_ss], w=[r_ss])
                        act(lambda: A.activation(out=xnb[:sub, :], in_=xst[:sub, :], func=AF.Copy, scale=ss[:sub, 3:4]),
                            r=[r_xst, r_ss], w=[r_xnb])
                    else:
                        ldc(xnb[:sub, :], src[rows, :], r=[rsrc], w=[r_xnb])
                    for c8 in range(4):
                        hb = c8 % 2
                        for j in range(8):
                            c = c8 * 8 + j
                            pe(lambda c=c, j=j: T.transpose(out=ptb[:, j, 0:sub], in_=xnb[:sub, c * 128:(c + 1) * 128],
                                                            identity=ident_b[0:sub, 0:sub]),
                               r=[r_xnb, r_idb], w=[r_ptb[0]], inc=(j == 7))
                        for j in range(8):
                            c = c8 * 8 + j
                            if norm:
                                dve(lambda c=c, j=j: V.tensor_scalar(out=Xn[:, c, s * 128:s * 128 + sub], in0=ptb[:, j, 0:sub],
                                                                     scalar1=gcol[:, c:c + 1], scalar2=None, op0=ALU.mult),
                                    r=[r_ptb[0], r_gcol], w=[r_Xn])
                            else:
                                dve(lambda c=c, j=j: V.tensor_copy(out=Xn[:, c, s * 128:s * 128 + sub], in_=ptb[:, j, 0:sub]),
                                    r=[r_ptb[0]], w=[r_Xn])

            def wload(view_shape, src_ap):
                i = cnt["w"] % NW
                cnt["w"] += 1
                n = 1
                for v in view_shape[1:]:
                    n *= v
                flat = wbuf[i][:, 0:n]
                if len(view_shape) == 3:
                    view = flat.rearrange("p (a b) -> p a b", a=view_shape[1])
                else:
                    view = flat
                ldc(view, src_ap, w=[r_wb[i]])
                return view, r_wb[i]

            def ffn(TT, wg, wu, wdn, resid, r_resid, dst, r_dst, t0):
                nsub = (TT + 127) // 128
                n0 = TT // 2
                thb = [(0, n0), (n0, TT - n0)]
                pairc = [0]
                for half in range(2):
                    nch = 44 if half == 0 else 42
                    jb = 0 if half == 0 else 44
                    tiles = [(gu, mt, kh) for mt in range(nch // 2) for gu in range(2) for kh in range(2)]
                    loaded = {}

                    def issue(i):
                        if i < len(tiles):
                            gu, mt, kh = tiles[i]
                            wsrc = wg if gu == 0 else wu
                            c0 = (jb + mt * 2) * 128
                            src = wsrc[kh * 2048:(kh + 1) * 2048, c0:c0 + 256].rearrange("(c p) n -> p c n", p=128)
                            loaded[i] = wload([128, 16, 256], src)
                    issue(0)
                    issue(1)
                    pair_of = {}
                    for i, (gu, mt, kh) in enumerate(tiles):
                        issue(i + 2)
                        wv, rw = loaded.pop(i)
                        si = mt % 2
                        for mc in range(2):
                            if kh == 0:
                                pair_of[(gu, mc)] = pairc[0] % 3
                                pairc[0] += 1
                            pr = pair_of[(gu, mc)]
                            for th, (c0, n) in enumerate(thb):
                                bank = pr * 2 + th
                                for k in range(16):
                                    pe(lambda: T.matmul(pb[bank][:, 0:n], lhsT=wv[:, k, mc * 128:(mc + 1) * 128],
                                                        rhs=Xn[:, kh * 16 + k, c0:c0 + n],
                                                        start=(kh == 0 and k == 0), stop=(kh == 1 and k == 15)),
                                       r=[rw, r_Xn], w=[r_pb[bank]], inc=(mc == 1 and th == 1 and k == 15))
                        if kh == 1:
                            for mc in range(2):
                                pr = pair_of[(gu, mc)]
                                for th, (c0, n) in enumerate(thb):
                                    bank = pr * 2 + th
                                    if gu == 0:
                                        act(lambda: A.activation(out=sg[si][:, mc, c0:c0 + n], in_=pb[bank][:, 0:n], func=AF.Silu),
                                            r=[r_pb[bank]], w=[r_sg[si]])
                                    else:
                                        dve(lambda: V.tensor_tensor(out=H[:, mt * 2 + mc, c0:c0 + n], in0=pb[bank][:, 0:n],
                                                                    in1=sg[si][:, mc, c0:c0 + n], op=ALU.mult),
                                            r=[r_pb[bank], r_sg[si]], w=[r_H])
                    groups = [(0, 11), (11, 11), (22, 11), (33, 11)] if half == 0 else [(0, 11), (11, 11), (22, 10), (32, 10)]
                    tiles = [(fb, g) for fb in range(8) for g in range(4)]
                    loaded = {}

                    def issue2(i):
                        if i < len(tiles):
                            fb, g = tiles[i]
                            j0, nj = groups[g]
                            src = wdn[(jb + j0) * 128:(jb + j0 + nj) * 128, fb * 512:(fb + 1) * 512].rearrange("(c p) n -> p c n", p=128)
                            loaded[i] = wload([128, nj, 512], src)
                    issue2(0)
                    issue2(1)
                    rs_src, rs_res = (resid, r_resid) if half == 0 else (dst, r_dst)
                    for i, (fb, g) in enumerate(tiles):
                        issue2(i + 2)
                        wv, rw = loaded.pop(i)
                        j0, nj = groups[g]
                        for s in range(nsub):
                            sub = min(128, TT - s * 128)
                            for jj in range(nj):
                                j = j0 + jj
                                pe(lambda: T.matmul(pb[s][:sub, :], lhsT=H[:, j, s * 128:s * 128 + sub], rhs=wv[:, jj, :],
                                                    start=(j == 0), stop=(j == nch - 1)), r=[rw, r_H], w=[r_pb[s]],
                                   inc=(s == nsub - 1 and jj == nj - 1))
                        if g == 3:
                            for s in range(nsub):
                                sub = min(128, TT - s * 128)
                                rows = slice(t0 + s * 128, t0 + s * 128 + sub)
                                cols = slice(fb * 512, (fb + 1) * 512)
                                ri = cnt["rsd"] % 2
                                cnt["rsd"] += 1
                                ld(rsd[ri][:sub, :], rs_src[rows, cols], r=[rs_res], w=[r_rsd[ri]])
                                dve(lambda: V.scalar_tensor_tensor(
                                    out=yo[ri][:sub, :], in0=pb[s][:sub, :], scalar=0.5, in1=rsd[ri][:sub, :],
                                    op0=ALU.mult, op1=ALU.add), r=[r_pb[s], r_rsd[ri]], w=[r_yo[ri]])
                                ld(dst[rows, cols], yo[ri][:sub, :], r=[r_yo[ri]], w=[r_dst])

            def linear_tm(TT, wsrc, ncols, evac):
                nsub = (TT + 127) // 128
                ncb_n = (ncols + 511) // 512
                tiles = [(cb, kg) for cb in range(ncb_n) for kg in range(4)]
                loaded = {}

                def issue(i):
                    if i < len(tiles):
                        cb, kg = tiles[i]
                        ncb = min(512, ncols - cb * 512)
                        src = wsrc[kg * 1024:(kg + 1) * 1024, cb * 512:cb * 512 + ncb].rearrange("(c p) n -> p c n", p=128)
                        loaded[i] = wload([128, 8, ncb], src)
                issue(0)
                issue(1)
                for i, (cb, kg) in enumerate(tiles):
                    issue(i + 2)
                    wv, rw = loaded.pop(i)
                    ncb = min(512, ncols - cb * 512)
                    for s in range(nsub):
                        sub = min(128, TT - s * 128)
                        for k in range(8):
                            pe(lambda s=s, sub=sub, k=k, wv=wv: T.matmul(
                                pb[s][:sub, 0:ncb], lhsT=Xn[:, kg * 8 + k, s * 128:s * 128 + sub], rhs=wv[:, k, :],
                                start=(kg == 0 and k == 0), stop=(kg == 3 and k == 7)), r=[rw, r_Xn], w=[r_pb[s]],
                               inc=(s == nsub - 1 and k == 7))
                    if kg == 3:
                        for s in range(nsub):
                            sub = min(128, TT - s * 128)
                            evac(s, sub, cb, ncb)

            ctx = dict(load_gcol=load_gcol, load_norm=load_norm, ffn=ffn, linear_tm=linear_tm, rsd=rsd, r_rsd=r_rsd,
                       yo=yo, r_yo=r_yo, cnt=cnt)
            for job in jobs:
                job(ctx)

        TILES = [(0, 768), (768, 768), (1536, 520)]

        def job_ffn(l, which, src, rsrc, dst, rdst):
            def run(ctx):
                gname = "ln_ffn1" if which == 1 else "ln_ffn2"
                ctx["load_gcol"](Wd[gname][l])
                for (t0, TT) in TILES:
                    ctx["load_norm"](src, rsrc, t0, TT)
                    ctx["ffn"](TT, Wd["w%d_gate" % which][l], Wd["w%d_up" % which][l], Wd["w%d_down" % which][l],
                               src, rsrc, dst, rdst, t0)
            return run

        def job_win(l, src, rsrc):
            def run(ctx):
                ctx["load_gcol"](Wd["ln_mix"][l])
                yo, r_yo, cnt = ctx["yo"], ctx["r_yo"], ctx["cnt"]
                for (t0, TT) in TILES:
                    ctx["load_norm"](src, rsrc, t0, TT)

                    def evac(s, sub, cb, ncb, t0=t0):
                        ri = cnt["yo"] % 2
                        cnt["yo"] += 1
                        rows = slice(t0 + s * 128, t0 + s * 128 + sub)
                        act(lambda: A.copy(out=yo[ri][:sub, 0:ncb], in_=pb[s][:sub, 0:ncb]), r=[r_pb[s]], w=[r_yo[ri]])
                        ld(Z[rows, cb * 512:cb * 512 + ncb], yo[ri][:sub, 0:ncb], r=[r_yo[ri]], w=[rZ])
                    ctx["linear_tm"](TT, Wd["w_in"][l], PROJ, evac)
            return run

        def job_wout(l, hsrc, rh, dst, rdst):
            def run(ctx):
                yo, r_yo, rsd, r_rsd, cnt = ctx["yo"], ctx["r_yo"], ctx["rsd"], ctx["r_rsd"], ctx["cnt"]
                for (t0, TT) in TILES:
                    ctx["load_norm"](MIX, rMIX, t0, TT, norm=False)

                    def evac(s, sub, cb, ncb, t0=t0):
                        ri = cnt["yo"] % 2
                        cnt["yo"] += 1
                        rows = slice(t0 + s * 128, t0 + s * 128 + sub)
                        cols = slice(cb * 512, cb * 512 + ncb)
                        ld(rsd[ri][:sub, :], hsrc[rows, cols], r=[rh], w=[r_rsd[ri]])
                        dve(lambda: V.tensor_tensor(out=yo[ri][:sub, :], in0=pb[s][:sub, :], in1=rsd[ri][:sub, :], op=ALU.add),
                            r=[r_pb[s], r_rsd[ri]], w=[r_yo[ri]])
                        ld(dst[rows, cols], yo[ri][:sub, :], r=[r_yo[ri]], w=[rdst])
                    ctx["linear_tm"](TT, Wd["w_out"][l], D, evac)
            return run

        def job_final(src, rsrc):
            def run(ctx):
                pass
            return run

        def rms_rows(st_tiles, x_ap, sub, gbc_ap, out_ap, r_x, r_g, r_out, n):
            junk, r_junk, ss, r_ss = st_tiles
            act(lambda: A.activation(out=junk[:sub, 0:n], in_=x_ap, func=AF.Square, accum_out=ss[:sub, 0:1]),
                r=[r_x], w=[r_junk, r_ss])
            dve(lambda: V.tensor_scalar(out=ss[:sub, 1:2], in0=ss[:sub, 0:1], scalar1=1.0 / n, scalar2=EPS,
                                        op0=ALU.mult, op1=ALU.add), r=[r_ss], w=[r_ss])
            act(lambda: A.activation(out=ss[:sub, 2:3], in_=ss[:sub, 1:2], func=AF.Sqrt), r=[r_ss], w=[r_ss])
            dve(lambda: V.reciprocal(out=ss[:sub, 3:4], in_=ss[:sub, 2:3]), r=[r_ss], w=[r_ss])
            dve(lambda: V.scalar_tensor_tensor(out=out_ap, in0=x_ap, scalar=ss[:sub, 3:4], in1=gbc_ap,
                                               op0=ALU.mult, op1=ALU.mult), r=[r_x, r_ss, r_g], w=[r_out])

        SEQS = [dict(r0=0, T=TP, nprev=0, si=0), dict(r0=TP, T=TS, nprev=NPREV, si=1)]

        def final_norm(src, rsrc):
            with ExitStack() as st:
                xst = sb(st, "f_x", [128, D]); r_x = Res()
                gbc = sb(st, "f_g", [128, D]); r_g = Res()
                junk = sb(st, "f_j", [128, D]); r_j = Res()
                ss = sb(st, "f_ss", [128, 8]); r_ss = Res()
                yo = sb(st, "f_y", [128, D]); r_y = Res()
                ld(gbc[:], Wd["ln_final"].rearrange("a b -> (a b)").unsqueeze(0).partition_broadcast(128), w=[r_g])
                for (t0, TT) in [(i * 128, 128) for i in range(16)] + [(TP, TS)]:
                    ld(xst[:TT, :], src[t0:t0 + TT, :], r=[rsrc], w=[r_x])
                    rms_rows((junk, r_j, ss, r_ss), xst[:TT, :], TT, gbc[:TT, :], yo[:TT, :], r_x, r_g, r_y, D)
                    ld(Y[t0:t0 + TT, :], yo[:TT, :], r=[r_y], w=[rOUT])
            fw.barrier()

        def attention(l):
            for sq in SEQS:
                with ExitStack() as st:
                    r0, Tn, nprev = sq["r0"], sq["T"], sq["nprev"]
                    Tk = nprev + Tn
                    nkt = (Tk + 127) // 128
                    qT = sb(st, "qT", [128, 8, max(Tn, 128)], BF16); r_qT = Res()
                    kT = sb(st, "kT", [128, 8, nkt * 128], BF16); r_kT = Res()
                    va = sb(st, "va", [128, nkt, 8, 130], BF16); r_va = Res()
                    mt = sb(st, "mt", [128, MASKW], BF16); r_mt = Res()
                    zqk = sb(st, "zqk", [128, 2048]); r_zqk = Res()
                    rot = sb(st, "rot", [128, 2048]); r_rot = Res()
                    rotb = sb(st, "rotb", [128, 2048], BF16); r_rotb = Res()
                    t1 = sb(st, "t1", [128, 1024]); r_t1 = Res()
                    t2 = sb(st, "t2", [128, 1024]); r_t2 = Res()
                    cs = sb(st, "cs", [128, 128]); r_cs = Res()
                    eb = [sb(st, "eb%d" % i, [128, 512], BF16) for i in range(2)]; r_eb = [Res(), Res()]
                    pbf = [sb(st, "pbf%d" % i, [128, 512], BF16) for i in range(2)]; r_pbf = [Res(), Res()]
                    oa = sb(st, "oa", [128, 4, 1024]); r_oa = [Res() for _ in range(4)]
                    gbc = sb(st, "gbc", [128, 1024]); r_gbc = Res()
                    junk = sb(st, "ajunk", [128, 1024]); r_junk = Res()
                    ss = sb(st, "ass", [128, 8]); r_ss = Res()
                    rd = sb(st, "ard", [128, 8]); r_rd = Res()
                    yo = sb(st, "ayo", [128, 1024]); r_yo = Res()
                    ldc(mt[:], CMT[:, :], w=[r_mt])
                    ld(gbc[:], Wd["g_out_a"][l].partition_broadcast(128), w=[r_gbc])
                    pool(lambda: G.memset(kT[:], 0.0), w=[r_kT])
                    pool(lambda: G.memset(va[:], 0.0), w=[r_va])
                    pool(lambda: G.memset(va[:, :, :, 128:129], 1.0), w=[r_va])
                    if nprev > 0:
                        kc = sb(st, "kc", [128, 16, 1024], BF16); r_kc = Res()
                        ldc(kc[:], CK[l].rearrange("(s p) c -> p s c", p=128), w=[r_kc])
                        for s in range(16):
                            ldc(va[:, s, :, 0:128], CV[l, s * 128:(s + 1) * 128, :].rearrange("p (h d) -> p h d", d=128), r=[r_va], w=[r_va])
                        for s in range(16):
                            for h in range(8):
                                pe(lambda s=s, h=h: T.transpose(out=ptb[:, h, :], in_=kc[:, s, h * 128:(h + 1) * 128],
                                                                identity=ident_b[:, :]), r=[r_kc, r_idb], w=[r_ptb[0]])
                            dve(lambda s=s: V.tensor_copy(out=kT[:, :, s * 128:(s + 1) * 128], in_=ptb[:, :, :]),
                                r=[r_ptb[0]], w=[r_kT])
                    nsub = (Tn + 127) // 128
                    if Tn >= 128:
                        for s in range(nsub):
                            ldc(va[:, nprev // 128 + s, :, 0:128],
                                Z[r0 + s * 128:r0 + (s + 1) * 128, 2048:3072].rearrange("p (h d) -> p h d", d=128), r=[rZ, r_va], w=[r_va])
                    else:
                        ldc(va[0:Tn, nprev // 128, :, 0:128],
                            Z[r0:r0 + Tn, 2048:3072].rearrange("p (h d) -> p h d", d=128), r=[rZ, r_va], w=[r_va])
                    ld(NEWV[l, r0:r0 + Tn, :], Z[r0:r0 + Tn, 2048:3072], r=[rZ], w=[rOUT])
                    for s in range(nsub):
                        sub = min(128, Tn - s * 128)
                        rows = slice(r0 + s * 128, r0 + s * 128 + sub)
                        ld(zqk[:sub, :], Z[rows, 0:2048], r=[rZ], w=[r_zqk])
                        ld(cs[:sub, :], CCS[rows, :], w=[r_cs])
                        zv = zqk[:sub, :].rearrange("p (h two d) -> p h two d", two=2, d=64)
                        rv = rot[:sub, :].rearrange("p (h two d) -> p h two d", two=2, d=64)
                        x1, x2 = zv[:, :, 0, :], zv[:, :, 1, :]
                        cb_ = cs[:sub, 0:64].unsqueeze(1).to_broadcast([sub, 16, 64])
                        sb_ = cs[:sub, 64:128].unsqueeze(1).to_broadcast([sub, 16, 64])
                        t1v = t1[:sub, :].rearrange("p (h d) -> p h d", d=64)
                        t2v = t2[:sub, :].rearrange("p (h d) -> p h d", d=64)
                        dve(lambda: V.tensor_tensor(out=t1v, in0=x1, in1=cb_, op=ALU.mult), r=[r_zqk, r_cs], w=[r_t1])
                        dve(lambda: V.tensor_tensor(out=t2v, in0=x2, in1=sb_, op=ALU.mult), r=[r_zqk, r_cs], w=[r_t2])
                        dve(lambda: V.tensor_tensor(out=rv[:, :, 0, :], in0=t1v, in1=t2v, op=ALU.subtract), r=[r_t1, r_t2], w=[r_rot])
                        dve(lambda: V.tensor_tensor(out=t1v, in0=x1, in1=sb_, op=ALU.mult), r=[r_zqk, r_cs], w=[r_t1])
                        dve(lambda: V.tensor_tensor(out=t2v, in0=x2, in1=cb_, op=ALU.mult), r=[r_zqk, r_cs], w=[r_t2])
                        dve(lambda: V.tensor_tensor(out=rv[:, :, 1, :], in0=t1v, in1=t2v, op=ALU.add), r=[r_t1, r_t2], w=[r_rot])
                        ld(NEWK[l, rows, :], rot[:sub, 1024:2048], r=[r_rot], w=[rOUT])
                        act(lambda: A.copy(out=rotb[:sub, :], in_=rot[:sub, :]), r=[r_rot], w=[r_rotb])
                        for half in range(2):
                            for h in range(8):
                                c = half * 8 + h
                                pe(lambda c=c, h=h: T.transpose(out=ptb[:, h, 0:sub], in_=rotb[:sub, c * 128:(c + 1) * 128],
                                                                identity=ident_b[0:sub, 0:sub]), r=[r_rotb, r_idb], w=[r_ptb[0]])
                            if half == 0:
                                dve(lambda: V.tensor_copy(out=qT[:, :, s * 128:s * 128 + sub], in_=ptb[:, :, 0:sub]),
                                    r=[r_ptb[0]], w=[r_qT])
                            else:
                                dve(lambda: V.tensor_copy(out=kT[:, :, nprev + s * 128:nprev + s * 128 + sub], in_=ptb[:, :, 0:sub]),
                                    r=[r_ptb[0]], w=[r_kT])
                    QB = min(512, Tn)
                    iters = []
                    for qb in range(Tn // QB):
                        q0 = nprev + qb * QB
                        kt_lo = max(0, (q0 - 2048) // 128)
                        kt_hi = (q0 + QB - 1) // 128
                        for h in range(8):
                            for kt in range(kt_lo, kt_hi + 1):
                                iters.append((qb, q0, h, kt, kt_lo, kt_hi))
                    nq = QB
                    nqs = (nq + 127) // 128

                    def emit_qk(i):
                        qb, q0, h, kt, kt_lo, kt_hi = iters[i]
                        bi = i % 2
                        pe(lambda: T.matmul(pb[4 + bi][:, 0:nq], lhsT=kT[:, h, kt * 128:(kt + 1) * 128],
                                            rhs=qT[:, h, qb * QB:qb * QB + nq], start=True, stop=True),
                           r=[r_kT, r_qT], w=[r_pb[4 + bi]])

                    def emit_rest(i):
                        qb, q0, h, kt, kt_lo, kt_hi = iters[i]
                        bi = i % 2
                        c0 = q0 - kt * 128 + C0
                        act(lambda: A.activation(out=eb[bi][:, 0:nq], in_=pb[4 + bi][:, 0:nq], func=AF.Exp,
                                                 scale=float(128 ** -0.5)), r=[r_pb[4 + bi]], w=[r_eb[bi]])
                        dve(lambda: V.tensor_tensor(out=pbf[bi][:, 0:nq], in0=eb[bi][:, 0:nq],
                                                    in1=mt[:, c0:c0 + nq], op=ALU.mult),
                            r=[r_eb[bi], r_mt], w=[r_pbf[bi]])
                        for qs in range(nqs):
                            qsub = min(128, nq - qs * 128)
                            pe(lambda: T.matmul(
                                pb[qs][:qsub, 0:129], lhsT=pbf[bi][:, qs * 128:qs * 128 + qsub], rhs=va[:, kt, h, 0:129],
                                start=(kt == kt_lo), stop=(kt == kt_hi)), r=[r_pbf[bi], r_va], w=[r_pb[qs]],
                               inc=(qs == nqs - 1))
                        if kt == kt_hi:
                            for qs in range(nqs):
                                qsub = min(128, nq - qs * 128)
                                dve(lambda: V.reciprocal(out=rd[:qsub, qs:qs + 1], in_=pb[qs][:qsub, 128:129]),
                                    r=[r_pb[qs]], w=[r_rd])
                                dve(lambda: V.tensor_scalar(
                                    out=oa[:qsub, qs, h * 128:(h + 1) * 128], in0=pb[qs][:qsub, 0:128], scalar1=rd[:qsub, qs:qs + 1],
                                    scalar2=None, op0=ALU.mult), r=[r_pb[qs], r_rd], w=[r_oa[qs]])
                            if h == 7:
                                for qs in range(nqs):
                                    qsub = min(128, nq - qs * 128)
                                    rows = slice(r0 + qb * QB + qs * 128, r0 + qb * QB + qs * 128 + qsub)
                                    rms_rows((junk, r_junk, ss, r_ss), oa[:qsub, qs, :], qsub, gbc[:qsub, :], yo[:qsub, :],
                                             r_oa[qs], r_gbc, r_yo, 1024)
                                    ld(MIX[rows, 0:1024], yo[:qsub, :], r=[r_yo], w=[rMIX])

                    emit_qk(0)
                    for i in range(len(iters)):
                        if i + 1 < len(iters):
                            emit_qk(i + 1)
                        emit_rest(i)
                fw.barrier()

        def gmlp_pool(l):
            with ExitStack() as st:
                wsT = sb(st, "wsT", [128, 8, 128], BF16); r_wsT = Res()
                wsr = sb(st, "wsr", [128, 128]); r_wsr = Res()
                wsm = sb(st, "wsm", [128, 128], BF16); r_wsm = Res()
                tril = sb(st, "tril", [128, 128]); r_tril = Res()
                bsr = sb(st, "bsr", [8, 128]); r_bsr = Res()
                bcol = sb(st, "bcol", [128, 8]); r_bcol = Res()
                gc = sb(st, "gc", [128, 1024]); r_gc = Res()
                gd = sb(st, "gd", [128, 1024]); r_gd = Res()
                psc = sb(st, "psc", [128, 1024]); r_psc = Res()
                pm = sb(st, "pm", [128, 12, 128], BF16); r_pm = Res()
                pms = sb(st, "pms", [32, 4, 8], BF16); r_pms = Res()
                wp = sb(st, "wp", [128, 4, 2, 256], BF16); r_wp = Res()
                u = sb(st, "gu", [128, 1024]); r_u = Res()
                vb = sb(st, "gvb", [128, 1024], BF16); r_vb = Res()
                oc = sb(st, "goc", [128, 1024]); r_oc = Res()
                pcur = sb(st, "pcur", [128, 1024], BF16); r_pcur = Res()
                pprev = sb(st, "pprev", [128, 1024], BF16); r_pprev = Res()
                pext = sb(st, "pext", [32, 1024], BF16); r_pext = Res()
                plT = sb(st, "plT", [128, 8, 128], BF16); r_plT = Res()
                od = sb(st, "god", [128, 1024]); r_od = Res()
                junk = sb(st, "gjunk", [128, 1024]); r_junk = Res()
                ss = sb(st, "gss", [128, 8]); r_ss = Res()
                yo = sb(st, "gyo", [128, 1024]); r_yo = Res()
                ld(tril[:], CTRIL[:, :], w=[r_tril])
                ld(gc[:], Wd["g_out_c"][l].partition_broadcast(128), w=[r_gc])
                ld(gd[:], Wd["g_out_d"][l].partition_broadcast(128), w=[r_gd])
                ld(psc[:], Wd["pool_scale"][l].partition_broadcast(128), w=[r_psc])
                ldc(pm[:], CPM.rearrange("p (a b) -> p a b", b=128), w=[r_pm])
                ldc(pms[:], CPMS.rearrange("p (a b) -> p a b", b=8), w=[r_pms])
                ldc(wp[:], Wd["w_pool"][l].rearrange("g (c p) d -> p g c d", p=128), w=[r_wp])
                ld(bsr[:], Wd["b_s"][l], w=[r_bsr])
                pe(lambda: T.transpose(out=ptf[:, 0, 0:8], in_=bsr[:, :], identity=ident_f[0:8, 0:8]), r=[r_bsr, r_idf], w=[r_ptf])
                dve(lambda: V.tensor_copy(out=bcol[:], in_=ptf[:, 0, 0:8]), r=[r_ptf], w=[r_bcol])
                for g in range(8):
                    ld(wsr[:], Wd["w_s"][l, g], w=[r_wsr])
                    dve(lambda: V.tensor_tensor(out=wsm[:], in0=wsr[:], in1=tril[:], op=ALU.mult), r=[r_wsr, r_tril], w=[r_wsm])
                    pe(lambda g=g: T.transpose(out=ptb[:, g, :], in_=wsm[:, :], identity=ident_b[:, :]), r=[r_wsm, r_idb], w=[r_ptb[0]])
                dve(lambda: V.tensor_copy(out=wsT[:], in_=ptb[:]), r=[r_ptb[0]], w=[r_wsT])
                for sq in SEQS:
                    r0, Tn, nprev, si = sq["r0"], sq["T"], sq["nprev"], sq["si"]
                    nsub = (Tn + 127) // 128
                    if si == 0:
                        ld(PLO[l, 0], Z[r0 + Tn - 15:r0 + Tn, 8480:9504], r=[rZ], w=[rOUT])
                    else:
                        ld(PLO[l, 1, 0:7, :], PL0[l, 8:15, :], w=[rOUT])
                        ld(PLO[l, 1, 7:15, :], Z[r0:r0 + Tn, 8480:9504], r=[rZ], w=[rOUT])
                        ld(GVO[l], Z[r0:r0 + Tn, 7456:8480], r=[rZ], w=[rOUT])
                    ld(SHO[l, si:si + 1, :], Z[r0 + Tn - 1:r0 + Tn, 3072:3072 + BFEAT], r=[rZ], w=[rOUT])
                    for s in range(nsub):
                        sub = min(128, Tn - s * 128)
                        rows = slice(r0 + s * 128, r0 + s * 128 + sub)
                        ld(u[:sub, :], Z[rows, 6432:7456], r=[rZ], w=[r_u])
                        ldc(vb[:sub, :], Z[rows, 7456:8480], r=[rZ], w=[r_vb])
                        for g in range(8):
                            bank = g // 4
                            pe(lambda g=g, bank=bank: T.matmul(pb[bank][:sub, (g % 4) * 128:(g % 4 + 1) * 128], lhsT=wsT[:sub, g, 0:sub],
                                                               rhs=vb[:sub, g * 128:(g + 1) * 128], start=True, stop=True),
                               r=[r_wsT, r_vb], w=[r_pb[bank]])
                        for g in range(8):
                            bank = g // 4
                            dve(lambda g=g, bank=bank: V.scalar_tensor_tensor(
                                out=oc[:sub, g * 128:(g + 1) * 128], in0=pb[bank][:sub, (g % 4) * 128:(g % 4 + 1) * 128],
                                scalar=bcol[:sub, g:g + 1], in1=u[:sub, g * 128:(g + 1) * 128], op0=ALU.add, op1=ALU.mult),
                                r=[r_pb[bank], r_bcol, r_u], w=[r_oc])
                        rms_rows((junk, r_junk, ss, r_ss), oc[:sub, :], sub, gc[:sub, :], yo[:sub, :], r_oc, r_gc, r_yo, 1024)
                        ld(MIX[rows, 2048:3072], yo[:sub, :], r=[r_yo], w=[rMIX])
                        if si == 0:
                            cur, r_cur = (pcur, r_pcur) if s % 2 == 0 else (pprev, r_pprev)
                            prv, r_prv = (pprev, r_pprev) if s % 2 == 0 else (pcur, r_pcur)
                            ldc(cur[:sub, :], Z[rows, 8480:9504], r=[rZ], w=[r_cur])
                            for g in range(4):
                                for cc in range(2):
                                    ch = g * 2 + cc
                                    cols = slice(ch * 128, (ch + 1) * 128)
                                    if s == 0:
                                        pe(lambda g=g, ch=ch, cols=cols: T.matmul(pb[2 + ch // 4][:, (ch % 4) * 128:(ch % 4 + 1) * 128],
                                                                               lhsT=cur[:sub, cols], rhs=pm[:sub, g * 3 + 2, 0:sub],
                                                                               start=True, stop=True), r=[r_cur, r_pm], w=[r_pb[2 + ch // 4]])
                                    else:
                                        pe(lambda g=g, ch=ch, cols=cols: T.matmul(pb[2 + ch // 4][:, (ch % 4) * 128:(ch % 4 + 1) * 128],
                                                                               lhsT=cur[:sub, cols], rhs=pm[:sub, g * 3 + 0, 0:sub],
                                                                               start=True, stop=False), r=[r_cur, r_pm], w=[r_pb[2 + ch // 4]])
                                        pe(lambda g=g, ch=ch, cols=cols: T.matmul(pb[2 + ch // 4][:, (ch % 4) * 128:(ch % 4 + 1) * 128],
                                                                               lhsT=prv[:, cols], rhs=pm[:, g * 3 + 1, 0:sub],
                                                                               start=False, stop=True), r=[r_prv, r_pm], w=[r_pb[2 + ch // 4]])
                        else:
                            ldc(pext[0:15, :], PL0[l], w=[r_pext])
                            ldc(pext[15:15 + sub, :], Z[rows, 8480:9504], r=[rZ, r_pext], w=[r_pext])
                            for g in range(4):
                                for cc in range(2):
                                    ch = g * 2 + cc
                                    cols = slice(ch * 128, (ch + 1) * 128)
                                    pe(lambda g=g, ch=ch, cols=cols: T.matmul(pb[2 + ch // 4][:, (ch % 4) * 128:(ch % 4) * 128 + sub],
                                                                           lhsT=pext[0:23, cols], rhs=pms[0:23, g, 0:sub],
                                                                           start=True, stop=True), r=[r_pext, r_pms], w=[r_pb[2 + ch // 4]])
                        for hb in range(2):
                            dve(lambda hb=hb: V.tensor_copy(out=plT[:, hb * 4:(hb + 1) * 4, 0:sub],
                                                            in_=pb[2 + hb][:, :].rearrange("p (a b) -> p a b", b=128)[:, :, 0:sub]),
                                r=[r_pb[2 + hb]], w=[r_plT])
                        for g in range(4):
                            bank = g // 2
                            for cc in range(2):
                                pe(lambda g=g, cc=cc, bank=bank: T.matmul(pb[bank][:sub, (g % 2) * 256:(g % 2 + 1) * 256],
                                                                          lhsT=plT[:, g * 2 + cc, 0:sub], rhs=wp[:, g, cc, :],
                                                                          start=(cc == 0), stop=(cc == 1)), r=[r_plT, r_wp], w=[r_pb[bank]])
                        for bank in range(2):
                            dve(lambda bank=bank: V.tensor_tensor(out=od[:sub, bank * 512:(bank + 1) * 512], in0=pb[bank][:sub, :],
                                                                  in1=psc[:sub, bank * 512:(bank + 1) * 512], op=ALU.mult),
                                r=[r_pb[bank], r_psc], w=[r_od])
                        rms_rows((junk, r_junk, ss, r_ss), od[:sub, :], sub, gd[:sub, :], yo[:sub, :], r_od, r_gd, r_yo, 1024)
                        ld(MIX[rows, 3072:4096], yo[:sub, :], r=[r_yo], w=[rMIX])
            fw.barrier()

        def rwkv(l):
            CH = 64
            with ExitStack() as st:
                def bc(name, src):
                    t = sb(st, name, [128, 1024]); r = Res()
                    ld(t[:], src.partition_broadcast(128), w=[r])
                    return t, r
                mu = sb(st, "mu", [128, BFEAT]); r_mu = Res()
                ld(mu[:], Wd["mu_b"][l].partition_broadcast(128), w=[r_mu])
                w0b, r_w0b = bc("w0b", Wd["w0"][l])
                a0b, r_a0b = bc("a0b", Wd["a0"][l])
                kkb, r_kkb = bc("kkb", Wd["k_k"][l])
                kab, r_kab = bc("kab", Wd["k_a"][l])
                rkb, r_rkb = bc("rkb", Wd["r_k"][l])
                lwb, r_lwb = bc("lwb", Wd["ln_x_w"][l])
                lbb, r_lbb = bc("lbb", Wd["ln_x_b"][l])
                wup = sb(st, "wup", [64, 1024], BF16); r_wup = Res()
                aup = sb(st, "aup", [64, 1024], BF16); r_aup = Res()
                gup = sb(st, "gup", [128, 2, 1024], BF16); r_gup = Res()
                ldc(wup[:], Wd["w_up"][l], w=[r_wup])
                ldc(aup[:], Wd["a_up"][l], w=[r_aup])
                ldc(gup[:, 0, :], Wd["g_up"][l, 0:128, :], w=[r_gup])
                ldc(gup[0:32, 1, :], Wd["g_up"][l, 128:160, :], r=[r_gup], w=[r_gup])
                tri2 = sb(st, "tri2", [128, 128]); r_tri2 = Res()
                ones2 = sb(st, "ones2", [128, 128]); r_ones2 = Res()
                msk = sb(st, "msk", [128, 4, 64]); r_msk = Res()
                ld(tri2[:], CTRI2[:, :], w=[r_tri2])
                ld(ones2[:], CONES2[:, :], w=[r_ones2])
                ld(msk[:], CMSK.rearrange("p (a b) -> p a b", b=64), w=[r_msk])

                def mb(i):
                    return msk[:, i:i + 1, :].to_broadcast([128, 8, 64])
                valid = sb(st, "valid", [128, 1]); r_valid = Res()
                fbt = sb(st, "fbt", [128, BFEAT]); r_fb = Res()
                pvt = sb(st, "pvt", [128, BFEAT]); r_pv = Res()
                lz = sb(st, "lz", [128, 288], BF16); r_lz = Res()
                lzT = sb(st, "lzT", [128, 4, 128], BF16); r_lzT = Res()
                NA = 14
                arr = [sb(st, "ar%d" % i, [128, 1024]) for i in range(NA)]
                r_arr = [Res() for _ in range(NA)]
                sm = sb(st, "sm", [128, 64]); r_sm = Res()
                NF = 5
                fm = [sb(st, "fm%d" % i, [128, 8, 64]) for i in range(NF)]; r_fm = [Res() for _ in range(NF)]
                NM = 14
                mm = [sb(st, "mm%d" % i, [128, 8, 64]) for i in range(NM)]; r_mm = [Res() for _ in range(NM)]
                ST = sb(st, "ST", [128, 8, 64]); r_ST = Res()
                stmp = sb(st, "stmp", [64, 8, 128]); r_stmp = Res()

                def pbv(i):
                    return pb[i][:, :].rearrange("p (a b) -> p a b", b=64)

                def headmm(bank, terms, r, first=True, last=True):
                    n = len(terms)
                    for hp in range(8):
                        for hh in range(2):
                            R = slice(hh * 64, (hh + 1) * 64)
                            for ti, (lf, rf) in enumerate(terms):
                                is_last_inst = (hp == 7 and hh == 1 and ti == n - 1)
                                pe(lambda: T.matmul(pb[bank][R, hp * 64:(hp + 1) * 64], lhsT=lf(R, hp, hh), rhs=rf(R, hp, hh),
                                                    start=(first and ti == 0), stop=(last and ti == n - 1)),
                                   r=r, w=[r_pb[bank]], inc=is_last_inst)

                def f3(t):
                    return lambda R, hp, hh: t[R, hp, :]

                def tmcols(ap2d):
                    return lambda R, hp, hh: ap2d[R, hp * 128 + hh * 64:hp * 128 + hh * 64 + 64]

                for sq in SEQS:
                    r0, Tn, nprev, si = sq["r0"], sq["T"], sq["nprev"], sq["si"]
                    nch = (Tn + CH - 1) // CH
                    if si == 0:
                        dve(lambda: V.memset(ST[:], 0.0), w=[r_ST])
                    else:
                        for hh in range(2):
                            ld(stmp[:, :, hh * 64:(hh + 1) * 64],
                               WKV0[l].rearrange("(hp hh) v k -> hh v hp k", hh=2)[hh], r=[r_stmp], w=[r_stmp])
                        for hb in range(2):
                            for j in range(4):
                                pe(lambda: T.transpose(out=ptf[:, j, 0:64], in_=stmp[:, hb * 4 + j, :], identity=ident_f[0:64, 0:64]),
                                   r=[r_stmp, r_idf], w=[r_ptf], inc=(j == 3))
                            dve(lambda: V.tensor_copy(out=ST[:, hb * 4:(hb + 1) * 4, :], in_=ptf[:, :, 0:64]), r=[r_ptf], w=[r_ST])
                    for c in range(nch):
                        nv = min(CH, Tn - c * CH)
                        ta = r0 + c * CH
                        fcol = slice(3072, 3072 + BFEAT)
                        if nv < CH:
                            dve(lambda: V.memset(fbt[:], 0.0), w=[r_fb])
                            dve(lambda: V.memset(pvt[:], 0.0), w=[r_pv])
                        if nv < CH or c == 0:
                            dve(lambda: V.memset(valid[:], 0.0), w=[r_valid])
                            for d in range(2):
                                dve(lambda: V.memset(valid[d * 64:d * 64 + nv, :], 1.0), r=[r_valid], w=[r_valid])
                        for d in range(2):
                            P0 = d * 64
                            ld(fbt[P0:P0 + nv, :], Z[ta:ta + nv, fcol], r=[rZ, r_fb], w=[r_fb])
                            if c == 0:
                                if si == 0:
                                    dve(lambda: V.memset(pvt[P0:P0 + 1, :], 0.0), r=[r_pv], w=[r_pv])
                                else:
                                    ld(pvt[P0:P0 + 1, :], SH0[l], r=[r_pv], w=[r_pv])
                                if nv > 1:
                                    ld(pvt[P0 + 1:P0 + nv, :], Z[ta:ta + nv - 1, fcol], r=[rZ, r_pv], w=[r_pv])
                            else:
                                ld(pvt[P0:P0 + nv, :], Z[ta - 1:ta - 1 + nv, fcol], r=[rZ, r_pv], w=[r_pv])
                        dve(lambda: V.tensor_tensor(out=pvt[:, :], in0=pvt[:, :], in1=fbt[:, :], op=ALU.subtract), r=[r_pv, r_fb], w=[r_pv])
                        pool(lambda: G.tensor_tensor(out=pvt[:, :], in0=pvt[:, :], in1=mu[:, :], op=ALU.mult), r=[r_pv, r_mu], w=[r_pv])
                        dve(lambda: V.tensor_tensor(out=pvt[:, :], in0=pvt[:, :], in1=fbt[:, :], op=ALU.add), r=[r_pv, r_fb], w=[r_pv])
                        fs = pvt
                        R_, K_, V_ = fs[:, 0:1024], fs[:, 1024:2048], fs[:, 2048:3072]
                        act(lambda: A.activation(out=lz[:, 0:64], in_=fs[:, 3072:3136], func=AF.Tanh), r=[r_pv], w=[r_lz])
                        act(lambda: A.copy(out=lz[:, 64:128], in_=fs[:, 3136:3200]), r=[r_pv], w=[r_lz])
                        act(lambda: A.activation(out=lz[:, 128:288], in_=fs[:, 3200:3360], func=AF.Sigmoid), r=[r_pv], w=[r_lz])
                        for j, (c0_, cn) in enumerate([(0, 64), (64, 64), (128, 128), (256, 32)]):
                            pe(lambda: T.transpose(out=ptb[0:cn, j, :], in_=lz[:, c0_:c0_ + cn], identity=ident_b[:, :]),
                               r=[r_lz, r_idb], w=[r_ptb[0]], inc=(j == 3))
                        dve(lambda: V.tensor_copy(out=lzT[:, :, :], in_=ptb[:, 0:4, :]), r=[r_ptb[0]], w=[r_lzT])
                        for hb in range(2):
                            cs_ = slice(hb * 512, (hb + 1) * 512)
                            pe(lambda: T.matmul(pb[hb][:, :], lhsT=lzT[0:64, 0, :], rhs=wup[:, cs_], start=True, stop=True),
                               r=[r_lzT, r_wup], w=[r_pb[hb]])
                            pe(lambda: T.matmul(pb[2 + hb][:, :], lhsT=lzT[0:64, 1, :], rhs=aup[:, cs_], start=True, stop=True),
                               r=[r_lzT, r_aup], w=[r_pb[2 + hb]])
                            pe(lambda: T.matmul(pb[4 + hb][:, :], lhsT=lzT[:, 2, :], rhs=gup[:, 0, cs_], start=True, stop=False),
                               r=[r_lzT, r_gup], w=[r_pb[4 + hb]], inc=False)
                            pe(lambda: T.matmul(pb[4 + hb][:, :], lhsT=lzT[0:32, 3, :], rhs=gup[0:32, 1, cs_], start=False, stop=True),
                               r=[r_lzT, r_gup], w=[r_pb[4 + hb]])
                        (LW, At, Gt, KKt, Bt, KMt, BON, T1, T2, Lsb, Xa, Xb, Xc, Xd) = [a[:, :] for a in arr]
                        (rLW, rA, rG, rKK, rB, rKM, rBON, rT1, rT2, rL, rXa, rXb, rXc, rXd) = r_arr
                        for hb in range(2):
                            cs_ = slice(hb * 512, (hb + 1) * 512)
                            dve(lambda: V.tensor_tensor(out=arr[0][:, cs_], in0=pb[hb][:, :], in1=w0b[:, cs_], op=ALU.add),
                                r=[r_pb[hb], r_w0b], w=[rLW])
                            dve(lambda: V.tensor_tensor(out=arr[1][:, cs_], in0=pb[2 + hb][:, :], in1=a0b[:, cs_], op=ALU.add),
                                r=[r_pb[2 + hb], r_a0b], w=[rA])
                            act(lambda: A.copy(out=arr[2][:, cs_], in_=pb[4 + hb][:, :]), r=[r_pb[4 + hb]], w=[rG])
                        act(lambda: A.activation(out=LW, in_=LW, func=AF.Sigmoid), r=[rLW], w=[rLW])
                        act(lambda: A.activation(out=At, in_=At, func=AF.Sigmoid), r=[rA], w=[rA])
                        dve(lambda: V.tensor_scalar(out=LW, in0=LW, scalar1=valid[:, 0:1], scalar2=-float(np.exp(-0.5)), op0=ALU.mult, op1=ALU.mult),
                            r=[rLW, r_valid], w=[rLW])
                        pool(lambda: G.tensor_tensor(out=KKt, in0=K_, in1=kkb[:, :], op=ALU.mult), r=[r_pv, r_kkb], w=[rKK])
                        dve(lambda: V.tensor_tensor(out=T1, in0=KKt, in1=KKt, op=ALU.mult), r=[rKK], w=[rT1])
                        dve(lambda: V.tensor_reduce(out=sm[:, 0:16], in_=T1.rearrange("p (h d) -> p h d", d=64), axis=AX.X, op=ALU.add), r=[rT1], w=[r_sm])
                        act(lambda: A.activation(out=sm[:, 16:32], in_=sm[:, 0:16], func=AF.Sqrt), r=[r_sm], w=[r_sm])
                        dve(lambda: V.tensor_scalar(out=sm[:, 16:32], in0=sm[:, 16:32], scalar1=1e-12, scalar2=None, op0=ALU.max), r=[r_sm], w=[r_sm])
                        dve(lambda: V.reciprocal(out=sm[:, 32:48], in_=sm[:, 16:32]), r=[r_sm], w=[r_sm])
                        dve(lambda: V.tensor_tensor(out=KKt.rearrange("p (h d) -> p h d", d=64), in0=KKt.rearrange("p (h d) -> p h d", d=64),
                                                    in1=sm[:, 32:48].unsqueeze(2).to_broadcast([128, 16, 64]), op=ALU.mult), r=[rKK, r_sm], w=[rKK])
                        pool(lambda: G.tensor_tensor(out=Bt, in0=KKt, in1=At, op=ALU.mult), r=[rKK, rA], w=[rB])
                        dve(lambda: V.scalar_tensor_tensor(out=T1, in0=At, scalar=-1.0, in1=kab[:, :], op0=ALU.add, op1=ALU.mult), r=[rA, r_kab], w=[rT1])
                        dve(lambda: V.scalar_tensor_tensor(out=KMt, in0=T1, scalar=1.0, in1=K_, op0=ALU.add, op1=ALU.mult), r=[rT1, r_pv], w=[rKM])
                        pool(lambda: G.tensor_tensor(out=T1, in0=R_, in1=KMt, op=ALU.mult), r=[r_pv, rKM], w=[rT1])
                        pool(lambda: G.tensor_tensor(out=T1, in0=T1, in1=rkb[:, :], op=ALU.mult), r=[rT1, r_rkb], w=[rT1])
                        dve(lambda: V.tensor_reduce(out=sm[:, 48:64], in_=T1.rearrange("p (h d) -> p h d", d=64), axis=AX.X, op=ALU.add), r=[rT1], w=[r_sm])
                        pool(lambda: G.tensor_tensor(out=BON.rearrange("p (h d) -> p h d", d=64), in0=V_.rearrange("p (h d) -> p h d", d=64),
                                                     in1=sm[:, 48:64].unsqueeze(2).to_broadcast([128, 16, 64]), op=ALU.mult), r=[r_pv, r_sm], w=[rBON])
                        for hb in range(2):
                            cs_ = slice(hb * 512, (hb + 1) * 512)
                            pe(lambda: T.matmul(pb[hb][:, :], lhsT=tri2[:, :], rhs=arr[0][:, cs_], start=True, stop=True), r=[r_tri2, rLW], w=[r_pb[hb]])
                            pe(lambda: T.matmul(pb[2 + hb][:, :], lhsT=ones2[:, :], rhs=arr[0][:, cs_], start=True, stop=True), r=[r_ones2, rLW], w=[r_pb[2 + hb]])
                        for hb in range(2):
                            cs_ = slice(hb * 512, (hb + 1) * 512)
                            act(lambda: A.copy(out=arr[9][:, cs_], in_=pb[hb][:, :]), r=[r_pb[hb]], w=[rL])
                            dve(lambda: V.tensor_tensor(out=arr[8][:, cs_], in0=pb[2 + hb][:, :], in1=arr[9][:, cs_], op=ALU.subtract), r=[r_pb[2 + hb], rL], w=[rT2])
                        act(lambda: A.activation(out=T2, in_=T2, func=AF.Exp), r=[rT2], w=[rT2])
                        dve(lambda: V.tensor_tensor(out=T1, in0=Lsb, in1=LW, op=ALU.subtract), r=[rL, rLW], w=[rT1])
                        act(lambda: A.activation(out=T1, in_=T1, func=AF.Exp), r=[rT1], w=[rT1])
                        pool(lambda: G.tensor_tensor(out=Xa, in0=KKt, in1=T1, op=ALU.mult), r=[rKK, rT1], w=[rXa])
                        dve(lambda: V.tensor_tensor(out=KKt, in0=Bt, in1=T2, op=ALU.mult), r=[rB, rT2], w=[rKK])
                        pool(lambda: G.tensor_tensor(out=LW, in0=KMt, in1=T2, op=ALU.mult), r=[rKM, rT2], w=[rLW])
                        BH, KH, rBH, rKH = KKt, LW, rKK, rLW
                        act(lambda: A.activation(out=Xd, in_=Lsb, func=AF.Exp), r=[rL], w=[rXd])
                        act(lambda: A.activation(out=T1, in_=Lsb, func=AF.Exp, scale=-1.0), r=[rL], w=[rT1])
                        dve(lambda: V.tensor_tensor(out=Xb, in0=Bt, in1=T1, op=ALU.mult), r=[rB, rT1], w=[rXb])
                        pool(lambda: G.tensor_tensor(out=Xc, in0=KMt, in1=T1, op=ALU.mult), r=[rKM, rT1], w=[rXc])
                        dve(lambda: V.tensor_tensor(out=T2, in0=R_, in1=Xd, op=ALU.mult), r=[r_pv, rXd], w=[rT2])
                        for fi, (src_, rs_) in enumerate([(Xa, rXa), (Xb, rXb), (Xc, rXc), (T2, rT2), (Xd, rXd)]):
                            for hb in range(2):
                                for j in range(4):
                                    hp = hb * 4 + j
                                    pe(lambda: T.transpose(out=ptf[:, j, :], in_=src_[:, hp * 128:(hp + 1) * 128], identity=ident_f[:, :]),
                                       r=[rs_, r_idf], w=[r_ptf], inc=(j == 3))
                                if (fi + hb) % 2 == 0:
                                    dve(lambda: V.tensor_copy(out=fm[fi][:, hb * 4:(hb + 1) * 4, :], in_=ptf[:, :, 0:64]), r=[r_ptf], w=[r_fm[fi]])
                                else:
                                    act(lambda: A.copy(out=fm[fi][:, hb * 4:(hb + 1) * 4, :], in_=ptf[:, :, 0:64]), r=[r_ptf], w=[r_fm[fi]])
                        Af, Bf, Kf, Rf, Ef = fm
                        rAf, rBf, rKf, rRf, rEf = r_fm
                        headmm(0, [(f3(Bf), f3(Af))], [rBf, rAf])
                        headmm(1, [(f3(Af), f3(Bf))], [rBf, rAf])
                        headmm(2, [(f3(Kf), f3(Af))], [rKf, rAf])
                        headmm(3, [(f3(Bf), f3(Rf))], [rBf, rRf])
                        headmm(4, [(f3(Kf), f3(Rf))], [rKf, rRf])
                        E = [mm[0], mm[1]]; ET = [mm[2], mm[3]]; rE = [r_mm[0], r_mm[1]]; rET = [r_mm[2], r_mm[3]]
                        Tm, TTm, Mm, Np, Mp, XT, nU, osb = mm[4], mm[5], mm[6], mm[7], mm[8], mm[9], mm[10], mm[11]
                        rTm, rTTm, rMm, rNp, rMp, rXT, rnU, rosb = (r_mm[i] for i in range(4, 12))
                        dve(lambda: V.scalar_tensor_tensor(out=E[0][:], in0=pbv(0), scalar=-1.0, in1=mb(0), op0=ALU.mult, op1=ALU.mult),
                            r=[r_pb[0], r_msk], w=[rE[0]])
                        dve(lambda: V.scalar_tensor_tensor(out=ET[0][:], in0=pbv(1), scalar=-1.0, in1=mb(2), op0=ALU.mult, op1=ALU.mult),
                            r=[r_pb[1], r_msk], w=[rET[0]])
                        pool(lambda: G.tensor_tensor(out=Tm[:], in0=E[0][:], in1=mb(3), op=ALU.add), r=[rE[0], r_msk], w=[rTm])
                        pool(lambda: G.tensor_tensor(out=TTm[:], in0=ET[0][:], in1=mb(3), op=ALU.add), r=[rET[0], r_msk], w=[rTTm])
                        dve(lambda: V.tensor_tensor(out=Mm[:], in0=pbv(2), in1=mb(0), op=ALU.mult), r=[r_pb[2], r_msk], w=[rMm])
                        dve(lambda: V.tensor_tensor(out=Np[:], in0=pbv(3), in1=mb(1), op=ALU.mult), r=[r_pb[3], r_msk], w=[rNp])
                        dve(lambda: V.tensor_tensor(out=Mp[:], in0=pbv(4), in1=mb(1), op=ALU.mult), r=[r_pb[4], r_msk], w=[rMp])
                        cur = 0
                        for lev in range(5):
                            nxt = 1 - cur
                            lastlev = (lev == 4)
                            headmm(0, [(f3(ET[cur]), f3(E[cur]))], [rET[cur], rE[cur]])
                            if not lastlev:
                                headmm(1, [(f3(E[cur]), f3(ET[cur]))], [rET[cur], rE[cur]])
                            act(lambda: A.copy(out=E[nxt][:], in_=pbv(0)), r=[r_pb[0]], w=[rE[nxt]])
                            if not lastlev:
                                dve(lambda: V.tensor_copy(out=ET[nxt][:], in_=pbv(1)), r=[r_pb[1]], w=[rET[nxt]])
                            headmm(2, [(f3(TTm), f3(E[nxt]))], [rTTm, rE[nxt]])
                            if not lastlev:
                                headmm(3, [(f3(E[nxt]), f3(TTm))], [rTTm, rE[nxt]])
                            dve(lambda: V.tensor_tensor(out=Tm[:], in0=pbv(2), in1=Tm[:], op=ALU.add), r=[r_pb[2], rTm], w=[rTm])
                            if not lastlev:
                                dve(lambda: V.tensor_tensor(out=TTm[:], in0=pbv(3), in1=TTm[:], op=ALU.add), r=[r_pb[3], rTTm], w=[rTTm])
                            cur = nxt
                        vcols = tmcols(V_)
                        headmm(0, [(f3(Af), f3(ST)), (f3(Mm), vcols)], [rAf, r_ST, rMm, r_pv])
                        act(lambda: A.copy(out=XT[:], in_=pbv(0)), r=[r_pb[0]], w=[rXT])
                        headmm(1, [(f3(Tm), f3(XT))], [rTm, rXT])
                        dve(lambda: V.tensor_scalar(out=nU[:], in0=pbv(1), scalar1=-1.0, scalar2=None, op0=ALU.mult), r=[r_pb[1]], w=[rnU])
                        headmm(2, [(f3(Rf), f3(ST)), (f3(Np), f3(nU)), (f3(Mp), vcols)], [rRf, r_ST, rNp, rnU, rMp, r_pv])
                        headmm(3, [(tmcols(BH), f3(nU)), (tmcols(KH), vcols)], [rBH, rnU, rKH, r_pv])
                        act(lambda: A.copy(out=osb[:], in_=pbv(2)), r=[r_pb[2]], w=[rosb])
                        dve(lambda: V.tensor_tensor(out=ST[:], in0=ST[:], in1=Ef[:, :, 63:64].to_broadcast([128, 8, 64]), op=ALU.mult),
                            r=[r_ST, rEf], w=[r_ST])
                        dve(lambda: V.tensor_tensor(out=ST[:], in0=pbv(3), in1=ST[:], op=ALU.add), r=[r_pb[3], r_ST], w=[r_ST])
                        sel = [mm[12], mm[13], XT]
                        rsel = [r_mm[12], r_mm[13], rXT]
                        w2 = [Tm, TTm]
                        rw2 = [rTm, rTTm]
                        for d in range(2):
                            P_ = slice(d * 64, (d + 1) * 64)
                            def hv(ap2d):
                                return ap2d[P_, :].rearrange("p (hp hh v) -> p hp hh v", hh=2, v=64)[:, :, d, :]
                            pool(lambda: G.tensor_copy(out=sel[0][P_, :, :], in_=hv(BON)), r=[rBON], w=[rsel[0]])
                            pool(lambda: G.tensor_copy(out=sel[1][P_, :, :], in_=hv(Gt)), r=[rG], w=[rsel[1]])
                            pool(lambda: G.tensor_copy(out=w2[0][P_, :, :], in_=hv(lwb[:, :])), r=[r_lwb], w=[rw2[0]])
                            pool(lambda: G.tensor_copy(out=w2[1][P_, :, :], in_=hv(lbb[:, :])), r=[r_lbb], w=[rw2[1]])
                        o3 = osb[:]
                        sq3 = nU[:]
                        dve(lambda: V.tensor_reduce(out=sm[:, 0:8], in_=o3, axis=AX.X, op=ALU.add), r=[rosb], w=[r_sm])
                        dve(lambda: V.tensor_scalar(out=sm[:, 0:8], in0=sm[:, 0:8], scalar1=1.0 / 64, scalar2=None, op0=ALU.mult), r=[r_sm], w=[r_sm])
                        dve(lambda: V.tensor_tensor(out=o3, in0=o3, in1=sm[:, 0:8].unsqueeze(2).to_broadcast([128, 8, 64]), op=ALU.subtract),
                            r=[rosb, r_sm], w=[rosb])
                        dve(lambda: V.tensor_tensor(out=sq3, in0=o3, in1=o3, op=ALU.mult), r=[rosb], w=[rnU])
                        dve(lambda: V.tensor_reduce(out=sm[:, 8:16], in_=sq3, axis=AX.X, op=ALU.add), r=[rnU], w=[r_sm])
                        dve(lambda: V.tensor_scalar(out=sm[:, 8:16], in0=sm[:, 8:16], scalar1=1.0 / 64, scalar2=64e-5, op0=ALU.mult, op1=ALU.add),
                            r=[r_sm], w=[r_sm])
                        act(lambda: A.activation(out=sm[:, 8:16], in_=sm[:, 8:16], func=AF.Sqrt), r=[r_sm], w=[r_sm])
                        dve(lambda: V.reciprocal(out=sm[:, 16:24], in_=sm[:, 8:16]), r=[r_sm], w=[r_sm])
                        dve(lambda: V.tensor_tensor(out=o3, in0=o3, in1=sm[:, 16:24].unsqueeze(2).to_broadcast([128, 8, 64]), op=ALU.mult),
                            r=[rosb, r_sm], w=[rosb])
                        dve(lambda: V.tensor_tensor(out=o3, in0=o3, in1=w2[0][:], op=ALU.mult), r=[rosb, rw2[0]], w=[rosb])
                        dve(lambda: V.tensor_tensor(out=o3, in0=o3, in1=w2[1][:], op=ALU.add), r=[rosb, rw2[1]], w=[rosb])
                        dve(lambda: V.tensor_tensor(out=o3, in0=o3, in1=sel[0][:], op=ALU.add), r=[rosb, rsel[0]], w=[rosb])
                        dve(lambda: V.tensor_tensor(out=o3, in0=o3, in1=sel[1][:], op=ALU.mult), r=[rosb, rsel[1]], w=[rosb])
                        for d in range(2):
                            dst = MIX[ta:ta + nv, 1024:2048].rearrange("t (hp hh v) -> t hp hh v", hh=2, v=64)[:, :, d, :]
                            ld(dst, osb[d * 64:d * 64 + nv, :, :], r=[rosb], w=[rMIX])
                    for hb in range(2):
                        for j in range(4):
                            pe(lambda: T.transpose(out=ptf[0:64, j, :], in_=ST[:, hb * 4 + j, :], identity=ident_f[:, :]),
                               r=[r_ST, r_idf], w=[r_ptf], inc=(j == 3))
                        dve(lambda: V.tensor_copy(out=stmp[:, hb * 4:(hb + 1) * 4, :], in_=ptf[0:64, :, :]), r=[r_ptf, r_stmp], w=[r_stmp])
                    for hh in range(2):
                        ld(WKVO[l, si].rearrange("(hp hh) v k -> hh v hp k", hh=2)[hh],
                           stmp[:, :, hh * 64:(hh + 1) * 64], r=[r_stmp], w=[rOUT])
            fw.barrier()

        src, rsrc = X, rX
        outs = [(RA, rRA), (RB, rRB)]
        for l in range(2):
            dst, rdst = outs[l]
            with ExitStack() as st:
                token_local(st, [job_ffn(l, 1, src, rsrc, R1, rR1), job_win(l, R1, rR1)])
            fw.barrier()
            attention(l)
            rwkv(l)
            gmlp_pool(l)
            with ExitStack() as st:
                token_local(st, [job_wout(l, R1, rR1, R2, rR2), job_ffn(l, 2, R2, rR2, dst, rdst)])
            fw.barrier()
            src, rsrc = dst, rdst
        final_norm(src, rsrc)
        fw.finish()
    return nc


_NC_CACHE = {}


def kernel(**inputs):
    inp = {k: np.ascontiguousarray(np.asarray(v, dtype=np.float32)) for k, v in inputs.items()}
    consts = _host_consts()
    if "nc" not in _NC_CACHE:
        _NC_CACHE["nc"] = build_nc()
    nc = _NC_CACHE["nc"]
    shared = {}
    for n, shp in WNAMES:
        shared[n] = inp[n].reshape(shp)
    shared.update(consts)
    in_maps = []
    for c in range(8):
        m = dict(shared)
        m["x_all"] = np.concatenate([inp["x_prompt"][c % 4], inp["x_sample"][c]], axis=0)
        m["cache_k"] = inp["cache_k_swa"][:, c].reshape(2, NPREV, 1024)
        m["cache_v"] = inp["cache_v_swa"][:, c].reshape(2, NPREV, 1024)
        m["wkv0"] = inp["state_rwkv_wkv"][:, c]
        m["shift0"] = inp["state_rwkv_shift"][:, c].reshape(2, 1, BFEAT)
        m["pool0"] = inp["state_pool"][:, c]
        in_maps.append({k: np.ascontiguousarray(v) for k, v in m.items()})
    res = run_bass_kernel_spmd(nc, in_maps, core_ids=list(range(8)))
    R = res.results
    y_p = np.stack([R[b]["y"][:TP] for b in range(4)])
    y_s = np.stack([R[c]["y"][TP:] for c in range(8)])
    nk_p = np.stack([R[b]["newk"][:, :TP] for b in range(4)], axis=1).reshape(2, 4, TP, 8, 128)
    nv_p = np.stack([R[b]["newv"][:, :TP] for b in range(4)], axis=1).reshape(2, 4, TP, 8, 128)
    wkv_p = np.stack([R[b]["wkvo"][:, 0] for b in range(4)], axis=1)
    sh_p = np.stack([R[b]["sho"][:, 0] for b in range(4)], axis=1)
    pl_p = np.stack([R[b]["plo"][:, 0] for b in range(4)], axis=1)
    nk_s = np.stack([R[c]["newk"][:, TP:] for c in range(8)], axis=1).reshape(2, 8, TS, 8, 128)
    nv_s = np.stack([R[c]["newv"][:, TP:] for c in range(8)], axis=1).reshape(2, 8, TS, 8, 128)
    wkv_s = np.stack([R[c]["wkvo"][:, 1] for c in range(8)], axis=1)
    sh_s = np.stack([R[c]["sho"][:, 1] for c in range(8)], axis=1)
    pl_s = np.stack([R[c]["plo"][:, 1] for c in range(8)], axis=1)
    gv_s = np.stack([R[c]["gvo"] for c in range(8)], axis=1)
    outs = (y_p, y_s, nk_p, nv_p, wkv_p, sh_p, pl_p, nk_s, nv_s, wkv_s, sh_s, pl_s, gv_s)
    return tuple(np.ascontiguousarray(o.astype(np.float32)) for o in outs)
```
